# Optimizing a Trainium2 kernel written in Bass

```python
import jax
import jax.numpy as jnp
from jax import lax
import numpy as np

D_MODEL = 1024
BATCH = 8
SEQ = 4096
DEPTH = 1
DEC_BATCH = 32
DEC_SEQ = 1
PAST_LEN = 16384
PAGE_SIZE = 128

N_HEADS = 16
HEAD_DIM = D_MODEL // N_HEADS
N_KV_HEADS = 4
GROUP = N_HEADS // N_KV_HEADS
KV_W = N_KV_HEADS * HEAD_DIM
D_CONV = D_MODEL
CONV_WIDTH = 31
D_FF = 4 * D_MODEL
CMP_STRIDE = 16
CMP_LEN = 2 * CMP_STRIDE
SLC_BLOCK = 64
SLC_RATIO = SLC_BLOCK // CMP_STRIDE
N_SEL = 16
WINDOW = 512
NSA_Q_BLOCK = 16
EPS = 1e-6
IN_SIZES = (2 * D_CONV, N_HEADS * HEAD_DIM) + (KV_W,) * 6 + (3 * N_HEADS, D_MODEL, D_MODEL)
IN_OFFSETS = tuple(int(v) for v in np.cumsum(IN_SIZES)[:-1])
D_IN = sum(IN_SIZES)
F32 = jnp.float32

kernel_name = "hybrid_conformer_nsa_decode_step"


def rms_norm(x, g):
    xf = x.astype(F32)
    y = xf * lax.rsqrt(jnp.mean(xf * xf, axis=-1, keepdims=True) + EPS)
    return (y * g.astype(F32)).astype(x.dtype)


def layer_norm(x, g, b):
    xf = x.astype(F32)
    mu = jnp.mean(xf, axis=-1, keepdims=True)
    var = jnp.mean(jnp.square(xf - mu), axis=-1, keepdims=True)
    y = (xf - mu) * lax.rsqrt(var + EPS)
    return (y * g.astype(F32) + b.astype(F32)).astype(x.dtype)


def alibi_slopes():
    h = np.arange(1, N_HEADS + 1, dtype=np.float32)
    return jnp.asarray(np.power(np.float32(2.0), -8.0 * h / N_HEADS), dtype=F32)


def masked_softmax(s, valid):
    s = jnp.where(valid, s, -jnp.inf)
    m = jnp.max(s, axis=-1, keepdims=True)
    m = jnp.where(jnp.isfinite(m), m, 0.0)
    e = jnp.exp(s - m)
    return e / jnp.maximum(jnp.sum(e, axis=-1, keepdims=True), 1e-30)


def pre_mix(x, ln1_g, w_in):
    b, t, _ = x.shape
    z = rms_norm(x, ln1_g) @ w_in
    u2, q, kc, vc, ks, vs, kw, vw, ng, g_conv, g_attn = jnp.split(z, IN_OFFSETS, axis=-1)
    u = u2[..., :D_CONV] * jax.nn.sigmoid(u2[..., D_CONV:])
    kvh = lambda a: a.reshape(b, t, N_KV_HEADS, HEAD_DIM)
    cmp_kv = jnp.stack([kvh(kc), kvh(vc)], axis=2)
    slc_kv = jnp.stack([kvh(ks), kvh(vs)], axis=2)
    win_kv = jnp.stack([kvh(kw), kvh(vw)], axis=2)
    q = q.reshape(b, t, N_HEADS, HEAD_DIM)
    ng = ng.reshape(b, t, 3, N_HEADS)
    return u, q, cmp_kv, slc_kv, win_kv, ng, g_conv, g_attn


def conv_module(u_ext, conv_w, conv_b, ln_g, ln_b, w_pw):
    c = u_ext.shape[-1]
    y = lax.conv_general_dilated(u_ext, conv_w.astype(u_ext.dtype)[:, None, :], (1,), 'VALID',
                                 dimension_numbers=('NWC', 'WIO', 'NWC'), feature_group_count=c)
    y = jax.nn.silu(layer_norm(y + conv_b.astype(y.dtype), ln_g, ln_b))
    return y @ w_pw


def post_mix(x, conv_out, attn_out, g_conv, g_attn, w_out, ln2_g, w_up, w_down):
    m = jax.nn.sigmoid(g_conv) * conv_out + jax.nn.sigmoid(g_attn) * attn_out.astype(conv_out.dtype)
    x = x + m @ w_out
    h = rms_norm(x, ln2_g)
    return x + jnp.square(jax.nn.relu(h @ w_up)) @ w_down


def compress_kv(pieces, cmp_pos, cmp_w, cmp_w1, cmp_w2):
    b = pieces[0].shape[0]
    t = sum(p.shape[1] for p in pieces)
    nc = -(-t // CMP_STRIDE)
    pad = (nc + 1) * CMP_STRIDE - t
    zeros = jnp.zeros((b, pad, 2, N_KV_HEADS, HEAD_DIM), F32)
    kvp = jnp.concatenate([p.astype(F32) for p in pieces] + [zeros], axis=1)
    ch = kvp.reshape(b, nc + 1, CMP_STRIDE, 2, N_KV_HEADS, HEAD_DIM)
    w = cmp_w.astype(F32)
    first = jnp.einsum('bcpjhd,pj->bcjhd', ch[:, :nc], w[:CMP_STRIDE])
    second = jnp.einsum('bcpjhd,pj->bcjhd', ch[:, 1:], w[CMP_STRIDE:])
    pe = jnp.einsum('pjd,pj->jd', cmp_pos.astype(F32), w)
    pooled = first + second + pe[None, None, :, None, :]
    hid = jax.nn.silu(jnp.einsum('bcjhd,jde->bcjhe', pooled, cmp_w1.astype(F32)))
    comp = jnp.einsum('bcjhe,jef->bcjhf', hid, cmp_w2.astype(F32))
    end = jnp.arange(nc, dtype=jnp.int32) * CMP_STRIDE + (CMP_LEN - 1)
    return comp[:, :, 0], comp[:, :, 1], end


def nsa_attend(q, t_pos, cmp_k, cmp_v, cmp_end, nsb, gather, win_k, win_v, win_pos, gates, slopes):
    b, _, _, nq, _ = q.shape
    qf = q.astype(F32) * (HEAD_DIM ** -0.5)
    tq = t_pos.astype(F32)
    sl = slopes.reshape(N_KV_HEADS, GROUP, 1, 1)
    s_c = jnp.einsum('bkgqd,bckd->bkgqc', qf, cmp_k.astype(F32))
    s_c = s_c - sl * (tq[:, None] - cmp_end[None, :].astype(F32))
    p_c = masked_softmax(s_c, cmp_end[None, :] <= t_pos[:, None])
    o_c = jnp.einsum('bkgqc,bckd->bkgqd', p_c, cmp_v.astype(F32))
    imp = jnp.sum(p_c, axis=2)
    nc = imp.shape[-1]
    imp = jnp.pad(imp, ((0, 0), (0, 0), (0, 0), (1, SLC_RATIO * nsb + SLC_RATIO - 1 - nc)))
    head = imp[..., :SLC_RATIO * nsb].reshape(b, N_KV_HEADS, nq, nsb, SLC_RATIO)
    nxt = imp[..., SLC_RATIO:].reshape(b, N_KV_HEADS, nq, nsb, SLC_RATIO)[..., 0]
    score = head[..., 0] + 2.0 * jnp.sum(head[..., 1:], axis=-1) + nxt
    blk = jnp.arange(nsb, dtype=jnp.int32)[None, :]
    cur = (t_pos // SLC_BLOCK)[:, None]
    forced = (blk == 0) | (blk == cur) | (blk == cur - 1)
    score = jnp.where(forced, jnp.inf, jnp.where(blk > cur, -jnp.inf, score))
    _, idx = lax.top_k(score, min(N_SEL, nsb))
    n_top = idx.shape[-1]
    sk, sv = gather(idx)
    pos = idx[..., None] * SLC_BLOCK + jnp.arange(SLC_BLOCK, dtype=jnp.int32)
    dist_s = (t_pos[:, None, None] - pos)[:, :, None]
    s_s = jnp.einsum('bkgqd,bkqnsd->bkgqns', qf, sk.astype(F32))
    s_s = s_s - slopes.reshape(N_KV_HEADS, GROUP, 1, 1, 1) * dist_s.astype(F32)
    flat = n_top * SLC_BLOCK
    p_s = masked_softmax(s_s.reshape(b, N_KV_HEADS, GROUP, nq, flat),
                         (dist_s >= 0).reshape(b, N_KV_HEADS, 1, nq, flat))
    o_s = jnp.einsum('bkgqns,bkqnsd->bkgqd', p_s.reshape(s_s.shape), sv.astype(F32))
    s_w = jnp.einsum('bkgqd,blkd->bkgql', qf, win_k.astype(F32))
    dist_w = t_pos[:, None] - win_pos[None, :]
    valid_w = (dist_w >= 0) & (dist_w <= WINDOW) & (win_pos[None, :] >= 0)
    s_w = s_w - sl * dist_w.astype(F32)
    p_w = masked_softmax(s_w, valid_w)
    o_w = jnp.einsum('bkgql,blkd->bkgqd', p_w, win_v.astype(F32))
    g = jax.nn.sigmoid(gates.astype(F32)).reshape(b, nq, 3, N_KV_HEADS, GROUP).transpose(2, 0, 3, 4, 1)[..., None]
    out = g[0] * o_c + g[1] * o_s + g[2] * o_w
    return out.astype(q.dtype)


def nsa_prompt(q, cmp_kv, slc_kv, win_kv, gates, cmp_pos, cmp_w, cmp_w1, cmp_w2, slopes):
    b, t = q.shape[:2]
    ck, cv, cend = compress_kv([cmp_kv], cmp_pos, cmp_w, cmp_w1, cmp_w2)
    nsb = -(-t // SLC_BLOCK)
    sblk = jnp.pad(slc_kv, ((0, 0), (0, nsb * SLC_BLOCK - t), (0, 0), (0, 0), (0, 0)))
    sblk = sblk.reshape(b, nsb, SLC_BLOCK, 2, N_KV_HEADS, HEAD_DIM)
    bb = jnp.arange(b)[:, None, None, None]
    hh = jnp.arange(N_KV_HEADS)[None, :, None, None]

    def gather(idx):
        g = sblk[bb, idx, :, :, hh]
        return g[..., 0, :], g[..., 1, :]

    wpad = jnp.pad(win_kv, ((0, 0), (WINDOW, 0), (0, 0), (0, 0), (0, 0)))
    qg = q.reshape(b, t, N_KV_HEADS, GROUP, HEAD_DIM).transpose(0, 2, 3, 1, 4)
    n_win = WINDOW + NSA_Q_BLOCK

    def one_block(i):
        s0 = i * NSA_Q_BLOCK
        qb = lax.dynamic_slice_in_dim(qg, s0, NSA_Q_BLOCK, axis=3)
        gb = lax.dynamic_slice_in_dim(gates, s0, NSA_Q_BLOCK, axis=1)
        wb = lax.dynamic_slice_in_dim(wpad, s0, n_win, axis=1)
        t_pos = s0 + jnp.arange(NSA_Q_BLOCK, dtype=jnp.int32)
        win_pos = s0 - WINDOW + jnp.arange(n_win, dtype=jnp.int32)
        return nsa_attend(qb, t_pos, ck, cv, cend, nsb, gather, wb[:, :, 0], wb[:, :, 1], win_pos, gb, slopes)

    out = lax.map(one_block, jnp.arange(t // NSA_Q_BLOCK, dtype=jnp.int32))
    return out.transpose(1, 0, 4, 2, 3, 5).reshape(b, t, N_HEADS * HEAD_DIM)


def nsa_sample(q, cmp_kv, slc_kv, win_kv, gates, layer, cache_cmp_kv, cache_slc_kv, win_state, page_table,
               cmp_pos, cmp_w, cmp_w1, cmp_w2, slopes):
    b, s = q.shape[:2]
    n_pages = page_table.shape[1]
    past = n_pages * PAGE_SIZE
    past_cmp = cache_cmp_kv[layer, page_table].reshape(b, past, 2, N_KV_HEADS, HEAD_DIM)
    ck, cv, cend = compress_kv([past_cmp, cmp_kv], cmp_pos, cmp_w, cmp_w1, cmp_w2)
    total = past + s
    nsb = -(-total // SLC_BLOCK)
    nb_past = past // SLC_BLOCK
    nb_new = nsb - nb_past
    bpp = PAGE_SIZE // SLC_BLOCK
    new_rows = jnp.pad(slc_kv.astype(cache_slc_kv.dtype),
                       ((0, 0), (0, nb_new * SLC_BLOCK - s), (0, 0), (0, 0), (0, 0)))
    bb = jnp.arange(b)[:, None, None, None]
    hh = jnp.arange(N_KV_HEADS)[None, :, None, None]
    offs = jnp.arange(SLC_BLOCK, dtype=jnp.int32)

    def gather(idx):
        lp = jnp.minimum(idx, nb_past - 1)
        page = page_table[bb, lp // bpp]
        rows = ((lp % bpp) * SLC_BLOCK)[..., None] + offs
        g_past = cache_slc_kv[layer, page[..., None], rows, :, hh[..., None]]
        rows_new = (jnp.clip(idx - nb_past, 0, nb_new - 1) * SLC_BLOCK)[..., None] + offs
        g_new = new_rows[bb[..., None], rows_new, :, hh[..., None]]
        g = jnp.where((idx < nb_past)[..., None, None, None], g_past, g_new)
        return g[..., 0, :], g[..., 1, :]

    n_buf = win_state.shape[1]
    win_all = jnp.concatenate([win_state, win_kv.astype(win_state.dtype)], axis=1)
    win_pos = past - n_buf + jnp.arange(n_buf + s, dtype=jnp.int32)
    t_pos = past + jnp.arange(s, dtype=jnp.int32)
    qg = q.reshape(b, s, N_KV_HEADS, GROUP, HEAD_DIM).transpose(0, 2, 3, 1, 4)
    out = nsa_attend(qg, t_pos, ck, cv, cend, nsb, gather, win_all[:, :, 0], win_all[:, :, 1], win_pos, gates, slopes)
    out = out.transpose(0, 3, 1, 2, 4).reshape(b, s, N_HEADS * HEAD_DIM)
    return out, win_all[:, -n_buf:]


def setup_inputs(seed: int = 0) -> dict:
    key = jax.random.key(seed)
    k = jax.random.split(key, 24)
    n_pages = PAST_LEN // PAGE_SIZE
    n_phys = (DEC_BATCH * n_pages * 5) // 4

    def nrm(kk, shape, scale):
        return jax.random.normal(kk, shape, F32) * scale

    kv_row = (2, N_KV_HEADS, HEAD_DIM)
    x_prompt = nrm(k[0], (BATCH, SEQ, D_MODEL), 1.0)
    x_sample = nrm(k[1], (DEC_BATCH, DEC_SEQ, D_MODEL), 1.0)
    cache_cmp_kv = nrm(k[2], (DEPTH, n_phys, PAGE_SIZE) + kv_row, 1.0)
    cache_slc_kv = nrm(k[3], (DEPTH, n_phys, PAGE_SIZE) + kv_row, 1.0)
    state_win_kv = nrm(k[4], (DEPTH, DEC_BATCH, min(WINDOW, PAST_LEN)) + kv_row, 1.0)
    state_conv = nrm(k[5], (DEPTH, DEC_BATCH, CONV_WIDTH - 1, D_CONV), 0.5)
    page_table = jax.random.permutation(k[6], n_phys)[: DEC_BATCH * n_pages].reshape(DEC_BATCH, n_pages).astype(jnp.int32)
    ln1_g = 1.0 + nrm(k[7], (DEPTH, D_MODEL), 0.02)
    w_in = nrm(k[8], (DEPTH, D_MODEL, D_IN), D_MODEL ** -0.5)
    cmp_pos = nrm(k[9], (DEPTH, CMP_LEN, 2, HEAD_DIM), 0.1)
    cmp_w = (1.0 + nrm(k[10], (DEPTH, CMP_LEN, 2), 0.1)) * (CMP_LEN ** -0.5)
    cmp_w1 = nrm(k[11], (DEPTH, 2, HEAD_DIM, HEAD_DIM), HEAD_DIM ** -0.5)
    cmp_w2 = nrm(k[12], (DEPTH, 2, HEAD_DIM, HEAD_DIM), HEAD_DIM ** -0.5)
    conv_w = nrm(k[13], (DEPTH, CONV_WIDTH, D_CONV), CONV_WIDTH ** -0.5)
    conv_b = nrm(k[14], (DEPTH, D_CONV), 0.01)
    conv_ln_g = 1.0 + nrm(k[15], (DEPTH, D_CONV), 0.02)
    conv_ln_b = nrm(k[16], (DEPTH, D_CONV), 0.01)
    w_conv_pw = nrm(k[17], (DEPTH, D_CONV, D_MODEL), D_CONV ** -0.5)
    w_out = nrm(k[18], (DEPTH, N_HEADS * HEAD_DIM, D_MODEL), D_MODEL ** -0.5)
    ln2_g = 1.0 + nrm(k[19], (DEPTH, D_MODEL), 0.02)
    w_up = nrm(k[20], (DEPTH, D_MODEL, D_FF), D_MODEL ** -0.5)
    w_down = nrm(k[21], (DEPTH, D_FF, D_MODEL), D_FF ** -0.5)
    lnf_g = 1.0 + nrm(k[22], (D_MODEL,), 0.02)
    return {"x_prompt": x_prompt, "x_sample": x_sample, "cache_cmp_kv": cache_cmp_kv,
            "cache_slc_kv": cache_slc_kv, "state_win_kv": state_win_kv, "state_conv": state_conv,
            "page_table": page_table, "ln1_g": ln1_g, "w_in": w_in, "cmp_pos": cmp_pos, "cmp_w": cmp_w,
            "cmp_w1": cmp_w1, "cmp_w2": cmp_w2, "conv_w": conv_w, "conv_b": conv_b, "conv_ln_g": conv_ln_g,
            "conv_ln_b": conv_ln_b, "w_conv_pw": w_conv_pw, "w_out": w_out, "ln2_g": ln2_g, "w_up": w_up,
            "w_down": w_down, "lnf_g": lnf_g}


def reference(x_prompt, x_sample, cache_cmp_kv, cache_slc_kv, state_win_kv, state_conv, page_table,
              ln1_g, w_in, cmp_pos, cmp_w, cmp_w1, cmp_w2, conv_w, conv_b, conv_ln_g, conv_ln_b,
              w_conv_pw, w_out, ln2_g, w_up, w_down, lnf_g):
    slopes = alibi_slopes()
    hp, hs = x_prompt, x_sample
    cmp_p, cmp_s, slc_p, slc_s, win_p, win_s, conv_p, conv_s = [], [], [], [], [], [], [], []
    for l in range(DEPTH):
        u, q, ckv, skv, wkv, ng, g_conv, g_attn = pre_mix(hp, ln1_g[l], w_in[l])
        u_ext = jnp.pad(u, ((0, 0), (CONV_WIDTH - 1, 0), (0, 0)))
        c_out = conv_module(u_ext, conv_w[l], conv_b[l], conv_ln_g[l], conv_ln_b[l], w_conv_pw[l])
        a_out = nsa_prompt(q, ckv, skv, wkv, ng, cmp_pos[l], cmp_w[l], cmp_w1[l], cmp_w2[l], slopes)
        hp = post_mix(hp, c_out, a_out, g_conv, g_attn, w_out[l], ln2_g[l], w_up[l], w_down[l])
        cmp_p.append(ckv)
        slc_p.append(skv)
        win_p.append(wkv[:, -min(WINDOW, wkv.shape[1]):])
        conv_p.append(u_ext[:, -(CONV_WIDTH - 1):])
        u, q, ckv, skv, wkv, ng, g_conv, g_attn = pre_mix(hs, ln1_g[l], w_in[l])
        u_ext = jnp.concatenate([state_conv[l].astype(u.dtype), u], axis=1)
        c_out = conv_module(u_ext, conv_w[l], conv_b[l], conv_ln_g[l], conv_ln_b[l], w_conv_pw[l])
        a_out, win_new = nsa_sample(q, ckv, skv, wkv, ng, l, cache_cmp_kv, cache_slc_kv, state_win_kv[l],
                                    page_table, cmp_pos[l], cmp_w[l], cmp_w1[l], cmp_w2[l], slopes)
        hs = post_mix(hs, c_out, a_out, g_conv, g_attn, w_out[l], ln2_g[l], w_up[l], w_down[l])
        cmp_s.append(ckv)
        slc_s.append(skv)
        win_s.append(win_new)
        conv_s.append(u_ext[:, -(CONV_WIDTH - 1):])
    y_prompt = rms_norm(hp, lnf_g)
    y_sample = rms_norm(hs, lnf_g)
    return (y_prompt, y_sample, jnp.stack(cmp_p), jnp.stack(cmp_s), jnp.stack(slc_p), jnp.stack(slc_s),
            jnp.stack(win_p), jnp.stack(win_s), jnp.stack(conv_p), jnp.stack(conv_s))
```

```python
import os
import numpy as np
from contextlib import ExitStack
import concourse.bass as bass
import concourse.mybir as mybir
from concourse.bass_utils import run_bass_kernel_spmd

F32 = mybir.dt.float32
BF16 = mybir.dt.bfloat16
I32 = mybir.dt.int32
AF = mybir.ActivationFunctionType
ALU = mybir.AluOpType
AX = mybir.AxisListType

D = 1024
NH = 16
HD = 64
NKV = 4
KVW = 256
DCONV = 1024
CW = 31
DFF = 4096
DIN = 6704
EPS = 1e-6
OFF_U2, OFF_Q, OFF_KC, OFF_VC, OFF_KS, OFF_VS, OFF_KW, OFF_VW, OFF_NG, OFF_GC, OFF_GA = (
    0, 2048, 3072, 3328, 3584, 3840, 4096, 4352, 4608, 4656, 5680)
NSEM = 12
STOP = int(os.environ.get('KSTOP', '0'))
SKIP = int(os.environ.get('KSKIP', '0'))


class Buf:
    __slots__ = ("t", "w", "r", "name")

    def __init__(self, t=None, name=""):
        self.t = t
        self.w = {}
        self.r = {}
        self.name = name


def _merge(dst, src):
    for k, v in src.items():
        if k not in dst or dst[k][1] < v[1]:
            dst[k] = v


class Eng:
    def __init__(self, name, h):
        self.name = name
        self.h = h
        self.sem = None
        self.count = 0
        self.waited = {}
        self.dsems = []
        self.ndma = 0


class KB:
    def __init__(self, nc, es):
        self.nc = nc
        self.es = es
        self.eng = {
            "pe": Eng("pe", nc.tensor), "act": Eng("act", nc.scalar), "dve": Eng("dve", nc.vector),
            "pool": Eng("pool", nc.gpsimd), "sp": Eng("sp", nc.sync),
        }
        for n, e in self.eng.items():
            e.sem = es.enter_context(nc.semaphore("c_" + n))
        for n in ("sp", "pool", "act"):
            e = self.eng[n]
            e.dsems = [es.enter_context(nc.semaphore(f"d_{n}{i}")) for i in range(NSEM)]
        self.bar = es.enter_context(nc.semaphore("bar"))
        self.nbar = 0
        self.psum_banks = []
        self.ninstr = 0

    def sb(self, name, shape, dt, es=None):
        t = (es or self.es).enter_context(self.nc.sbuf_tensor("sb_" + name, list(shape), dt))
        return Buf(t, name)

    def ps(self, name, shape, dt, es=None):
        t = (es or self.es).enter_context(self.nc.psum_tensor("ps_" + name, list(shape), dt))
        return Buf(t, name)

    def _wait(self, E, deps):
        for k, (sem, v) in deps.items():
            if v <= 0:
                continue
            if E.name == "pe" and sem is E.sem:
                continue
            if E.waited.get(k, 0) >= v:
                continue
            E.h.wait_ge(sem, v)
            E.waited[k] = v
            self.ninstr += 1

    def _deps(self, reads, writes):
        deps = {}
        for b in reads:
            _merge(deps, b.w)
        for b in writes:
            _merge(deps, b.w)
            _merge(deps, b.r)
        return deps

    def _commit(self, tok, reads, writes):
        for b in reads:
            _merge(b.r, tok)
        for b in writes:
            b.w = dict(tok)
            b.r = {}

    def op(self, eng, fn, reads=(), writes=()):
        E = self.eng[eng]
        self._wait(E, self._deps(reads, writes))
        ins = fn(E.h)
        E.count += 1
        ins.then_inc(E.sem, 1)
        self.ninstr += 1
        tok = {id(E.sem): (E.sem, E.count)}
        self._commit(tok, reads, writes)
        return tok

    def dma(self, q, out, in_, reads=(), writes=(), **kw):
        Q = self.eng[q]
        k = Q.ndma
        sem = Q.dsems[k % NSEM]
        deps = self._deps(reads, writes)
        _merge(deps, {id(sem): (sem, 16 * (k // NSEM))})
        self._wait(Q, deps)
        ins = Q.h.dma_start(out=out, in_=in_, **kw)
        ins.then_inc(sem, 16)
        Q.ndma += 1
        self.ninstr += 1
        tok = {id(sem): (sem, 16 * (k // NSEM + 1))}
        self._commit(tok, reads, writes)
        return tok

    def gather(self, out, in_, idx_ap, reads=(), writes=()):
        Q = self.eng["pool"]
        k = Q.ndma
        sem = Q.dsems[k % NSEM]
        deps = self._deps(reads, writes)
        _merge(deps, {id(sem): (sem, 16 * (k // NSEM))})
        self._wait(Q, deps)
        ins = Q.h.indirect_dma_start(out=out, out_offset=None, in_=in_,
                                     in_offset=bass.IndirectOffsetOnAxis(ap=idx_ap, axis=0))
        ins.then_inc(sem, 16)
        Q.ndma += 1
        self.ninstr += 1
        tok = {id(sem): (sem, 16 * (k // NSEM + 1))}
        self._commit(tok, reads, writes)
        return tok

    def all_tokens(self):
        toks = {}
        for e in self.eng.values():
            if e.count:
                toks[id(e.sem)] = (e.sem, e.count)
            for i, s in enumerate(e.dsems):
                n = (e.ndma - i + NSEM - 1) // NSEM if e.ndma > i else 0
                if n:
                    toks[id(s)] = (s, 16 * n)
        return toks

    def barrier(self):
        toks = self.all_tokens()
        sp = self.eng["sp"]
        self._wait(sp, toks)
        self.nbar += 1
        sp.h.sem_inc(self.bar, 1)
        for n, e in self.eng.items():
            if n != "sp":
                e.h.wait_ge(self.bar, self.nbar)
            for k, (sem, v) in toks.items():
                e.waited[k] = max(e.waited.get(k, 0), v)

    def finish(self):
        sp = self.eng["sp"]
        self._wait(sp, self.all_tokens())


class Cfg:
    def __init__(self, T=4096, NS=4, NP=128, NPHYS=5120):
        self.T = T
        self.NS = NS
        self.NP = NP
        self.NPHYS = NPHYS
        self.TT = T + NS
        self.NT = T // 128
        self.TB = 512 if T >= 512 else T
        self.NTB = T // self.TB


def build_program(cfg):
    nc = bass.Bass("TRN2", target_bir_lowering=False)
    T, NS, TT, NT, TB, NTB = cfg.T, cfg.NS, cfg.TT, cfg.NT, cfg.TB, cfg.NTB

    def din(name, shape, dt=F32):
        return nc.dram_tensor(name, list(shape), dt, kind="ExternalInput").ap()

    def dout(name, shape, dt=F32):
        return nc.dram_tensor(name, list(shape), dt, kind="ExternalOutput").ap()

    def dscr(name, shape, dt):
        return nc.dram_tensor(name, list(shape), dt, kind="ExternalOutput").ap()

    xp = din("xp", [T, D])
    xs = din("xs", [NS, D])
    st_win = din("st_win", [NS, 512, 512])
    st_conv = din("st_conv", [NS, 30, D])
    ln1_g = din("ln1_g", [1, D])
    w_in = din("w_in", [D, DIN])
    ident_in = din("ident", [128, 128])

    y_p = dout("y_p", [T, D])
    y_s = dout("y_s", [NS, D])
    cmp_p = dout("cmp_p", [T, 512])
    cmp_s = dout("cmp_s", [NS, 512])
    slc_p = dout("slc_p", [T, 512])
    slc_s = dout("slc_s", [NS, 512])
    win_p = dout("win_p", [512, 512])
    win_s = dout("win_s", [NS, 512, 512])
    conv_p = dout("conv_p", [30, D])
    conv_s = dout("conv_s", [NS, 30, D])

    v_s = dscr("v_s", [2, TT, KVW], BF16)
    ng_s = dscr("ng_s", [TT, 512], F32)
    sg_s = dscr("sg_s", [TT, 2048], BF16)
    uT_s = dscr("uT_s", [D, TT], BF16)
    qT_s = dscr("qT_s", [D, TT], BF16)
    kT_s = dscr("kT_s", [3, KVW, TT], BF16)

    cvec = din("cvec", [34, D])
    w_pw = din("w_pw", [D, D])
    w_out = din("w_out", [D, D])
    ln2_g = din("ln2_g", [1, D])
    w_up = din("w_up", [D, DFF])
    w_down = din("w_down", [DFF, D])
    lnf_g = din("lnf_g", [1, D])
    relc = din("relc", [128, 4096])
    rels = din("rels", [128, 4096 + 128])
    relw = din("relw", [128, 768])
    fmk = din("fmk", [128, 32 * 63])
    cmat = din("cmat", [128, 2 * 63])
    onehot = din("onehot", [64, 4096])
    wpool = din("wpool", [128, 32])
    cposd = din("cposd", [32, 256])
    cw = din("cw", [32, 2])
    w1bd = din("w1bd", [128, 256])
    w2bd = din("w2bd", [128, 256])
    w2sel = din("w2sel", [128, 128])
    NP_, NPH_ = cfg.NP, cfg.NPHYS
    NCT_ = (8 * NP_ - 1 + 127) // 128
    cache_cmp = din("cache_cmp", [NPH_ * 128, 512])
    cache_slc = din("cache_slc", [NPH_ * 128, 512])
    ptab = din("ptab", [NS, NP_], I32)
    iota_in = din("iota", [128, 1], I32)
    bsel = din("bsel", [128, NP_ * 16])
    bwin = din("bwin", [128, 64])
    bcmp = din("bcmp", [128, NCT_ * 16])
    cms = din("cms", [128, NCT_ * 2 * NP_])
    fms = din("fms", [4, 2 * NP_])
    ao_s = dscr("ao_s", [TT, D], F32)
    mc_s = dscr("mc_s", [TT, D], F32)
    x1_s = dscr("x1_s", [TT, D], F32)
    aT_s = dscr("aT_s", [DFF, TT], BF16)

    es = ExitStack()
    with es:
        kb = KB(nc, es)
        block = es.enter_context(nc.Block())

        @block.sync
        def _(sync):
            phase_a(kb, cfg, locals_ns := dict(
                xp=xp, xs=xs, st_win=st_win, st_conv=st_conv, ln1_g=ln1_g, w_in=w_in, ident_in=ident_in,
                y_p=y_p, y_s=y_s, cmp_p=cmp_p, cmp_s=cmp_s, slc_p=slc_p, slc_s=slc_s, win_p=win_p,
                win_s=win_s, conv_p=conv_p, conv_s=conv_s, uT_s=uT_s, qT_s=qT_s, kT_s=kT_s, v_s=v_s,
                ng_s=ng_s, sg_s=sg_s, cvec=cvec, w_pw=w_pw, w_out=w_out, ln2_g=ln2_g, w_up=w_up,
                w_down=w_down, lnf_g=lnf_g, mc_s=mc_s, x1_s=x1_s, aT_s=aT_s, ao_s=ao_s, relc=relc, rels=rels,
                relw=relw, fmk=fmk, cmat=cmat, onehot=onehot, wpool=wpool, cposd=cposd, cw=cw, w1bd=w1bd,
                w2bd=w2bd, w2sel=w2sel, cache_cmp=cache_cmp, cache_slc=cache_slc, ptab=ptab, iota=iota_in,
                bsel=bsel, bwin=bwin, bcmp=bcmp, cms=cms, fms=fms))
            if (STOP == 0 or STOP >= 12) and not (SKIP & 32):
                phase_att(kb, cfg, locals_ns)
                phase_smp(kb, cfg, locals_ns)
            if STOP == 0 or STOP >= 10:
                phase_b(kb, cfg, locals_ns)
            if STOP == 0 or STOP >= 11:
                phase_e(kb, cfg, locals_ns)
            kb.finish()
    return nc


def phase_a(kb, cfg, g):
    nc = kb.nc
    T, NS, TT, NT, TB, NTB = cfg.T, cfg.NS, cfg.TT, cfg.NT, cfg.TB, cfg.NTB
    xp, xs, w_in = g["xp"], g["xs"], g["w_in"]
    es = ExitStack()
    with es:
        hT = kb.sb("hT", [128, 8, TT], BF16, es)
        ident = kb.sb("ident", [128, 128], BF16, es)
        identf = kb.sb("identf", [128, 128], F32, es)
        g_bc = kb.sb("g_bc", [128, D], F32, es)
        xt = [kb.sb(f"xt{i}", [128, D], F32, es) for i in range(2)]
        hb = [kb.sb(f"hb{i}", [128, D], BF16, es) for i in range(2)]
        junk = kb.sb("junk", [128, D], BF16, es)
        stat = [kb.sb(f"stat{i}", [128, 4], F32, es) for i in range(2)]
        psT = [kb.ps(f"psT{i}", [128, 8, 128], BF16, es) for i in range(2)]
        pm = [kb.ps(f"pm{i}", [128, 512], F32, es) for i in range(6)]
        wfm = [kb.sb(f"wfm{i}", [128, 8, 128], BF16, es) for i in range(4)]
        wtm = [kb.sb(f"wtm{i}", [128, 8, 512], BF16, es) for i in range(2)]
        sgt = [kb.sb(f"sgt{i}", [128, 512], F32, es) for i in range(2)]
        obf = [kb.sb(f"obf{i}", [128, 512], BF16, es) for i in range(3)]
        of32 = [kb.sb(f"of32{i}", [128, 512], F32, es) for i in range(3)]
        ovb = [kb.sb(f"ovb{i}", [128, 256], BF16, es) for i in range(2)]
        utail = kb.sb("utail", [128, 8, 32 + NS], F32, es)
        uto = kb.sb("uto", [32 + NS, D], F32, es)

        kb.dma("pool", ident.t[:], g["ident_in"][:, :], writes=[ident])
        kb.dma("sp", identf.t[:], g["ident_in"][:, :], writes=[identf])
        kb.dma("sp", g_bc.t[:], g["ln1_g"][0:1, :].to_broadcast([128, D]), writes=[g_bc])

        ntile = NT + 1
        for i in range(ntile):
            rows = 128 if i < NT else NS
            src = xp[128 * i:128 * i + 128, :] if i < NT else xs[:, :]
            x_t, h_b, s_t, p_t = xt[i % 2], hb[i % 2], stat[i % 2], psT[i % 2]
            kb.dma("sp", x_t.t[0:rows, :], src, writes=[x_t])
            kb.op("act", lambda e: e.activation(out=junk.t[0:rows, :], in_=x_t.t[0:rows, :], func=AF.Square,
                                                accum_out=s_t.t[0:rows, 0:1]),
                  reads=[x_t], writes=[junk, s_t])
            kb.op("act", lambda e: e.activation(out=s_t.t[0:rows, 1:2], in_=s_t.t[0:rows, 0:1], func=AF.Sqrt,
                                                scale=1.0 / D, bias=EPS), reads=[s_t], writes=[s_t])
            kb.op("dve", lambda e: e.reciprocal(out=s_t.t[0:rows, 2:3], in_=s_t.t[0:rows, 1:2]),
                  reads=[s_t], writes=[s_t])
            kb.op("dve", lambda e: e.scalar_tensor_tensor(out=h_b.t[0:rows, :], in0=x_t.t[0:rows, :],
                                                          scalar=s_t.t[0:rows, 2:3], in1=g_bc.t[0:rows, :],
                                                          op0=ALU.mult, op1=ALU.mult),
                  reads=[x_t, s_t, g_bc], writes=[h_b])
            for j in range(8):
                kb.op("pe", lambda e: e.transpose(out=p_t.t[:, j, 0:rows], in_=h_b.t[0:rows, 128 * j:128 * j + 128],
                                                  identity=ident.t[0:rows, 0:rows]),
                      reads=[h_b, ident], writes=[p_t])
            eng = "act" if i % 2 == 0 else "dve"
            if eng == "act":
                kb.op("act", lambda e: e.activation(out=hT.t[:, :, 128 * i:128 * i + rows], in_=p_t.t[:, :, 0:rows],
                                                    func=AF.Copy), reads=[p_t], writes=[hT])
            else:
                kb.op("dve", lambda e: e.tensor_copy(out=hT.t[:, :, 128 * i:128 * i + rows], in_=p_t.t[:, :, 0:rows]),
                      reads=[p_t], writes=[hT])

        if STOP == 1:
            kb.barrier(); return
        tblocks = [(TB * b, TB) for b in range(NTB)] + [(T, NS)]

        def load_wfm(slot, c0):
            wb = wfm[slot]
            kb.dma("pool", wb.t[:, :, :], w_in[:, c0:c0 + 128].rearrange("(k p) c -> p k c", p=128), writes=[wb])
            return wb

        pmi = [0]

        def next_pm():
            p = pm[pmi[0] % len(pm)]
            pmi[0] += 1
            return p

        def fm_matmul(wb, t0, n):
            p = next_pm()
            for k in range(8):
                kb.op("pe", lambda e: e.matmul(p.t[:, 0:n], lhsT=wb.t[:, k, :], rhs=hT.t[:, k, t0:t0 + n],
                                               start=(k == 0), stop=(k == 7)),
                      reads=[wb, hT], writes=[p])
            return p

        cnt = [0]

        uT_s = g["uT_s"]
        for cc in range(8):
            wa = load_wfm((2 * cc) % 4, OFF_U2 + 128 * cc)
            wg = load_wfm((2 * cc + 1) % 4, OFF_U2 + 1024 + 128 * cc)
            for (t0, n) in tblocks:
                pa = fm_matmul(wa, t0, n)
                pg = fm_matmul(wg, t0, n)
                sg = sgt[cnt[0] % 2]
                ob = obf[cnt[0] % 3]
                cnt[0] += 1
                kb.op("act", lambda e: e.activation(out=sg.t[:, 0:n], in_=pg.t[:, 0:n], func=AF.Sigmoid),
                      reads=[pg], writes=[sg])
                kb.op("dve", lambda e: e.tensor_tensor(out=ob.t[:, 0:n], in0=pa.t[:, 0:n], in1=sg.t[:, 0:n], op=ALU.mult),
                      reads=[pa, sg], writes=[ob])
                if t0 + n == T:
                    kb.op("dve", lambda e: e.tensor_tensor(out=utail.t[:, cc, 0:32], in0=pa.t[:, n - 32:n],
                                                           in1=sg.t[:, n - 32:n], op=ALU.mult),
                          reads=[pa, sg], writes=[utail])
                if t0 == T:
                    kb.op("dve", lambda e: e.tensor_tensor(out=utail.t[:, cc, 32:32 + NS], in0=pa.t[:, 0:n],
                                                           in1=sg.t[:, 0:n], op=ALU.mult),
                          reads=[pa, sg], writes=[utail])
                kb.dma("sp", uT_s[128 * cc:128 * cc + 128, t0:t0 + n], ob.t[:, 0:n], reads=[ob])

        if STOP == 2:
            kb.barrier(); return
        ptl = pm[0]
        for cc in range(8):
            kb.op("pe", lambda e: e.transpose(out=ptl.t[0:32 + NS, 128 * (cc % 4):128 * (cc % 4) + 128],
                                              in_=utail.t[:, cc, :], identity=identf.t[:, :]),
                  reads=[utail, identf], writes=[ptl])
            if cc % 4 == 3:
                h0 = 512 * (cc // 4)
                kb.op("act", lambda e: e.activation(out=uto.t[:, h0:h0 + 512], in_=ptl.t[0:32 + NS, :], func=AF.Copy),
                      reads=[ptl], writes=[uto])
        kb.dma("sp", g["conv_p"][:, :], uto.t[2:32, :], reads=[uto])
        for s in range(NS):
            kb.dma("sp", g["conv_s"][s, 29:30, :], uto.t[32 + s:33 + s, :], reads=[uto])
            kb.dma("pool", g["conv_s"][s, 0:29, :], g["st_conv"][s, 1:30, :])

        if STOP == 3:
            kb.barrier(); return
        fm_cols = [(OFF_Q + 128 * j, g["qT_s"][128 * j:128 * j + 128, :], 0.125) for j in range(8)]
        for bi, off in enumerate((OFF_KC, OFF_KS, OFF_KW)):
            for j in range(2):
                fm_cols.append((off + 128 * j, g["kT_s"][bi, 128 * j:128 * j + 128, :], 1.0))
        for ci, (c0, dst, scale) in enumerate(fm_cols):
            wb = load_wfm(ci % 4, c0)
            for (t0, n) in tblocks:
                p = fm_matmul(wb, t0, n)
                ob = obf[cnt[0] % 3]
                cnt[0] += 1
                if cnt[0] % 2 == 0:
                    kb.op("act", lambda e: e.activation(out=ob.t[:, 0:n], in_=p.t[:, 0:n], func=AF.Copy, scale=scale),
                          reads=[p], writes=[ob])
                else:
                    kb.op("dve", lambda e: e.tensor_scalar(out=ob.t[:, 0:n], in0=p.t[:, 0:n], scalar1=scale, scalar2=None,
                                                           op0=ALU.mult), reads=[p], writes=[ob])
                kb.dma("sp", dst[:, t0:t0 + n], ob.t[:, 0:n], reads=[ob])

        if STOP == 4:
            kb.barrier(); return
        tm_blocks = [("kv", 0, OFF_KC, 512), ("kv", 1, OFF_KS, 512), ("kv", 2, OFF_KW, 512)]
        c0 = OFF_NG
        while c0 < DIN:
            w = min(512, DIN - c0)
            tm_blocks.append(("gate", None, c0, w))
            c0 += w
        kv_out_p = [g["cmp_p"], g["slc_p"], None]
        kv_out_s = [g["cmp_s"], g["slc_s"], None]
        for bi, (kind, which, c0, w) in enumerate(tm_blocks):
            wb = wtm[bi % 2]
            for q0 in range(0, w, 128):
                q1 = min(w, q0 + 128)
                kb.dma("pool", wb.t[:, :, q0:q1], w_in[:, c0 + q0:c0 + q1].rearrange("(k p) c -> p k c", p=128), writes=[wb])
            for i in range(ntile - (1 if SKIP & 16 else 0)):
                rows = 128 if i < NT else NS
                r0 = 128 * i
                p = next_pm()
                for k in range(8):
                    kb.op("pe", lambda e: e.matmul(p.t[0:rows, 0:w], lhsT=hT.t[:, k, r0:r0 + rows], rhs=wb.t[:, k, 0:w],
                                                   start=(k == 0), stop=(k == 7)),
                          reads=[wb, hT], writes=[p])
                cnt[0] += 1
                if kind == "kv":
                    of = of32[cnt[0] % 3]
                    kb.op("act", lambda e: e.activation(out=of.t[0:rows, :], in_=p.t[0:rows, :], func=AF.Copy),
                          reads=[p], writes=[of])
                    if which < 2 and not (SKIP & 4):
                        dst = kv_out_p[which][r0:r0 + rows, :] if i < NT else kv_out_s[which][:, :]
                        kb.dma("sp", dst, of.t[0:rows, :], reads=[of])
                    elif which == 2 and not (SKIP & 4):
                        if i < NT and r0 >= T - 512:
                            kb.dma("sp", g["win_p"][r0 - (T - 512):r0 - (T - 512) + 128, :], of.t[0:rows, :], reads=[of])
                        if i == NT:
                            for s in range(NS):
                                kb.dma("sp", g["win_s"][s, 511:512, :], of.t[s:s + 1, :], reads=[of])
                                if not (SKIP & 1):
                                    kb.dma("pool", g["win_s"][s, 0:511, :], g["st_win"][s, 1:512, :])
                    if which >= 1 and not (SKIP & 8):
                        ov = ovb[cnt[0] % 2]
                        if True:
                            kb.op("dve", lambda e: e.tensor_copy(out=ov.t[0:rows, :], in_=of.t[0:rows, 256:512]),
                                  reads=[of], writes=[ov])
                        else:
                            kb.op("dve", lambda e: e.tensor_copy(out=ov.t[0:rows, :], in_=p.t[0:rows, 256:512]),
                                  reads=[p], writes=[ov])
                        kb.dma(os.environ.get("VQ", "sp"), g["v_s"][which - 1, r0:r0 + rows, :], ov.t[0:rows, :], reads=[ov])
                elif not (SKIP & 2):
                    lo = 0
                    if c0 == OFF_NG:
                        of = of32[cnt[0] % 3]
                        ob = obf[cnt[0] % 3]
                        kb.op("act", lambda e: e.activation(out=of.t[0:rows, :], in_=p.t[0:rows, :], func=AF.Sigmoid),
                              reads=[p], writes=[of])
                        kb.dma("sp", g["ng_s"][r0:r0 + rows, :], of.t[0:rows, :], reads=[of])
                        kb.op("dve", lambda e: e.tensor_copy(out=ob.t[0:rows, 48:512], in_=of.t[0:rows, 48:512]),
                              reads=[of], writes=[ob])
                        kb.dma("sp", g["sg_s"][r0:r0 + rows, 0:464], ob.t[0:rows, 48:512], reads=[ob])
                        continue
                    ob = obf[cnt[0] % 3]
                    kb.op("act", lambda e: e.activation(out=ob.t[0:rows, lo:w], in_=p.t[0:rows, lo:w], func=AF.Sigmoid),
                          reads=[p], writes=[ob])
                    d0 = c0 + lo - OFF_GC
                    kb.dma("sp", g["sg_s"][r0:r0 + rows, d0:d0 + (w - lo)], ob.t[0:rows, lo:w], reads=[ob])
        kb.barrier()


def phase_b(kb, cfg, g):
    T, NS, TT, NT, TB, NTB = cfg.T, cfg.NS, cfg.TT, cfg.NT, cfg.TB, cfg.NTB
    uT_s, sg_s, mc_s = g["uT_s"], g["sg_s"], g["mc_s"]
    es = ExitStack()
    with es:
        identf = kb.sb("b_identf", [128, 128], F32, es)
        ones = kb.sb("b_ones", [128, 128], F32, es)
        raw = kb.sb("b_raw", [34, D], F32, es)
        wT = kb.sb("b_wT", [128, 8, 34], F32, es)
        wpw = kb.sb("b_wpw", [128, 8, D], BF16, es)
        ut = [kb.sb(f"b_ut{i}", [128, 30 + TB], BF16, es) for i in range(3)]
        acc = kb.sb("b_acc", [128, 8, TB], F32, es)
        ysq = kb.sb("b_ysq", [128, 8, TB], F32, es)
        sT = kb.sb("b_sT", [128, 8, TB], BF16, es)
        mean = kb.sb("b_mean", [128, TB], F32, es)
        ex2 = kb.sb("b_ex2", [128, TB], F32, es)
        rstd = kb.sb("b_rstd", [128, TB], F32, es)
        tmp = [kb.sb(f"b_tmp{i}", [128, TB], F32, es) for i in range(2)]
        sgc = [kb.sb(f"b_sgc{i}", [128, D], BF16, es) for i in range(2)]
        mc = [kb.sb(f"b_mc{i}", [128, D], F32, es) for i in range(2)]
        stt = kb.sb("b_st", [30, D], F32, es)
        uext = kb.sb("b_uext", [128, 8, NS, 31], F32, es)
        unew = kb.sb("b_unew", [128, 8, NS], BF16, es)
        prod = kb.sb("b_prod", [128, 8, 31], F32, es)
        ys = kb.sb("b_ys", [128, 8], F32, es)
        pt = kb.ps("b_pt", [128, 8, 64], F32, es)
        pmean = kb.ps("b_pmean", [128, 512], F32, es)
        psq = kb.ps("b_psq", [128, 512], F32, es)
        pc = [kb.ps(f"b_pc{i}", [128, 512], F32, es) for i in range(4)]

        kb.dma("sp", identf.t[:], g["ident_in"][:, :], writes=[identf])
        kb.dma("sp", raw.t[:], g["cvec"][:, :], writes=[raw])
        kb.op("dve", lambda e: e.memset(ones.t[:], 1.0 / D), writes=[ones])
        for q0 in range(0, D, 256):
            kb.dma("pool", wpw.t[:, :, q0:q0 + 256], g["w_pw"][:, q0:q0 + 256].rearrange("(k p) c -> p k c", p=128),
                   writes=[wpw])
        for cc in range(8):
            kb.op("pe", lambda e: e.transpose(out=pt.t[:, cc, 0:34], in_=raw.t[0:34, 128 * cc:128 * cc + 128],
                                              identity=identf.t[0:34, 0:34]), reads=[raw, identf], writes=[pt])
        kb.op("act", lambda e: e.activation(out=wT.t[:, :, :], in_=pt.t[:, :, 0:34], func=AF.Copy), reads=[pt], writes=[wT])

        for s in range(NS):
            kb.dma("sp", stt.t[:, :], g["st_conv"][s, :, :], writes=[stt])
            for cc in range(8):
                kb.op("pe", lambda e: e.transpose(out=pt.t[:, cc, 0:30], in_=stt.t[0:30, 128 * cc:128 * cc + 128],
                                                  identity=identf.t[0:30, 0:30]), reads=[stt, identf], writes=[pt])
            kb.op("act", lambda e: e.activation(out=uext.t[:, :, s, 0:30], in_=pt.t[:, :, 0:30], func=AF.Copy),
                  reads=[pt], writes=[uext])
        for cc in range(8):
            kb.dma("sp", unew.t[:, cc, :], uT_s[128 * cc:128 * cc + 128, T:T + NS], writes=[unew])
        kb.op("dve", lambda e: e.tensor_copy(out=uext.t[:, :, :, 30], in_=unew.t[:, :, :]), reads=[unew], writes=[uext])

        tblocks = [(TB * b, TB) for b in range(NTB)] + [(T, NS)]
        ui = 0
        tix = 0
        for (t0, n) in tblocks:
            if t0 < T:
                for cc in range(8):
                    u = ut[ui % 3]
                    ui += 1
                    if t0 == 0:
                        kb.op("pool", lambda e: e.memset(u.t[:, 0:30], 0.0), writes=[u])
                        kb.dma("sp", u.t[:, 30:30 + n], uT_s[128 * cc:128 * cc + 128, 0:n], writes=[u])
                    else:
                        kb.dma("sp", u.t[:, 0:30 + n], uT_s[128 * cc:128 * cc + 128, t0 - 30:t0 + n], writes=[u])
                    eng = "dve"
                    kb.op(eng, lambda e: e.tensor_scalar(out=acc.t[:, cc, 0:n], in0=u.t[:, 0:n], scalar1=wT.t[:, cc, 0:1],
                                                         scalar2=wT.t[:, cc, 31:32], op0=ALU.mult, op1=ALU.add),
                          reads=[u, wT], writes=[acc])
                    for k in range(1, CW):
                        kb.op(eng, lambda e: e.scalar_tensor_tensor(out=acc.t[:, cc, 0:n], in0=u.t[:, k:k + n],
                                                                    scalar=wT.t[:, cc, k:k + 1], in1=acc.t[:, cc, 0:n],
                                                                    op0=ALU.mult, op1=ALU.add),
                              reads=[u, wT, acc], writes=[acc])
            else:
                for s in range(NS):
                    kb.op("dve", lambda e: e.tensor_tensor(out=prod.t[:, :, :], in0=uext.t[:, :, s, :], in1=wT.t[:, :, 0:31],
                                                           op=ALU.mult), reads=[uext, wT], writes=[prod])
                    kb.op("dve", lambda e: e.tensor_reduce(out=ys.t[:, :], in_=prod.t[:, :, :], axis=AX.X, op=ALU.add),
                          reads=[prod], writes=[ys])
                    kb.op("dve", lambda e: e.tensor_tensor(out=acc.t[:, :, s], in0=ys.t[:, :], in1=wT.t[:, :, 31], op=ALU.add),
                          reads=[ys, wT], writes=[acc])
            kb.op("act", lambda e: e.activation(out=ysq.t[:, :, 0:n], in_=acc.t[:, :, 0:n], func=AF.Square),
                  reads=[acc], writes=[ysq])
            for cc in range(8):
                kb.op("pe", lambda e: e.matmul(pmean.t[:, 0:n], lhsT=ones.t[:, :], rhs=acc.t[:, cc, 0:n],
                                               start=(cc == 0), stop=(cc == 7)), reads=[ones, acc], writes=[pmean])
            for cc in range(8):
                kb.op("pe", lambda e: e.matmul(psq.t[:, 0:n], lhsT=ones.t[:, :], rhs=ysq.t[:, cc, 0:n],
                                               start=(cc == 0), stop=(cc == 7)), reads=[ones, ysq], writes=[psq])
            kb.op("act", lambda e: e.activation(out=mean.t[:, 0:n], in_=pmean.t[:, 0:n], func=AF.Copy),
                  reads=[pmean], writes=[mean])
            kb.op("dve", lambda e: e.tensor_copy(out=ex2.t[:, 0:n], in_=psq.t[:, 0:n]), reads=[psq], writes=[ex2])
            kb.op("dve", lambda e: e.tensor_tensor(out=rstd.t[:, 0:n], in0=mean.t[:, 0:n], in1=mean.t[:, 0:n], op=ALU.mult),
                  reads=[mean], writes=[rstd])
            kb.op("dve", lambda e: e.tensor_tensor(out=ex2.t[:, 0:n], in0=ex2.t[:, 0:n], in1=rstd.t[:, 0:n], op=ALU.subtract),
                  reads=[ex2, rstd], writes=[ex2])
            kb.op("act", lambda e: e.activation(out=ex2.t[:, 0:n], in_=ex2.t[:, 0:n], func=AF.Sqrt, bias=EPS),
                  reads=[ex2], writes=[ex2])
            kb.op("dve", lambda e: e.reciprocal(out=rstd.t[:, 0:n], in_=ex2.t[:, 0:n]), reads=[ex2], writes=[rstd])
            for cc in range(8):
                tm_ = tmp[cc % 2]
                eng = "dve" if cc % 2 == 0 else "pool"
                kb.op(eng, lambda e: e.tensor_tensor(out=tm_.t[:, 0:n], in0=acc.t[:, cc, 0:n], in1=mean.t[:, 0:n],
                                                     op=ALU.subtract), reads=[acc, mean], writes=[tm_])
                kb.op(eng, lambda e: e.tensor_tensor(out=tm_.t[:, 0:n], in0=tm_.t[:, 0:n], in1=rstd.t[:, 0:n], op=ALU.mult),
                      reads=[tm_, rstd], writes=[tm_])
                kb.op("act", lambda e: e.activation(out=sT.t[:, cc, 0:n], in_=tm_.t[:, 0:n], func=AF.Silu,
                                                    scale=wT.t[:, cc, 32:33], bias=wT.t[:, cc, 33:34]),
                      reads=[tm_, wT], writes=[sT])
            for r in range(0, n, 128):
                rows = min(128, n - r)
                tok = t0 + r
                sgt_ = sgc[tix % 2]
                mct = mc[tix % 2]
                tix += 1
                kb.dma("sp", sgt_.t[0:rows, :], sg_s[tok:tok + rows, 0:D], writes=[sgt_])
                for half in range(2):
                    p = pc[(2 * tix + half) % 4]
                    for cc in range(8):
                        kb.op("pe", lambda e: e.matmul(p.t[0:rows, :], lhsT=sT.t[:, cc, r:r + rows],
                                                       rhs=wpw.t[:, cc, 512 * half:512 * half + 512],
                                                       start=(cc == 0), stop=(cc == 7)), reads=[sT, wpw], writes=[p])
                    kb.op("dve", lambda e: e.tensor_tensor(out=mct.t[0:rows, 512 * half:512 * half + 512], in0=p.t[0:rows, :],
                                                           in1=sgt_.t[0:rows, 512 * half:512 * half + 512], op=ALU.mult),
                          reads=[p, sgt_], writes=[mct])
                kb.dma("sp", mc_s[tok:tok + rows, :], mct.t[0:rows, :], reads=[mct])
        kb.barrier()


def phase_e(kb, cfg, g):
    T, NS, TT, NT, TB, NTB = cfg.T, cfg.NS, cfg.TT, cfg.NT, cfg.TB, cfg.NTB
    xp, xs = g["xp"], g["xs"]
    mc_s, x1_s, aT_s = g["mc_s"], g["x1_s"], g["aT_s"]
    ntile = NT + 1
    tblocks = [(TB * b, TB) for b in range(NTB)] + [(T, NS)]
    es0 = ExitStack()
    with es0:
        h2T = kb.sb("e_h2T", [128, 8, TT], BF16, es0)
        ident = kb.sb("e_ident", [128, 128], BF16, es0)
        kb.dma("pool", ident.t[:], g["ident_in"][:, :], writes=[ident])
        es = ExitStack()
        with es:
            wo = kb.sb("e_wo", [128, 8, D], BF16, es)
            g2 = kb.sb("e_g2", [128, D], F32, es)
            mt = [kb.sb(f"e_mt{i}", [128, D], F32, es) for i in range(2)]
            mb = [kb.sb(f"e_mb{i}", [128, D], BF16, es) for i in range(2)]
            e_at = [kb.sb(f"e_ao{i}", [128, D], F32, es) for i in range(2)]
            e_sga = [kb.sb(f"e_sga{i}", [128, D], BF16, es) for i in range(2)]
            mT = [kb.sb(f"e_mT{i}", [128, 8, 128], BF16, es) for i in range(2)]
            xt = [kb.sb(f"e_xt{i}", [128, D], F32, es) for i in range(2)]
            x1 = [kb.sb(f"e_x1{i}", [128, D], F32, es) for i in range(2)]
            hb = [kb.sb(f"e_hb{i}", [128, D], BF16, es) for i in range(2)]
            junk = kb.sb("e_junk", [128, D], BF16, es)
            stat = [kb.sb(f"e_stat{i}", [128, 4], F32, es) for i in range(2)]
            psT = [kb.ps(f"e_psT{i}", [128, 8, 128], BF16, es) for i in range(2)]
            psH = [kb.ps(f"e_psH{i}", [128, 8, 128], BF16, es) for i in range(2)]
            px = [kb.ps(f"e_px{i}", [128, 512], F32, es) for i in range(4)]
            for q0 in range(0, D, 256):
                kb.dma("pool", wo.t[:, :, q0:q0 + 256], g["w_out"][:, q0:q0 + 256].rearrange("(k p) c -> p k c", p=128),
                       writes=[wo])
            kb.dma("sp", g2.t[:], g["ln2_g"][0:1, :].to_broadcast([128, D]), writes=[g2])
            for i in range(ntile):
                rows = 128 if i < NT else NS
                r0 = 128 * i
                m_t, m_b, m_T, x_t, x_1, h_b, s_t = mt[i % 2], mb[i % 2], mT[i % 2], xt[i % 2], x1[i % 2], hb[i % 2], stat[i % 2]
                p_T, p_H = psT[i % 2], psH[i % 2]
                a_t, sga = e_at[i % 2], e_sga[i % 2]
                kb.dma("sp", m_t.t[0:rows, :], mc_s[r0:r0 + rows, :], writes=[m_t])
                kb.dma("sp", x_t.t[0:rows, :], xp[r0:r0 + rows, :] if i < NT else xs[:, :], writes=[x_t])
                kb.dma("sp", a_t.t[0:rows, :], g["ao_s"][r0:r0 + rows, :], writes=[a_t])
                kb.dma("sp", sga.t[0:rows, :], g["sg_s"][r0:r0 + rows, D:2 * D], writes=[sga])
                kb.op("pool", lambda e: e.tensor_tensor(out=a_t.t[0:rows, :], in0=a_t.t[0:rows, :], in1=sga.t[0:rows, :],
                                                        op=ALU.mult), reads=[a_t, sga], writes=[a_t])
                kb.op("pool", lambda e: e.tensor_tensor(out=m_b.t[0:rows, :], in0=a_t.t[0:rows, :], in1=m_t.t[0:rows, :],
                                                        op=ALU.add), reads=[a_t, m_t], writes=[m_b])
                for j in range(8):
                    kb.op("pe", lambda e: e.transpose(out=p_T.t[:, j, 0:rows], in_=m_b.t[0:rows, 128 * j:128 * j + 128],
                                                      identity=ident.t[0:rows, 0:rows]), reads=[m_b, ident], writes=[p_T])
                kb.op("dve", lambda e: e.tensor_copy(out=m_T.t[:, :, 0:rows], in_=p_T.t[:, :, 0:rows]), reads=[p_T], writes=[m_T])
                for half in range(2):
                    p = px[(2 * i + half) % 4]
                    for k in range(8):
                        kb.op("pe", lambda e: e.matmul(p.t[0:rows, :], lhsT=m_T.t[:, k, 0:rows],
                                                       rhs=wo.t[:, k, 512 * half:512 * half + 512],
                                                       start=(k == 0), stop=(k == 7)), reads=[m_T, wo], writes=[p])
                    kb.op("dve", lambda e: e.tensor_tensor(out=x_1.t[0:rows, 512 * half:512 * half + 512], in0=p.t[0:rows, :],
                                                           in1=x_t.t[0:rows, 512 * half:512 * half + 512], op=ALU.add),
                          reads=[p, x_t], writes=[x_1])
                kb.dma("sp", x1_s[r0:r0 + rows, :], x_1.t[0:rows, :], reads=[x_1])
                kb.op("act", lambda e: e.activation(out=junk.t[0:rows, :], in_=x_1.t[0:rows, :], func=AF.Square,
                                                    accum_out=s_t.t[0:rows, 0:1]), reads=[x_1], writes=[junk, s_t])
                kb.op("act", lambda e: e.activation(out=s_t.t[0:rows, 1:2], in_=s_t.t[0:rows, 0:1], func=AF.Sqrt,
                                                    scale=1.0 / D, bias=EPS), reads=[s_t], writes=[s_t])
                kb.op("dve", lambda e: e.reciprocal(out=s_t.t[0:rows, 2:3], in_=s_t.t[0:rows, 1:2]), reads=[s_t], writes=[s_t])
                kb.op("dve", lambda e: e.scalar_tensor_tensor(out=h_b.t[0:rows, :], in0=x_1.t[0:rows, :],
                                                              scalar=s_t.t[0:rows, 2:3], in1=g2.t[0:rows, :],
                                                              op0=ALU.mult, op1=ALU.mult), reads=[x_1, s_t, g2], writes=[h_b])
                for j in range(8):
                    kb.op("pe", lambda e: e.transpose(out=p_H.t[:, j, 0:rows], in_=h_b.t[0:rows, 128 * j:128 * j + 128],
                                                      identity=ident.t[0:rows, 0:rows]), reads=[h_b, ident], writes=[p_H])
                kb.op("act", lambda e: e.activation(out=h2T.t[:, :, r0:r0 + rows], in_=p_H.t[:, :, 0:rows], func=AF.Copy),
                      reads=[p_H], writes=[h2T])
            kb.barrier()
        es = ExitStack()
        with es:
            wfm = [kb.sb(f"e_wfm{i}", [128, 8, 128], BF16, es) for i in range(3)]
            rl = [kb.sb(f"e_rl{i}", [128, 512], F32, es) for i in range(2)]
            ab = [kb.sb(f"e_ab{i}", [128, 512], BF16, es) for i in range(3)]
            pm = [kb.ps(f"e_pm{i}", [128, 512], F32, es) for i in range(6)]
            c = 0
            for f in range(DFF // 128):
                wb = wfm[f % 3]
                kb.dma("pool", wb.t[:, :, :], g["w_up"][:, 128 * f:128 * f + 128].rearrange("(k p) c -> p k c", p=128),
                       writes=[wb])
                for (t0, n) in tblocks:
                    p = pm[c % 6]
                    r_, a_ = rl[c % 2], ab[c % 3]
                    c += 1
                    for k in range(8):
                        kb.op("pe", lambda e: e.matmul(p.t[:, 0:n], lhsT=wb.t[:, k, :], rhs=h2T.t[:, k, t0:t0 + n],
                                                       start=(k == 0), stop=(k == 7)), reads=[wb, h2T], writes=[p])
                    kb.op("act", lambda e: e.activation(out=r_.t[:, 0:n], in_=p.t[:, 0:n], func=AF.Relu), reads=[p], writes=[r_])
                    eng = "dve" if c % 2 == 0 else "pool"
                    kb.op(eng, lambda e: e.tensor_tensor(out=a_.t[:, 0:n], in0=r_.t[:, 0:n], in1=r_.t[:, 0:n], op=ALU.mult),
                          reads=[r_], writes=[a_])
                    kb.dma("sp", aT_s[128 * f:128 * f + 128, t0:t0 + n], a_.t[:, 0:n], reads=[a_])
            kb.barrier()
    es = ExitStack()
    with es:
        NF = DFF // 128
        wd = kb.sb("e_wd", [128, NF, D], BF16, es)
        gf = kb.sb("e_gf", [128, D], F32, es)
        at = [kb.sb(f"e_at{i}", [128, NF, 128], BF16, es) for i in range(2)]
        x1 = [kb.sb(f"e3_x1{i}", [128, D], F32, es) for i in range(2)]
        yp = [kb.sb(f"e_yp{i}", [128, D], F32, es) for i in range(2)]
        yo = [kb.sb(f"e_yo{i}", [128, D], F32, es) for i in range(2)]
        junk = kb.sb("e3_junk", [128, D], BF16, es)
        stat = [kb.sb(f"e3_stat{i}", [128, 4], F32, es) for i in range(2)]
        py = [kb.ps(f"e_py{i}", [128, 512], F32, es) for i in range(4)]
        for f0 in range(0, NF, 4):
            for q0 in range(0, D, 512):
                kb.dma("pool", wd.t[:, f0:f0 + 4, q0:q0 + 512],
                       g["w_down"][128 * f0:128 * f0 + 512, q0:q0 + 512].rearrange("(f p) c -> p f c", p=128), writes=[wd])
        kb.dma("sp", gf.t[:], g["lnf_g"][0:1, :].to_broadcast([128, D]), writes=[gf])
        for i in range(ntile):
            rows = 128 if i < NT else NS
            r0 = 128 * i
            a_t, x_1, y_p, y_o, s_t = at[i % 2], x1[i % 2], yp[i % 2], yo[i % 2], stat[i % 2]
            for f0 in range(0, NF, 8):
                kb.dma("sp", a_t.t[:, f0:f0 + 8, 0:rows],
                       aT_s[128 * f0:128 * f0 + 1024, r0:r0 + rows].rearrange("(f p) t -> p f t", p=128), writes=[a_t])
            kb.dma("sp", x_1.t[0:rows, :], x1_s[r0:r0 + rows, :], writes=[x_1])
            for half in range(2):
                p = py[(2 * i + half) % 4]
                for f in range(NF):
                    kb.op("pe", lambda e: e.matmul(p.t[0:rows, :], lhsT=a_t.t[:, f, 0:rows],
                                                   rhs=wd.t[:, f, 512 * half:512 * half + 512],
                                                   start=(f == 0), stop=(f == NF - 1)), reads=[a_t, wd], writes=[p])
                kb.op("dve", lambda e: e.tensor_tensor(out=y_p.t[0:rows, 512 * half:512 * half + 512], in0=p.t[0:rows, :],
                                                       in1=x_1.t[0:rows, 512 * half:512 * half + 512], op=ALU.add),
                      reads=[p, x_1], writes=[y_p])
            kb.op("act", lambda e: e.activation(out=junk.t[0:rows, :], in_=y_p.t[0:rows, :], func=AF.Square,
                                                accum_out=s_t.t[0:rows, 0:1]), reads=[y_p], writes=[junk, s_t])
            kb.op("act", lambda e: e.activation(out=s_t.t[0:rows, 1:2], in_=s_t.t[0:rows, 0:1], func=AF.Sqrt,
                                                scale=1.0 / D, bias=EPS), reads=[s_t], writes=[s_t])
            kb.op("dve", lambda e: e.reciprocal(out=s_t.t[0:rows, 2:3], in_=s_t.t[0:rows, 1:2]), reads=[s_t], writes=[s_t])
            kb.op("dve", lambda e: e.scalar_tensor_tensor(out=y_o.t[0:rows, :], in0=y_p.t[0:rows, :],
                                                          scalar=s_t.t[0:rows, 2:3], in1=gf.t[0:rows, :],
                                                          op0=ALU.mult, op1=ALU.mult), reads=[y_p, s_t, gf], writes=[y_o])
            kb.dma("sp", g["y_p"][r0:r0 + rows, :] if i < NT else g["y_s"][:, :], y_o.t[0:rows, :], reads=[y_o])
        kb.barrier()


SLOPES = [float(np.float32(2.0) ** np.float32(-8.0 * (h + 1) / NH)) for h in range(NH)]


def phase_att(kb, cfg, g):
    T, NS, TT, NT = cfg.T, cfg.NS, cfg.TT, cfg.NT
    NB = 8 * NT - 1
    NCT = (NB + 127) // 128
    NSB = T // 64
    do_sel = NSB > 16
    qT_s, kT_s, v_s = g["qT_s"], g["kT_s"], g["v_s"]
    es = ExitStack()
    with es:
        ident = kb.sb("a_ident", [128, 128], BF16, es)
        relc = kb.sb("a_relc", [128, 4096], F32, es)
        rels = kb.sb("a_rels", [128, T + 128], F32, es)
        relw = kb.sb("a_relw", [128, 768], F32, es)
        fmk = kb.sb("a_fmk", [128, 32, 63], F32, es)
        ksa = kb.sb("a_ksa", [128, 4, T], BF16, es)
        vsa = kb.sb("a_vsa", [128, NT, 4, 65], BF16, es)
        ckT = kb.sb("a_ckT", [64, 4, 256], BF16, es)
        rc = kb.sb("a_rc", [128, 2, 4, 128], BF16, es)
        kb.dma("pool", ident.t[:], g["ident_in"][:, :], writes=[ident])
        kb.dma("sp", relc.t[:], g["relc"][:, :], writes=[relc])
        kb.dma("sp", rels.t[:], g["rels"][:, 0:T + 128], writes=[rels])
        kb.dma("sp", relw.t[:], g["relw"][:, :], writes=[relw])
        kb.dma("sp", fmk.t[:], g["fmk"][:, :].rearrange("p (a b) -> p a b", b=63), writes=[fmk])
        for kv in range(4):
            kb.dma("sp", ksa.t[0:64, kv, :], kT_s[1, 64 * kv:64 * kv + 64, 0:T], writes=[ksa])
            kb.dma("pool", ksa.t[64:128, kv, :], g["onehot"][:, 0:T], writes=[ksa])
        kb.op("pool", lambda e: e.memset(vsa.t[:, :, :, 64:65], 1.0), writes=[vsa])
        for kv in range(4):
            for a0 in range(0, NT, 8):
                a1 = min(NT, a0 + 8)
                kb.dma("sp", vsa.t[:, a0:a1, kv, 0:64],
                       v_s[0, 128 * a0:128 * a1, 64 * kv:64 * kv + 64].rearrange("(a p) d -> p a d", p=128), writes=[vsa])
        kb.op("pool", lambda e: e.memset(ckT.t[:], 0.0), writes=[ckT])
        kb.op("pool", lambda e: e.memset(rc.t[:], 0.0), writes=[rc])
        kb.op("pool", lambda e: e.memset(rc.t[:, :, :, 64:65], 1.0), reads=[rc], writes=[rc])
        for kv in range(4):
            kb.dma("pool", rc.t[:, :, kv, 65:128], g["cmat"][:, :].rearrange("p (a b) -> p a b", b=63), writes=[rc])

        es1 = ExitStack()
        with es1:
            wpool = kb.sb("c_wpool", [128, 2, 16], F32, es1)
            cposd = kb.sb("c_cposd", [32, 2, 128], F32, es1)
            cw = kb.sb("c_cw", [32, 2], F32, es1)
            w1bd = kb.sb("c_w1bd", [128, 2, 128], F32, es1)
            w2bd = kb.sb("c_w2bd", [128, 2, 128], F32, es1)
            w2sel = kb.sb("c_w2sel", [128, 2, 64], F32, es1)
            pg = [kb.sb(f"c_pg{i}", [128, 512], F32, es1) for i in range(2)]
            ab = kb.sb("c_ab", [128, 4, NT, 16], F32, es1)
            af = kb.sb("c_af", [128, 4, 8 * NT], F32, es1)
            bf = kb.sb("c_bf", [128, 4, 8 * NT], F32, es1)
            pooled = kb.sb("c_pooled", [128, 4, 8 * NT], F32, es1)
            hid = kb.sb("c_hid", [128, 4, 8 * NT], F32, es1)
            pe_sb = kb.sb("c_pe", [128, 2], F32, es1)
            pp = [kb.ps(f"c_pp{i}", [128, 32, 16], F32, es1) for i in range(4)]
            pe_ps = kb.ps("c_peps", [128, 2], F32, es1)
            hp = [kb.ps(f"c_hp{i}", [128, 512], F32, es1) for i in range(2)]
            for nm, t_, src in (("wpool", wpool, g["wpool"][:, :].rearrange("p (a b) -> p a b", b=16)),
                                ("cposd", cposd, g["cposd"][:, :].rearrange("p (a b) -> p a b", b=128)),
                                ("cw", cw, g["cw"][:, :]),
                                ("w1bd", w1bd, g["w1bd"][:, :].rearrange("p (a b) -> p a b", b=128)),
                                ("w2bd", w2bd, g["w2bd"][:, :].rearrange("p (a b) -> p a b", b=128)),
                                ("w2sel", w2sel, g["w2sel"][:, :].rearrange("p (a b) -> p a b", b=64))):
                kb.dma("sp", t_.t[:], src, writes=[t_])
            for j in range(2):
                kb.op("pe", lambda e: e.matmul(pe_ps.t[:, j:j + 1], lhsT=cposd.t[:, j, :], rhs=cw.t[:, j:j + 1],
                                               start=True, stop=True), reads=[cposd, cw], writes=[pe_ps])
            kb.op("act", lambda e: e.activation(out=pe_sb.t[:, :], in_=pe_ps.t[:, :], func=AF.Copy), reads=[pe_ps], writes=[pe_sb])
            for i in range(NT):
                pgt = pg[i % 2]
                kb.dma("sp", pgt.t[:, :], g["cmp_p"][128 * i:128 * i + 128, :], writes=[pgt])
                for fc in range(4):
                    kb.op("pe", lambda e: e.matmul(pp[fc].t[:, i, :], lhsT=pgt.t[:, 128 * fc:128 * fc + 128],
                                                   rhs=wpool.t[:, fc // 2, :], start=True, stop=True),
                          reads=[pgt, wpool], writes=[pp[fc]])
            for fc in range(4):
                kb.op("act", lambda e: e.activation(out=ab.t[:, fc, :, :], in_=pp[fc].t[:, 0:NT, :], func=AF.Copy),
                      reads=[pp[fc]], writes=[ab])
            kb.op("dve", lambda e: e.tensor_copy(out=af.t[:, :, :].rearrange("p f (t c) -> p f t c", c=8), in_=ab.t[:, :, :, 0:8]),
                  reads=[ab], writes=[af])
            kb.op("dve", lambda e: e.tensor_copy(out=bf.t[:, :, :].rearrange("p f (t c) -> p f t c", c=8), in_=ab.t[:, :, :, 8:16]),
                  reads=[ab], writes=[bf])
            kb.op("dve", lambda e: e.tensor_tensor(out=pooled.t[:, :, 0:NB], in0=af.t[:, :, 0:NB], in1=bf.t[:, :, 1:NB + 1],
                                                   op=ALU.add), reads=[af, bf], writes=[pooled])
            for fc in range(4):
                kb.op("dve", lambda e: e.tensor_scalar(out=pooled.t[:, fc, 0:NB], in0=pooled.t[:, fc, 0:NB],
                                                       scalar1=pe_sb.t[:, fc // 2:fc // 2 + 1], scalar2=None, op0=ALU.add),
                      reads=[pooled, pe_sb], writes=[pooled])
            for fc in range(4):
                h_ = hp[fc % 2]
                kb.op("pe", lambda e: e.matmul(h_.t[:, 0:NB], lhsT=w1bd.t[:, fc // 2, :], rhs=pooled.t[:, fc, 0:NB],
                                               start=True, stop=True), reads=[w1bd, pooled], writes=[h_])
                kb.op("act", lambda e: e.activation(out=hid.t[:, fc, 0:NB], in_=h_.t[:, 0:NB], func=AF.Silu),
                      reads=[h_], writes=[hid])
            for kv in range(4):
                h_ = hp[kv % 2]
                kb.op("pe", lambda e: e.matmul(h_.t[0:64, 0:NB], lhsT=w2sel.t[:, kv % 2, :], rhs=hid.t[:, kv // 2, 0:NB],
                                               start=True, stop=True), reads=[w2sel, hid], writes=[h_])
                kb.op("act", lambda e: e.activation(out=ckT.t[0:64, kv, 0:NB], in_=h_.t[0:64, 0:NB], func=AF.Copy),
                      reads=[h_], writes=[ckT])
            for ct in range(NCT):
                nb = min(128, NB - 128 * ct)
                for fv in range(2):
                    h_ = hp[fv % 2]
                    kb.op("pe", lambda e: e.matmul(h_.t[0:nb, 0:128], lhsT=hid.t[:, 2 + fv, 128 * ct:128 * ct + nb],
                                                   rhs=w2bd.t[:, 1, :], start=True, stop=True), reads=[hid, w2bd], writes=[h_])
                    kb.op("act", lambda e: e.activation(out=rc.t[0:nb, ct, 2 * fv:2 * fv + 2, 0:64],
                                                        in_=h_.t[0:nb, 0:128].rearrange("p (a b) -> p a b", b=64), func=AF.Copy),
                          reads=[h_], writes=[rc])
            kb.barrier()

        es2 = ExitStack()
        with es2:
            qa = [kb.sb(f"a_qa{i}", [128, 16, 128], BF16, es2) for i in range(2)]
            ngt = [kb.sb(f"a_ng{i}", [128, 512], F32, es2) for i in range(2)]
            kwt = [kb.sb(f"a_kw{i}", [64, 4, 640], BF16, es2) for i in range(2)]
            vwt = [kb.sb(f"a_vw{i}", [128, 5, 4, 65], BF16, es2) for i in range(2)]
            sbs = [kb.sb(f"a_sbs{i}", [128, 512], F32, es2) for i in range(3)]
            ptb = [kb.sb(f"a_ptb{i}", [128, 512], BF16, es2) for i in range(3)]
            o4 = [kb.sb(f"a_o4{i}", [128, 4, 128], F32, es2) for i in range(2)]
            rd = [kb.sb(f"a_rd{i}", [128, 4], F32, es2) for i in range(2)]
            gsc = [kb.sb(f"a_gsc{i}", [128, 4], F32, es2) for i in range(2)]
            sc = kb.sb("a_sc", [128, 64], F32, es2)
            sc2 = kb.sb("a_sc2", [128, 64], F32, es2)
            m8a = kb.sb("a_m8a", [128, 8], F32, es2)
            m8b = kb.sb("a_m8b", [128, 8], F32, es2)
            mk = kb.sb("a_mk", [128, 128], BF16, es2)
            ao = [kb.sb(f"a_ao{i}", [128, D], F32, es2) for i in range(2)]
            ps_s = [kb.ps(f"a_pss{i}", [128, 512], F32, es2) for i in range(3)]
            ps_o = [kb.ps(f"a_pso{i}", [128, 4, 128], F32, es2) for i in range(3)]
            ps_t = kb.ps("a_pst", [128, 128], BF16, es2)
            for i in range(2):
                kb.op("pool", lambda e: e.memset(vwt[i].t[:, :, :, 64:65], 1.0), writes=[vwt[i]])
            kb.op("pool", lambda e: e.memset(mk.t[:], 0.0), writes=[mk])
            rr = [0, 0]

            def branch(qt, kv, q_a, kts, lhs_fn, rel, rel_x0_fn, rhs_fn, nk_rows):
                po = ps_o[rr[1] % 3]
                rr[1] += 1
                chunks = [kts[a:a + 4] for a in range(0, len(kts), 4)]
                for gi in range(4):
                    h = 4 * kv + gi
                    first = True
                    for ci, ch in enumerate(chunks):
                        pss = ps_s[rr[0] % 3]
                        sb_, pt_ = sbs[rr[0] % 3], ptb[rr[0] % 3]
                        rr[0] += 1
                        w = 128 * len(ch)
                        for a, kt in enumerate(ch):
                            lhsT = lhs_fn(kv, kt)
                            kb.op("pe", lambda e: e.matmul(pss.t[:, 128 * a:128 * a + 128], lhsT=lhsT[0], rhs=q_a.t[0:nk_rows, h, :],
                                                           start=True, stop=True), reads=[lhsT[1], q_a], writes=[pss])
                        x0 = rel_x0_fn(ch[0])
                        kb.op("dve", lambda e: e.scalar_tensor_tensor(out=sb_.t[:, 0:w], in0=rel.t[:, x0:x0 + w], scalar=SLOPES[h],
                                                                      in1=pss.t[:, 0:w], op0=ALU.mult, op1=ALU.add),
                              reads=[rel, pss], writes=[sb_])
                        kb.op("act", lambda e: e.activation(out=pt_.t[:, 0:w], in_=sb_.t[:, 0:w], func=AF.Exp),
                              reads=[sb_], writes=[pt_])
                        for a, kt in enumerate(ch):
                            rhs = rhs_fn(kv, kt)
                            last = (ci == len(chunks) - 1) and (a == len(ch) - 1)
                            kb.op("pe", lambda e: e.matmul(po.t[:, gi, 0:rhs[2]], lhsT=pt_.t[:, 128 * a:128 * a + 128], rhs=rhs[0],
                                                           start=first, stop=last), reads=[pt_, rhs[1]], writes=[po])
                            first = False
                return po

            for qt in range(NT):
                q0 = 128 * qt
                q_a, ng_, kw_, vw_, ao_ = qa[qt % 2], ngt[qt % 2], kwt[qt % 2], vwt[qt % 2], ao[qt % 2]
                kb.dma("sp", q_a.t[0:64, :, :], qT_s[:, q0:q0 + 128].rearrange("(h d) t -> d h t", d=64), writes=[q_a])
                kb.dma("sp", ng_.t[:, :], g["ng_s"][q0:q0 + 128, :], writes=[ng_])
                wk0 = max(0, qt - 4)
                nwk = qt - wk0 + 1
                for kv in range(4):
                    kb.dma("sp", kw_.t[0:64, kv, 0:128 * nwk], kT_s[2, 64 * kv:64 * kv + 64, 128 * wk0:128 * (qt + 1)], writes=[kw_])
                    kb.dma("sp", vw_.t[:, 0:nwk, kv, 0:64],
                           v_s[1, 128 * wk0:128 * (qt + 1), 64 * kv:64 * kv + 64].rearrange("(a p) d -> p a d", p=128), writes=[vw_])
                for kv in range(4):
                    cts = [ct for ct in range(NCT) if qt >= 16 * ct]
                    o_c = o4[0]
                    poc = ps_o[rr[1] % 3]
                    rr[1] += 1
                    for gi in range(4):
                        h = 4 * kv + gi
                        for ci, ct in enumerate(cts):
                            pss = ps_s[rr[0] % 3]
                            sb_, pt_ = sbs[rr[0] % 3], ptb[rr[0] % 3]
                            rr[0] += 1
                            kb.op("pe", lambda e: e.matmul(pss.t[:, 0:128], lhsT=ckT.t[0:64, kv, 128 * ct:128 * ct + 128],
                                                           rhs=q_a.t[0:64, h, :], start=True, stop=True),
                                  reads=[ckT, q_a], writes=[pss])
                            y0 = q0 - 2048 * ct
                            kb.op("dve", lambda e: e.scalar_tensor_tensor(out=sb_.t[:, 0:128], in0=relc.t[:, y0:y0 + 128],
                                                                          scalar=SLOPES[h], in1=pss.t[:, 0:128],
                                                                          op0=ALU.mult, op1=ALU.add),
                                  reads=[relc, pss], writes=[sb_])
                            kb.op("act", lambda e: e.activation(out=pt_.t[:, 0:128], in_=sb_.t[:, 0:128], func=AF.Exp),
                                  reads=[sb_], writes=[pt_])
                            kb.op("pe", lambda e: e.matmul(poc.t[:, gi, :], lhsT=pt_.t[:, 0:128], rhs=rc.t[:, ct, kv, :],
                                                           start=(ci == 0), stop=(ci == len(cts) - 1)),
                                  reads=[pt_, rc], writes=[poc])
                    kb.op("act", lambda e: e.activation(out=o_c.t[:, :, :], in_=poc.t[:, :, :], func=AF.Copy), reads=[poc], writes=[o_c])
                    r_c, g_c = rd[0], gsc[0]
                    kb.op("dve", lambda e: e.tensor_scalar(out=r_c.t[:, :], in0=o_c.t[:, :, 64], scalar1=1e-30, scalar2=None,
                                                           op0=ALU.max), reads=[o_c], writes=[r_c])
                    kb.op("dve", lambda e: e.reciprocal(out=r_c.t[:, :], in_=r_c.t[:, :]), reads=[r_c], writes=[r_c])
                    kb.op("dve", lambda e: e.tensor_tensor(out=g_c.t[:, :], in0=r_c.t[:, :], in1=ng_.t[:, 4 * kv:4 * kv + 4],
                                                           op=ALU.mult), reads=[r_c, ng_], writes=[g_c])
                    for gi in range(4):
                        h = 4 * kv + gi
                        kb.op("pool", lambda e: e.tensor_scalar(out=ao_.t[:, 64 * h:64 * h + 64], in0=o_c.t[:, gi, 0:64],
                                                                scalar1=g_c.t[:, gi:gi + 1], scalar2=None, op0=ALU.mult),
                              reads=[o_c, g_c], writes=[ao_])
                    if do_sel:
                        kb.op("dve", lambda e: e.tensor_scalar(out=sc.t[:, 0:63], in0=o_c.t[:, 0, 65:128], scalar1=r_c.t[:, 0:1],
                                                               scalar2=None, op0=ALU.mult), reads=[o_c, r_c], writes=[sc])
                        for gi in range(1, 4):
                            kb.op("dve", lambda e: e.scalar_tensor_tensor(out=sc.t[:, 0:63], in0=o_c.t[:, gi, 65:128],
                                                                          scalar=r_c.t[:, gi:gi + 1], in1=sc.t[:, 0:63],
                                                                          op0=ALU.mult, op1=ALU.add),
                                  reads=[o_c, r_c, sc], writes=[sc])
                        kb.op("dve", lambda e: e.tensor_tensor(out=sc.t[:, 0:63], in0=sc.t[:, 0:63], in1=fmk.t[:, qt, :], op=ALU.add),
                              reads=[sc, fmk], writes=[sc])
                        kb.op("dve", lambda e: e.max(out=m8a.t[:, :], in_=sc.t[:, 0:63]), reads=[sc], writes=[m8a])
                        kb.op("dve", lambda e: e.match_replace(out=sc2.t[:, 0:63], in_to_replace=m8a.t[:, :], in_values=sc.t[:, 0:63],
                                                               imm_value=-3.0e38), reads=[sc, m8a], writes=[sc2])
                        kb.op("dve", lambda e: e.max(out=m8b.t[:, :], in_=sc2.t[:, 0:63]), reads=[sc2], writes=[m8b])
                        kb.op("dve", lambda e: e.tensor_scalar(out=sc2.t[:, 0:63], in0=sc.t[:, 0:63], scalar1=m8b.t[:, 6:7],
                                                               scalar2=None, op0=ALU.is_ge), reads=[sc, m8b], writes=[sc2])
                        kb.op("dve", lambda e: e.tensor_scalar(out=mk.t[:, 65:128], in0=sc2.t[:, 0:63], scalar1=-1.0, scalar2=30000.0,
                                                               op0=ALU.add, op1=ALU.mult), reads=[sc2], writes=[mk])
                    kb.op("pe", lambda e: e.transpose(out=ps_t.t[:, :], in_=mk.t[:, :], identity=ident.t[:, :]),
                          reads=[mk, ident], writes=[ps_t])
                    kb.op("act", lambda e: e.activation(out=q_a.t[64:128, 4 * kv, :], in_=ps_t.t[64:128, :], func=AF.Copy),
                          reads=[ps_t], writes=[q_a])
                    for gi in range(1, 4):
                        kb.op("pool", lambda e: e.tensor_copy(out=q_a.t[64:128, 4 * kv + gi, :], in_=q_a.t[64:128, 4 * kv, :]),
                              reads=[q_a], writes=[q_a])
                    for br in (1, 2):
                        if br == 1:
                            kts = list(range(qt, -1, -1))
                            pob = branch(qt, kv, q_a, kts,
                                         lambda kv_, kt: (ksa.t[:, kv_, 128 * kt:128 * kt + 128], ksa),
                                         rels, lambda kt: 128 * (qt - kt),
                                         lambda kv_, kt: (vsa.t[:, kt, kv_, :], vsa, 65), 128)
                        else:
                            kts = list(range(qt, wk0 - 1, -1))
                            pob = branch(qt, kv, q_a, kts,
                                         lambda kv_, kt: (kw_.t[0:64, kv_, 128 * (kt - wk0):128 * (kt - wk0) + 128], kw_),
                                         relw, lambda kt: 128 * (qt - kt),
                                         lambda kv_, kt: (vw_.t[:, kt - wk0, kv_, :], vw_, 65), 64)
                        o_b, r_b, g_b = o4[1], rd[1], gsc[1]
                        kb.op("act", lambda e: e.activation(out=o_b.t[:, :, 0:65], in_=pob.t[:, :, 0:65], func=AF.Copy),
                              reads=[pob], writes=[o_b])
                        kb.op("dve", lambda e: e.tensor_scalar(out=r_b.t[:, :], in0=o_b.t[:, :, 64], scalar1=1e-30, scalar2=None,
                                                               op0=ALU.max), reads=[o_b], writes=[r_b])
                        kb.op("dve", lambda e: e.reciprocal(out=r_b.t[:, :], in_=r_b.t[:, :]), reads=[r_b], writes=[r_b])
                        kb.op("dve", lambda e: e.tensor_tensor(out=g_b.t[:, :], in0=r_b.t[:, :],
                                                               in1=ng_.t[:, 16 * br + 4 * kv:16 * br + 4 * kv + 4], op=ALU.mult),
                              reads=[r_b, ng_], writes=[g_b])
                        for gi in range(4):
                            h = 4 * kv + gi
                            kb.op("dve", lambda e: e.scalar_tensor_tensor(out=ao_.t[:, 64 * h:64 * h + 64], in0=o_b.t[:, gi, 0:64],
                                                                          scalar=g_b.t[:, gi:gi + 1], in1=ao_.t[:, 64 * h:64 * h + 64],
                                                                          op0=ALU.mult, op1=ALU.add),
                                  reads=[o_b, g_b, ao_], writes=[ao_])
                kb.dma("sp", g["ao_s"][q0:q0 + 128, :], ao_.t[:, :], reads=[ao_])
        kb.barrier()


def _att_consts(cmp_pos, cmp_w, cmp_w1, cmp_w2):
    f32 = np.float32
    NEG = f32(-1e32)
    j = np.arange(128, dtype=np.int64)[:, None]
    y = np.arange(4096, dtype=np.int64)[None, :]
    relc = np.where(y >= 16 * j + 31, (16 * j + 31 - y).astype(f32), NEG).astype(f32)
    x = np.arange(4096 + 128, dtype=np.int64)[None, :]
    rels = np.where(x >= j, (j - x).astype(f32), NEG).astype(f32)
    xw = np.arange(768, dtype=np.int64)[None, :]
    relw = np.where((xw - j >= 0) & (xw - j <= 512), (j - xw).astype(f32), NEG).astype(f32)
    t = (128 * np.arange(32)[None, :, None] + np.arange(128)[:, None, None])
    sb = np.arange(1, 64)[None, None, :]
    cur = t // 64
    fmk = np.where((sb == cur) | (sb == cur - 1), f32(1e30), np.where(sb > cur, f32(-1e30), f32(0))).astype(f32).reshape(128, 32 * 63)
    blk = (128 * np.arange(2)[None, :, None] + np.arange(128)[:, None, None])
    cm = np.where((blk == 4 * sb - 1) | (blk == 4 * sb + 3), f32(1), np.where((blk >= 4 * sb) & (blk <= 4 * sb + 2), f32(2), f32(0)))
    cmat = cm.astype(f32).reshape(128, 2 * 63)
    onehot = (np.arange(4096)[None, :] // 64 == np.arange(64)[:, None]).astype(f32)
    wpool = np.zeros((128, 2, 16), f32)
    for c in range(8):
        for p in range(16):
            wpool[16 * c + p, :, c] = cmp_w[p, :]
            wpool[16 * c + p, :, 8 + c] = cmp_w[16 + p, :]
    cposd = np.concatenate([cmp_pos, cmp_pos], axis=2)
    w1bd = np.zeros((128, 2, 128), f32)
    w2bd = np.zeros((128, 2, 128), f32)
    w2sel = np.zeros((128, 2, 64), f32)
    for jj in range(2):
        for hh in range(2):
            w1bd[64 * hh:64 * hh + 64, jj, 64 * hh:64 * hh + 64] = cmp_w1[jj]
            w2bd[64 * hh:64 * hh + 64, jj, 64 * hh:64 * hh + 64] = cmp_w2[jj]
    for hh in range(2):
        w2sel[64 * hh:64 * hh + 64, hh, :] = cmp_w2[0]
    c_ = np.ascontiguousarray
    return {"relc": c_(relc), "rels": c_(rels), "relw": c_(relw), "fmk": c_(fmk), "cmat": c_(cmat), "onehot": c_(onehot),
            "wpool": c_(wpool.reshape(128, 32)), "cposd": c_(cposd.reshape(32, 256)), "cw": c_(cmp_w.astype(f32)),
            "w1bd": c_(w1bd.reshape(128, 256)), "w2bd": c_(w2bd.reshape(128, 256)), "w2sel": c_(w2sel.reshape(128, 128))}


def phase_smp(kb, cfg, g):
    T, NS, TT, NP = cfg.T, cfg.NS, cfg.TT, cfg.NP
    NBs = 8 * NP - 1
    NCT = (NBs + 127) // 128
    NSC = 2 * NP
    NCH = (2 * NP + 63) // 64
    do_sel = (2 * NP + 1) > 16
    GP = min(16, NP)
    NG = NP // GP
    qT_s, kT_s, v_s = g["qT_s"], g["kT_s"], g["v_s"]
    es = ExitStack()
    with es:
        ident = kb.sb("s_ident", [128, 128], BF16, es)
        bsel = kb.sb("s_bsel", [128, NP, 16], F32, es)
        bwin = kb.sb("s_bwin", [128, 4, 16], F32, es)
        bcmp = kb.sb("s_bcmp", [128, NCT, 16], F32, es)
        cms = kb.sb("s_cms", [128, NCT, NSC], BF16, es)
        fms = kb.sb("s_fms", [4, NSC], F32, es)
        iota = kb.sb("s_iota", [128, 1], I32, es)
        kaug = [kb.sb(f"s_kaug{i}", [128, 4, 128 * GP], BF16, es) for i in range(2)]
        vg = [kb.sb(f"s_vg{i}", [128, GP, 4, 65], BF16, es) for i in range(2)]
        qs4 = kb.sb("s_qs4", [64, 16, NS], BF16, es)
        kn = kb.sb("s_kn", [64, 2, 4, NS], BF16, es)
        vn = kb.sb("s_vn", [1, 2, NS, 256], BF16, es)
        vnew = kb.sb("s_vnew", [1, 2, 4, 65], BF16, es)
        wpool = kb.sb("s_wpool", [128, 2, 16], F32, es)
        cposd = kb.sb("s_cposd", [32, 2, 128], F32, es)
        cw = kb.sb("s_cw", [32, 2], F32, es)
        w1bd = kb.sb("s_w1bd", [128, 2, 128], F32, es)
        w2bd = kb.sb("s_w2bd", [128, 2, 128], F32, es)
        w2sel = kb.sb("s_w2sel", [128, 2, 64], F32, es)
        pe_sb = kb.sb("s_pe", [128, 2], F32, es)
        pg = [kb.sb(f"s_pg{i}", [128, 512], F32, es) for i in range(3)]
        pkb = [kb.sb(f"s_pkb{i}", [128, 256], BF16, es) for i in range(2)]
        abseg = kb.sb("s_abseg", [128, 4, 32, 16], F32, es)
        af = kb.sb("s_af", [128, 4, 8 * NP], F32, es)
        bf = kb.sb("s_bf", [128, 4, 8 * NP], F32, es)
        hid = kb.sb("s_hid", [128, 4, 8 * NP], F32, es)
        ckT = kb.sb("s_ckT", [64, 4, 128 * NCT], BF16, es)
        rcs = kb.sb("s_rcs", [128, NCT, 4, 65], BF16, es)
        ptb_i = kb.sb("s_ptb", [128, NP], I32, es)
        idx = kb.sb("s_idx", [128, NP], I32, es)
        qsa = kb.sb("s_qsa", [128, NCH, 16], BF16, es)
        sbs = [kb.sb(f"s_sbs{i}", [128, 32, 16], F32, es) for i in range(2)]
        ptt = [kb.sb(f"s_ptt{i}", [128, 32, 16], BF16, es) for i in range(2)]
        ptn = kb.sb("s_ptn", [1, 16], BF16, es)
        osum = kb.sb("s_osum", [4, 4, 65], F32, es)
        oc = kb.sb("s_oc", [4, 4, NSC], F32, es)
        rd = kb.sb("s_rd", [4, 4], F32, es)
        gsc = kb.sb("s_gsc", [4, 4], F32, es)
        gs = kb.sb("s_gs", [4, 3, 4], F32, es)
        rsel = kb.sb("s_rsel", [4, 4, 4], F32, es)
        sc = kb.sb("s_sc", [4, NSC], F32, es)
        sc2 = kb.sb("s_sc2", [4, NSC], F32, es)
        m8a = kb.sb("s_m8a", [4, 8], F32, es)
        m8b = kb.sb("s_m8b", [4, 8], F32, es)
        maskp = kb.sb("s_maskp", [4, NCH, 128], BF16, es)
        mT = kb.sb("s_mT", [128, 4], BF16, es)
        a_s = kb.sb("s_as", [4, 4, 64], F32, es)
        pp = [kb.ps(f"s_pp{i}", [128, 32, 16], F32, es) for i in range(4)]
        hp = [kb.ps("s_hp0", [128, 512], F32, es)] * 2
        pss = kb.ps("s_pss", [128, 32, 16], F32, es)
        pmisc = kb.ps("s_pmisc", [128, 512], F32, es)
        ptr = None

        kb.dma("pool", ident.t[:], g["ident_in"][:, :], writes=[ident])
        kb.dma("sp", bsel.t[:], g["bsel"][:, :].rearrange("p (a b) -> p a b", b=16), writes=[bsel])
        kb.dma("sp", bwin.t[:], g["bwin"][:, :].rearrange("p (a b) -> p a b", b=16), writes=[bwin])
        kb.dma("sp", bcmp.t[:], g["bcmp"][:, :].rearrange("p (a b) -> p a b", b=16), writes=[bcmp])
        kb.dma("pool", cms.t[:], g["cms"][:, :].rearrange("p (a b) -> p a b", b=NSC), writes=[cms])
        kb.dma("sp", fms.t[:], g["fms"][:, :], writes=[fms])
        kb.dma("sp", iota.t[:], g["iota"][:, :], writes=[iota])
        for nm, t_, src in ((0, wpool, g["wpool"][:, :].rearrange("p (a b) -> p a b", b=16)),
                            (1, cposd, g["cposd"][:, :].rearrange("p (a b) -> p a b", b=128)),
                            (2, cw, g["cw"][:, :]),
                            (3, w1bd, g["w1bd"][:, :].rearrange("p (a b) -> p a b", b=128)),
                            (4, w2bd, g["w2bd"][:, :].rearrange("p (a b) -> p a b", b=128)),
                            (5, w2sel, g["w2sel"][:, :].rearrange("p (a b) -> p a b", b=64))):
            kb.dma("sp", t_.t[:], src, writes=[t_])
        for i in range(2):
            off = 128 * ((GP * i) % 32)
            for kv in range(4):
                kb.dma("pool", kaug[i].t[64:128, kv, :], g["onehot"][:, off:off + 128 * GP], writes=[kaug[i]])
            kb.op("pool", lambda e: e.memset(vg[i].t[:, :, :, 64:65], 1.0), writes=[vg[i]])
        kb.dma("sp", qs4.t[:, :, :], qT_s[:, T:T + NS].rearrange("(h d) t -> d h t", d=64), writes=[qs4])
        for b in range(2):
            kb.dma("sp", kn.t[:, b, :, :], kT_s[1 + b, :, T:T + NS].rearrange("(k d) t -> d k t", d=64), writes=[kn])
            kb.dma("sp", vn.t[0:1, b, :, :], v_s[b:b + 1, T:T + NS, :], writes=[vn])
        kb.op("pool", lambda e: e.memset(vnew.t[:, :, :, 64:65], 1.0), writes=[vnew])
        kb.op("pool", lambda e: e.memset(ckT.t[:], 0.0), writes=[ckT])
        kb.op("pool", lambda e: e.memset(rcs.t[:], 0.0), writes=[rcs])
        kb.op("pool", lambda e: e.memset(rcs.t[:, :, :, 64:65], 1.0), reads=[rcs], writes=[rcs])
        kb.op("pool", lambda e: e.memset(maskp.t[:], 0.0), writes=[maskp])
        kb.op("pool", lambda e: e.memset(rsel.t[:], 0.0), writes=[rsel])
        kb.op("pool", lambda e: e.memset(qsa.t[:], 0.0), writes=[qsa])
        for j in range(2):
            kb.op("pe", lambda e: e.matmul(pmisc.t[:, j:j + 1], lhsT=cposd.t[:, j, :], rhs=cw.t[:, j:j + 1],
                                           start=True, stop=True), reads=[cposd, cw], writes=[pmisc])
        kb.op("act", lambda e: e.activation(out=pe_sb.t[:, :], in_=pmisc.t[:, 0:2], func=AF.Copy), reads=[pmisc], writes=[pe_sb])
        pv = pmisc.t[0:4, 0:260].rearrange("p (k c) -> p k c", c=65)
        scps = pmisc.t[0:4, 0:NSC]
        cnt = [0]

        def prep_page(pgt, kb_, vg_, pi):
            pk = pkb[cnt[0] % 2]
            cnt[0] += 1
            kb.op("act", lambda e: e.activation(out=pk.t[:, :], in_=pgt.t[:, 0:256], func=AF.Copy), reads=[pgt], writes=[pk])
            for kv in range(4):
                kb.op("pe", lambda e: e.transpose(out=hpb.t[0:64, kv, :], in_=pk.t[:, 64 * kv:64 * kv + 64], identity=ident.t[:, :]),
                      reads=[pk, ident], writes=[hpb_buf])
            kb.op("dve", lambda e: e.tensor_copy(out=kb_.t[0:64, :, 128 * pi:128 * pi + 128], in_=hpb.t[0:64, :, :]),
                  reads=[hpb_buf], writes=[kb_])
            kb.op("pool", lambda e: e.tensor_copy(out=vg_.t[:, pi, :, 0:64],
                                                  in_=pgt.t[:, 256:512].rearrange("p (k d) -> p k d", d=64)),
                  reads=[pgt], writes=[vg_])

        hpb_buf = kb.ps("s_ptr", [128, 4, 128], BF16, es)
        hpb = hpb_buf

        def pv_batch(lhs_fn, ntile, v_fn, first):
            for kv in range(4):
                for a in range(ntile):
                    l_ = lhs_fn(a, kv)
                    v_ = v_fn(a, kv)
                    kb.op("pe", lambda e: e.matmul(pv[:, kv, :], lhsT=l_[0], rhs=v_[0], start=(a == 0), stop=(a == ntile - 1)),
                          reads=[l_[1], v_[1]], writes=[pmisc])
            if first:
                kb.op("act", lambda e: e.activation(out=osum.t[:, :, :], in_=pv, func=AF.Copy), reads=[pmisc], writes=[osum])
            else:
                kb.op("dve", lambda e: e.tensor_tensor(out=osum.t[:, :, :], in0=osum.t[:, :, :], in1=pv, op=ALU.add),
                      reads=[pmisc, osum], writes=[osum])

        def finish_branch(br, first):
            kb.op("dve", lambda e: e.tensor_scalar(out=rd.t[:, :], in0=osum.t[:, :, 64], scalar1=1e-30, scalar2=None, op0=ALU.max),
                  reads=[osum], writes=[rd])
            kb.op("dve", lambda e: e.reciprocal(out=rd.t[:, :], in_=rd.t[:, :]), reads=[rd], writes=[rd])
            kb.op("dve", lambda e: e.tensor_tensor(out=gsc.t[:, :], in0=rd.t[:, :], in1=gs.t[:, br, :], op=ALU.mult),
                  reads=[rd, gs], writes=[gsc])
            for kv in range(4):
                if first:
                    kb.op("dve", lambda e: e.tensor_scalar(out=a_s.t[:, kv, :], in0=osum.t[:, kv, 0:64], scalar1=gsc.t[:, kv:kv + 1],
                                                           scalar2=None, op0=ALU.mult), reads=[osum, gsc], writes=[a_s])
                else:
                    kb.op("dve", lambda e: e.scalar_tensor_tensor(out=a_s.t[:, kv, :], in0=osum.t[:, kv, 0:64],
                                                                  scalar=gsc.t[:, kv:kv + 1], in1=a_s.t[:, kv, :],
                                                                  op0=ALU.mult, op1=ALU.add), reads=[osum, gsc, a_s], writes=[a_s])

        def new_row(b, s, first):
            for kv in range(4):
                kb.op("pe", lambda e: e.matmul(pss.t[0:1, 0, 4 * kv:4 * kv + 4], lhsT=kn.t[0:64, b, kv, s:s + 1],
                                               rhs=qsa.t[0:64, 0, 4 * kv:4 * kv + 4], start=True, stop=True),
                      reads=[kn, qsa], writes=[pss])
            kb.op("act", lambda e: e.activation(out=ptn.t[0:1, :], in_=pss.t[0:1, 0, :], func=AF.Exp), reads=[pss], writes=[ptn])
            kb.op("pool", lambda e: e.tensor_copy(out=vnew.t[0:1, b, :, 0:64],
                                                  in_=vn.t[0:1, b, s, :].rearrange("p (k d) -> p k d", d=64)),
                  reads=[vn], writes=[vnew])
            pv_batch(lambda a, kv: (ptn.t[0:1, 4 * kv:4 * kv + 4], ptn), 1, lambda a, kv: (vnew.t[0:1, b, kv, :], vnew), first)

        def score_batch(ntile, lhs_fn, rows, chunk, bias_ap, bias_buf):
            sb_, pt_ = sbs[cnt[0] % 2], ptt[cnt[0] % 2]
            cnt[0] += 1
            for a in range(ntile):
                for kv in range(4):
                    l_ = lhs_fn(a, kv)
                    kb.op("pe", lambda e: e.matmul(pss.t[:, a, 4 * kv:4 * kv + 4], lhsT=l_[0], rhs=qsa.t[0:rows, chunk, 4 * kv:4 * kv + 4],
                                                   start=True, stop=True), reads=[l_[1], qsa], writes=[pss])
            kb.op("dve", lambda e: e.tensor_tensor(out=sb_.t[:, 0:ntile, :], in0=pss.t[:, 0:ntile, :], in1=bias_ap, op=ALU.add),
                  reads=[pss, bias_buf], writes=[sb_])
            kb.op("act", lambda e: e.activation(out=pt_.t[:, 0:ntile, :], in_=sb_.t[:, 0:ntile, :], func=AF.Exp),
                  reads=[sb_], writes=[pt_])
            return pt_

        for s in range(NS):
            kb.dma("sp", ptb_i.t[:, :], g["ptab"][s:s + 1, :].to_broadcast([128, NP]), writes=[ptb_i])
            kb.op("dve", lambda e: e.tensor_scalar(out=idx.t[:, :], in0=ptb_i.t[:, :], scalar1=128, scalar2=iota.t[:, 0:1],
                                                   op0=ALU.mult, op1=ALU.add), reads=[ptb_i, iota], writes=[idx])
            kb.dma("sp", gs.t[:, :, :], g["ng_s"][T + s:T + s + 1, 0:48].rearrange("o (b k g) -> g (o b) k", b=3, k=4),
                   writes=[gs], allow_slow_non_contiguous=True)
            for c in range(NCH):
                kb.op("pool", lambda e: e.tensor_copy(out=qsa.t[0:64, c, :], in_=qs4.t[0:64, :, s]), reads=[qs4], writes=[qsa])
            for p in range(NP):
                pgt = pg[p % 3]
                kb.gather(pgt.t[:, :], g["cache_cmp"][:, :], idx.t[:, p:p + 1], reads=[idx], writes=[pgt])
                for fc in range(4):
                    kb.op("pe", lambda e: e.matmul(pp[fc].t[:, p % 32, :], lhsT=pgt.t[:, 128 * fc:128 * fc + 128],
                                                   rhs=wpool.t[:, fc // 2, :], start=True, stop=True),
                          reads=[pgt, wpool], writes=[pp[fc]])
                if p % 32 == 31 or p == NP - 1:
                    seg0 = 32 * (p // 32)
                    n = p - seg0 + 1
                    for fc in range(4):
                        kb.op("act", lambda e: e.activation(out=abseg.t[:, fc, 0:n, :], in_=pp[fc].t[:, 0:n, :], func=AF.Copy),
                              reads=[pp[fc]], writes=[abseg])
                    kb.op("dve", lambda e: e.tensor_copy(out=af.t[:, :, 8 * seg0:8 * (seg0 + n)].rearrange("p f (t c) -> p f t c", c=8),
                                                         in_=abseg.t[:, :, 0:n, 0:8]), reads=[abseg], writes=[af])
                    kb.op("pool", lambda e: e.tensor_copy(out=bf.t[:, :, 8 * seg0:8 * (seg0 + n)].rearrange("p f (t c) -> p f t c", c=8),
                                                          in_=abseg.t[:, :, 0:n, 8:16]), reads=[abseg], writes=[bf])
            kb.op("dve", lambda e: e.tensor_tensor(out=af.t[:, :, 0:NBs], in0=af.t[:, :, 0:NBs], in1=bf.t[:, :, 1:NBs + 1], op=ALU.add),
                  reads=[af, bf], writes=[af])
            for fc in range(4):
                kb.op("dve", lambda e: e.tensor_scalar(out=af.t[:, fc, 0:NBs], in0=af.t[:, fc, 0:NBs],
                                                       scalar1=pe_sb.t[:, fc // 2:fc // 2 + 1], scalar2=None, op0=ALU.add),
                      reads=[af, pe_sb], writes=[af])
            hi = 0
            for fc in range(4):
                for c0 in range(0, NBs, 512):
                    n = min(512, NBs - c0)
                    h_ = hp[hi % 2]
                    hi += 1
                    kb.op("pe", lambda e: e.matmul(h_.t[:, 0:n], lhsT=w1bd.t[:, fc // 2, :], rhs=af.t[:, fc, c0:c0 + n],
                                                   start=True, stop=True), reads=[w1bd, af], writes=[h_])
                    kb.op("act", lambda e: e.activation(out=hid.t[:, fc, c0:c0 + n], in_=h_.t[:, 0:n], func=AF.Silu),
                          reads=[h_], writes=[hid])
            for kv in range(4):
                for c0 in range(0, NBs, 512):
                    n = min(512, NBs - c0)
                    h_ = hp[hi % 2]
                    hi += 1
                    kb.op("pe", lambda e: e.matmul(h_.t[0:64, 0:n], lhsT=w2sel.t[:, kv % 2, :], rhs=hid.t[:, kv // 2, c0:c0 + n],
                                                   start=True, stop=True), reads=[w2sel, hid], writes=[h_])
                    kb.op("act", lambda e: e.activation(out=ckT.t[0:64, kv, c0:c0 + n], in_=h_.t[0:64, 0:n], func=AF.Copy),
                          reads=[h_], writes=[ckT])
            for ct in range(NCT):
                nb = min(128, NBs - 128 * ct)
                for fv in range(2):
                    h_ = hp[hi % 2]
                    hi += 1
                    kb.op("pe", lambda e: e.matmul(h_.t[0:nb, 0:128], lhsT=hid.t[:, 2 + fv, 128 * ct:128 * ct + nb],
                                                   rhs=w2bd.t[:, 1, :], start=True, stop=True), reads=[hid, w2bd], writes=[h_])
                    kb.op("act", lambda e: e.activation(out=rcs.t[0:nb, ct, 2 * fv:2 * fv + 2, 0:64],
                                                        in_=h_.t[0:nb, 0:128].rearrange("p (a b) -> p a b", b=64), func=AF.Copy),
                          reads=[h_], writes=[rcs])
            pt_ = score_batch(NCT, lambda a, kv: (ckT.t[0:64, kv, 128 * a:128 * a + 128], ckT), 64, 0, bcmp.t[:, :, :], bcmp)
            pv_batch(lambda a, kv: (pt_.t[:, a, 4 * kv:4 * kv + 4], pt_), NCT, lambda a, kv: (rcs.t[:, a, kv, :], rcs), True)
            finish_branch(0, True)
            for kv in range(4):
                h_ = hp[hi % 2]
                hi += 1
                for a in range(NCT):
                    kb.op("pe", lambda e: e.matmul(h_.t[0:4, 0:NSC], lhsT=pt_.t[:, a, 4 * kv:4 * kv + 4], rhs=cms.t[:, a, :],
                                                   start=(a == 0), stop=(a == NCT - 1)), reads=[pt_, cms], writes=[h_])
                kb.op("act", lambda e: e.activation(out=oc.t[:, kv, :], in_=h_.t[0:4, 0:NSC], func=AF.Copy), reads=[h_], writes=[oc])
                kb.op("pool", lambda e: e.tensor_copy(out=rsel.t[:, kv, kv:kv + 1], in_=rd.t[:, kv:kv + 1]), reads=[rd], writes=[rsel])
            for kv in range(4):
                kb.op("pe", lambda e: e.matmul(scps, lhsT=rsel.t[:, kv, :], rhs=oc.t[:, kv, :], start=(kv == 0), stop=(kv == 3)),
                      reads=[rsel, oc], writes=[pmisc])
            kb.op("dve", lambda e: e.tensor_tensor(out=sc.t[:, :], in0=scps, in1=fms.t[:, :], op=ALU.add),
                  reads=[pmisc, fms], writes=[sc])
            if do_sel:
                kb.op("dve", lambda e: e.max(out=m8a.t[:, :], in_=sc.t[:, :]), reads=[sc], writes=[m8a])
                kb.op("dve", lambda e: e.match_replace(out=sc2.t[:, :], in_to_replace=m8a.t[:, :], in_values=sc.t[:, :],
                                                       imm_value=-3.0e38), reads=[sc, m8a], writes=[sc2])
                kb.op("dve", lambda e: e.max(out=m8b.t[:, :], in_=sc2.t[:, :]), reads=[sc2], writes=[m8b])
                kb.op("dve", lambda e: e.tensor_scalar(out=sc2.t[:, :], in0=sc.t[:, :], scalar1=m8b.t[:, 6:7], scalar2=None,
                                                       op0=ALU.is_ge), reads=[sc, m8b], writes=[sc2])
                kb.op("dve", lambda e: e.tensor_scalar(out=sc2.t[:, :], in0=sc2.t[:, :], scalar1=-1.0, scalar2=30000.0,
                                                       op0=ALU.add, op1=ALU.mult), reads=[sc2], writes=[sc2])
                for c in range(NCH):
                    b0 = max(1, 64 * c)
                    b1 = min(2 * NP - 1, 64 * c + 63)
                    kb.op("dve", lambda e: e.tensor_copy(out=maskp.t[:, c, 64 + b0 - 64 * c:64 + b1 - 64 * c + 1], in_=sc2.t[:, b0 - 1:b1]),
                          reads=[sc2], writes=[maskp])
            for c in range(NCH):
                kb.op("pe", lambda e: e.transpose(out=hpb.t[:, 0, 0:4], in_=maskp.t[0:4, c, :], identity=ident.t[0:4, 0:4]),
                      reads=[maskp, ident], writes=[hpb_buf])
                kb.op("act", lambda e: e.activation(out=mT.t[64:128, :], in_=hpb.t[64:128, 0, 0:4], func=AF.Copy),
                      reads=[hpb_buf], writes=[mT])
                for gi in range(4):
                    kb.op("pool", lambda e: e.tensor_copy(out=qsa.t[64:128, c, :].rearrange("p (k g) -> p k g", g=4)[:, :, gi],
                                                          in_=mT.t[64:128, :]), reads=[mT], writes=[qsa])
            first = True
            for grp in range(NG):
                kb_, vg_ = kaug[grp % 2], vg[grp % 2]
                for pi in range(GP):
                    p = GP * grp + pi
                    pgt = pg[p % 3]
                    kb.gather(pgt.t[:, :], g["cache_slc"][:, :], idx.t[:, p:p + 1], reads=[idx], writes=[pgt])
                    prep_page(pgt, kb_, vg_, pi)
                c = (GP * grp) // 32
                pt_ = score_batch(GP, lambda a, kv: (kb_.t[:, kv, 128 * a:128 * a + 128], kb_), 128, c,
                                  bsel.t[:, GP * grp:GP * grp + GP, :], bsel)
                pv_batch(lambda a, kv: (pt_.t[:, a, 4 * kv:4 * kv + 4], pt_), GP, lambda a, kv: (vg_.t[:, a, kv, :], vg_), first)
                first = False
            new_row(0, s, False)
            finish_branch(1, False)
            kb_, vg_ = kaug[NG % 2], vg[NG % 2]
            for a in range(4):
                pgt = pg[a % 3]
                kb.dma("sp", pgt.t[:, :], g["st_win"][s, 128 * a:128 * a + 128, :], writes=[pgt])
                prep_page(pgt, kb_, vg_, a)
            pt_ = score_batch(4, lambda a, kv: (kb_.t[0:64, kv, 128 * a:128 * a + 128], kb_), 64, 0, bwin.t[:, :, :], bwin)
            pv_batch(lambda a, kv: (pt_.t[:, a, 4 * kv:4 * kv + 4], pt_), 4, lambda a, kv: (vg_.t[:, a, kv, :], vg_), True)
            new_row(1, s, False)
            finish_branch(2, False)
            kb.dma("sp", g["ao_s"][T + s:T + s + 1, :].rearrange("o (k g d) -> g (o k) d", k=4, g=4), a_s.t[:, :, :], reads=[a_s])
        kb.barrier()


def _smp_consts(NP):
    f32 = np.float32
    P = 128 * NP
    NBs = 8 * NP - 1
    NCT = (NBs + 127) // 128
    sl = np.asarray(SLOPES, dtype=np.float64)[None, None, :]
    j = np.arange(128)[:, None, None]
    pos = 128 * np.arange(NP)[None, :, None] + j
    bsel = (sl * (pos - P)).astype(f32).reshape(128, NP * 16)
    dist = 512 - 128 * np.arange(4)[None, :, None] - j
    bwin = (-sl * dist).astype(f32).reshape(128, 64)
    blk = 128 * np.arange(NCT)[None, :, None] + j
    bcmp = np.where(blk <= NBs - 1, sl * (16 * blk + 31 - P), -1e32).astype(f32).reshape(128, NCT * 16)
    sb = np.arange(1, 2 * NP + 1)[None, None, :]
    cm = np.where((blk == 4 * sb - 1) | (blk == 4 * sb + 3), 1.0, np.where((blk >= 4 * sb) & (blk <= 4 * sb + 2), 2.0, 0.0))
    cms = cm.astype(f32).reshape(128, NCT * 2 * NP)
    fms = np.zeros((4, 2 * NP), f32)
    fms[:, 2 * NP - 2:] = 1e30
    c_ = np.ascontiguousarray
    return {"bsel": c_(bsel), "bwin": c_(bwin), "bcmp": c_(bcmp), "cms": c_(cms), "fms": c_(fms),
            "iota": np.arange(128, dtype=np.int32).reshape(128, 1)}


_PROG_CACHE = {}


def _get_prog(cfg_key):
    if cfg_key not in _PROG_CACHE:
        _PROG_CACHE[cfg_key] = build_program(Cfg(*cfg_key))
    return _PROG_CACHE[cfg_key]


def run_cores(cfg_key, per_core_inputs):
    nc = _get_prog(cfg_key)
    res = run_bass_kernel_spmd(nc, per_core_inputs, core_ids=list(range(len(per_core_inputs))))
    return res.results


def kernel(x_prompt, x_sample, cache_cmp_kv, cache_slc_kv, state_win_kv, state_conv, page_table,
           ln1_g, w_in, cmp_pos, cmp_w, cmp_w1, cmp_w2, conv_w, conv_b, conv_ln_g, conv_ln_b,
           w_conv_pw, w_out, ln2_g, w_up, w_down, lnf_g):
    f = lambda a: np.ascontiguousarray(np.asarray(a))
    B, T, _ = x_prompt.shape
    NSB = x_sample.shape[0]
    ncores = 8
    NS = NSB // ncores
    x_prompt = f(x_prompt); x_sample = f(x_sample)
    state_win_kv = f(state_win_kv); state_conv = f(state_conv)
    ident = np.eye(128, dtype=np.float32)
    cvec = np.ascontiguousarray(np.concatenate([f(conv_w)[0], f(conv_b), f(conv_ln_g), f(conv_ln_b)], axis=0))
    consts = _att_consts(f(cmp_pos)[0], f(cmp_w)[0], f(cmp_w1)[0], f(cmp_w2)[0])
    in_maps = []
    for c in range(ncores):
        in_maps.append({
            "xp": x_prompt[c],
            "xs": x_sample[NS * c:NS * c + NS, 0, :],
            "st_win": state_win_kv[0, NS * c:NS * c + NS].reshape(NS, 512, 512),
            "st_conv": state_conv[0, NS * c:NS * c + NS],
            "ln1_g": f(ln1_g).reshape(1, D),
            "w_in": f(w_in)[0],
            "ident": ident,
            "cvec": cvec,
            "w_pw": f(w_conv_pw)[0],
            "w_out": f(w_out)[0],
            "ln2_g": f(ln2_g).reshape(1, D),
            "w_up": f(w_up)[0],
            "w_down": f(w_down)[0],
            "lnf_g": f(lnf_g).reshape(1, D),
            **consts,
        })
    NP = int(page_table.shape[1])
    NPHYS = int(cache_cmp_kv.shape[1])
    sc_ = _smp_consts(NP)
    cc_ = f(cache_cmp_kv).reshape(NPHYS * 128, 512)
    cs_ = f(cache_slc_kv).reshape(NPHYS * 128, 512)
    ptab = np.ascontiguousarray(np.asarray(page_table).astype(np.int32))
    for c in range(ncores):
        in_maps[c].update(sc_)
        in_maps[c].update({"cache_cmp": cc_, "cache_slc": cs_, "ptab": ptab[NS * c:NS * c + NS]})
    res = run_cores((T, NS, NP, NPHYS), in_maps)
    cat = lambda k: np.stack([r[k] for r in res], axis=0)
    y_prompt = cat("y_p")
    y_sample = np.concatenate([r["y_s"] for r in res], axis=0).reshape(NSB, 1, D)
    cmp_kv_p = cat("cmp_p").reshape(1, B, T, 2, NKV, HD)
    cmp_kv_s = np.concatenate([r["cmp_s"] for r in res], axis=0).reshape(1, NSB, 1, 2, NKV, HD)
    slc_kv_p = cat("slc_p").reshape(1, B, T, 2, NKV, HD)
    slc_kv_s = np.concatenate([r["slc_s"] for r in res], axis=0).reshape(1, NSB, 1, 2, NKV, HD)
    win_kv_p = cat("win_p").reshape(1, B, 512, 2, NKV, HD)
    win_kv_s = np.concatenate([r["win_s"] for r in res], axis=0).reshape(1, NSB, 512, 2, NKV, HD)
    conv_pp = cat("conv_p").reshape(1, B, 30, D)
    conv_ss = np.concatenate([r["conv_s"] for r in res], axis=0).reshape(1, NSB, 30, D)
    return (y_prompt, y_sample, cmp_kv_p, cmp_kv_s, slc_kv_p, slc_kv_s, win_kv_p, win_kv_s, conv_pp, conv_ss)
```

```python
import os
import numpy as np
from contextlib import ExitStack
import concourse.bass as bass
import concourse.mybir as mybir
from concourse.bass_utils import run_bass_kernel_spmd

F32 = mybir.dt.float32
BF16 = mybir.dt.bfloat16
I32 = mybir.dt.int32
AF = mybir.ActivationFunctionType
ALU = mybir.AluOpType
AX = mybir.AxisListType

D = 1024
NH = 16
HD = 64
NKV = 4
KVW = 256
DCONV = 1024
CW = 31
DFF = 4096
DIN = 6704
EPS = 1e-6
OFF_U2, OFF_Q, OFF_KC, OFF_VC, OFF_KS, OFF_VS, OFF_KW, OFF_VW, OFF_NG, OFF_GC, OFF_GA = (
    0, 2048, 3072, 3328, 3584, 3840, 4096, 4352, 4608, 4656, 5680)
NSEM = 12
STOP = int(os.environ.get('KSTOP', '0'))
SKIP = int(os.environ.get('KSKIP', '0'))


class Buf:
    __slots__ = ("t", "w", "r", "name")

    def __init__(self, t=None, name=""):
        self.t = t
        self.w = {}
        self.r = {}
        self.name = name


def _merge(dst, src):
    for k, v in src.items():
        if k not in dst or dst[k][1] < v[1]:
            dst[k] = v


class Eng:
    def __init__(self, name, h):
        self.name = name
        self.h = h
        self.sem = None
        self.count = 0
        self.waited = {}
        self.dsems = []
        self.ndma = 0


class KB:
    def __init__(self, nc, es):
        self.nc = nc
        self.es = es
        self.eng = {
            "pe": Eng("pe", nc.tensor), "act": Eng("act", nc.scalar), "dve": Eng("dve", nc.vector),
            "pool": Eng("pool", nc.gpsimd), "sp": Eng("sp", nc.sync),
        }
        for n, e in self.eng.items():
            e.sem = es.enter_context(nc.semaphore("c_" + n))
        for n in ("sp", "pool", "act"):
            e = self.eng[n]
            e.dsems = [es.enter_context(nc.semaphore(f"d_{n}{i}")) for i in range(NSEM)]
        self.bar = es.enter_context(nc.semaphore("bar"))
        self.nbar = 0
        self.psum_banks = []
        self.ninstr = 0

    def sb(self, name, shape, dt, es=None):
        t = (es or self.es).enter_context(self.nc.sbuf_tensor("sb_" + name, list(shape), dt))
        return Buf(t, name)

    def ps(self, name, shape, dt, es=None):
        t = (es or self.es).enter_context(self.nc.psum_tensor("ps_" + name, list(shape), dt))
        return Buf(t, name)

    def _wait(self, E, deps):
        for k, (sem, v) in deps.items():
            if v <= 0:
                continue
            if E.name == "pe" and sem is E.sem:
                continue
            if E.waited.get(k, 0) >= v:
                continue
            E.h.wait_ge(sem, v)
            E.waited[k] = v
            self.ninstr += 1

    def _deps(self, reads, writes):
        deps = {}
        for b in reads:
            _merge(deps, b.w)
        for b in writes:
            _merge(deps, b.w)
            _merge(deps, b.r)
        return deps

    def _commit(self, tok, reads, writes):
        for b in reads:
            _merge(b.r, tok)
        for b in writes:
            b.w = dict(tok)
            b.r = {}

    def op(self, eng, fn, reads=(), writes=()):
        E = self.eng[eng]
        self._wait(E, self._deps(reads, writes))
        ins = fn(E.h)
        E.count += 1
        ins.then_inc(E.sem, 1)
        self.ninstr += 1
        tok = {id(E.sem): (E.sem, E.count)}
        self._commit(tok, reads, writes)
        return tok

    def dma(self, q, out, in_, reads=(), writes=(), **kw):
        Q = self.eng[q]
        k = Q.ndma
        sem = Q.dsems[k % NSEM]
        deps = self._deps(reads, writes)
        _merge(deps, {id(sem): (sem, 16 * (k // NSEM))})
        self._wait(Q, deps)
        ins = Q.h.dma_start(out=out, in_=in_, **kw)
        ins.then_inc(sem, 16)
        Q.ndma += 1
        self.ninstr += 1
        tok = {id(sem): (sem, 16 * (k // NSEM + 1))}
        self._commit(tok, reads, writes)
        return tok

    def gather(self, out, in_, idx_ap, reads=(), writes=()):
        Q = self.eng["pool"]
        k = Q.ndma
        sem = Q.dsems[k % NSEM]
        deps = self._deps(reads, writes)
        _merge(deps, {id(sem): (sem, 16 * (k // NSEM))})
        self._wait(Q, deps)
        ins = Q.h.indirect_dma_start(out=out, out_offset=None, in_=in_,
                                     in_offset=bass.IndirectOffsetOnAxis(ap=idx_ap, axis=0))
        ins.then_inc(sem, 16)
        Q.ndma += 1
        self.ninstr += 1
        tok = {id(sem): (sem, 16 * (k // NSEM + 1))}
        self._commit(tok, reads, writes)
        return tok

    def all_tokens(self):
        toks = {}
        for e in self.eng.values():
            if e.count:
                toks[id(e.sem)] = (e.sem, e.count)
            for i, s in enumerate(e.dsems):
                n = (e.ndma - i + NSEM - 1) // NSEM if e.ndma > i else 0
                if n:
                    toks[id(s)] = (s, 16 * n)
        return toks

    def barrier(self):
        toks = self.all_tokens()
        sp = self.eng["sp"]
        self._wait(sp, toks)
        self.nbar += 1
        sp.h.sem_inc(self.bar, 1)
        for n, e in self.eng.items():
            if n != "sp":
                e.h.wait_ge(self.bar, self.nbar)
            for k, (sem, v) in toks.items():
                e.waited[k] = max(e.waited.get(k, 0), v)

    def finish(self):
        sp = self.eng["sp"]
        self._wait(sp, self.all_tokens())


class Cfg:
    def __init__(self, T=4096, NS=4, NP=128, NPHYS=5120):
        self.T = T
        self.NS = NS
        self.NP = NP
        self.NPHYS = NPHYS
        self.TT = T + NS
        self.NT = T // 128
        self.TB = 512 if T >= 512 else T
        self.NTB = T // self.TB


def build_program(cfg):
    nc = bass.Bass("TRN2", target_bir_lowering=False)
    T, NS, TT, NT, TB, NTB = cfg.T, cfg.NS, cfg.TT, cfg.NT, cfg.TB, cfg.NTB

    def din(name, shape, dt=F32):
        return nc.dram_tensor(name, list(shape), dt, kind="ExternalInput").ap()

    def dout(name, shape, dt=F32):
        return nc.dram_tensor(name, list(shape), dt, kind="ExternalOutput").ap()

    def dscr(name, shape, dt):
        return nc.dram_tensor(name, list(shape), dt, kind="ExternalOutput").ap()

    xp = din("xp", [T, D])
    xs = din("xs", [NS, D])
    st_win = din("st_win", [NS, 512, 512])
    st_conv = din("st_conv", [NS, 30, D])
    ln1_g = din("ln1_g", [1, D])
    w_in = din("w_in", [D, DIN])
    ident_in = din("ident", [128, 128])

    y_p = dout("y_p", [T, D])
    y_s = dout("y_s", [NS, D])
    cmp_p = dout("cmp_p", [T, 512])
    cmp_s = dout("cmp_s", [NS, 512])
    slc_p = dout("slc_p", [T, 512])
    slc_s = dout("slc_s", [NS, 512])
    win_p = dout("win_p", [512, 512])
    win_s = dout("win_s", [NS, 512, 512])
    conv_p = dout("conv_p", [30, D])
    conv_s = dout("conv_s", [NS, 30, D])

    v_s = dscr("v_s", [2, TT, KVW], BF16)
    ng_s = dscr("ng_s", [TT, 512], F32)
    sg_s = dscr("sg_s", [TT, 2048], BF16)
    uT_s = dscr("uT_s", [D, TT], BF16)
    qT_s = dscr("qT_s", [D, TT], BF16)
    kT_s = dscr("kT_s", [3, KVW, TT], BF16)

    cvec = din("cvec", [34, D])
    w_pw = din("w_pw", [D, D])
    w_out = din("w_out", [D, D])
    ln2_g = din("ln2_g", [1, D])
    w_up = din("w_up", [D, DFF])
    w_down = din("w_down", [DFF, D])
    lnf_g = din("lnf_g", [1, D])
    relc = din("relc", [128, 4096])
    rels = din("rels", [128, 4096 + 128])
    relw = din("relw", [128, 768])
    fmk = din("fmk", [128, 32 * 63])
    cmat = din("cmat", [128, 2 * 63])
    onehot = din("onehot", [64, 4096])
    wpool = din("wpool", [128, 32])
    cposd = din("cposd", [32, 256])
    cw = din("cw", [32, 2])
    w1bd = din("w1bd", [128, 256])
    w2bd = din("w2bd", [128, 256])
    w2sel = din("w2sel", [128, 128])
    NP_, NPH_ = cfg.NP, cfg.NPHYS
    NCT_ = (8 * NP_ - 1 + 127) // 128
    cache_cmp = din("cache_cmp", [NPH_ * 128, 512])
    cache_slc = din("cache_slc", [NPH_ * 128, 512])
    ptab = din("ptab", [NS, NP_], I32)
    iota_in = din("iota", [128, 1], I32)
    bsel = din("bsel", [128, NP_ * 16])
    bwin = din("bwin", [128, 64])
    bcmp = din("bcmp", [128, NCT_ * 16])
    cms = din("cms", [128, NCT_ * 2 * NP_])
    fms = din("fms", [4, 2 * NP_])
    ao_s = dscr("ao_s", [TT, D], F32)
    mc_s = dscr("mc_s", [TT, D], F32)
    x1_s = dscr("x1_s", [TT, D], F32)
    aT_s = dscr("aT_s", [DFF, TT], BF16)

    es = ExitStack()
    with es:
        kb = KB(nc, es)
        block = es.enter_context(nc.Block())

        @block.sync
        def _(sync):
            phase_a(kb, cfg, locals_ns := dict(
                xp=xp, xs=xs, st_win=st_win, st_conv=st_conv, ln1_g=ln1_g, w_in=w_in, ident_in=ident_in,
                y_p=y_p, y_s=y_s, cmp_p=cmp_p, cmp_s=cmp_s, slc_p=slc_p, slc_s=slc_s, win_p=win_p,
                win_s=win_s, conv_p=conv_p, conv_s=conv_s, uT_s=uT_s, qT_s=qT_s, kT_s=kT_s, v_s=v_s,
                ng_s=ng_s, sg_s=sg_s, cvec=cvec, w_pw=w_pw, w_out=w_out, ln2_g=ln2_g, w_up=w_up,
                w_down=w_down, lnf_g=lnf_g, mc_s=mc_s, x1_s=x1_s, aT_s=aT_s, ao_s=ao_s, relc=relc, rels=rels,
                relw=relw, fmk=fmk, cmat=cmat, onehot=onehot, wpool=wpool, cposd=cposd, cw=cw, w1bd=w1bd,
                w2bd=w2bd, w2sel=w2sel, cache_cmp=cache_cmp, cache_slc=cache_slc, ptab=ptab, iota=iota_in,
                bsel=bsel, bwin=bwin, bcmp=bcmp, cms=cms, fms=fms))
            if (STOP == 0 or STOP >= 12) and not (SKIP & 32):
                phase_att(kb, cfg, locals_ns)
                phase_smp(kb, cfg, locals_ns)
            if STOP == 0 or STOP >= 10:
                phase_b(kb, cfg, locals_ns)
            if STOP == 0 or STOP >= 11:
                phase_e(kb, cfg, locals_ns)
            kb.finish()
    return nc


def phase_a(kb, cfg, g):
    nc = kb.nc
    T, NS, TT, NT, TB, NTB = cfg.T, cfg.NS, cfg.TT, cfg.NT, cfg.TB, cfg.NTB
    xp, xs, w_in = g["xp"], g["xs"], g["w_in"]
    es = ExitStack()
    with es:
        hT = kb.sb("hT", [128, 8, TT], BF16, es)
        ident = kb.sb("ident", [128, 128], BF16, es)
        identf = kb.sb("identf", [128, 128], F32, es)
        g_bc = kb.sb("g_bc", [128, D], F32, es)
        xt = [kb.sb(f"xt{i}", [128, D], F32, es) for i in range(2)]
        hb = [kb.sb(f"hb{i}", [128, D], BF16, es) for i in range(2)]
        junk = kb.sb("junk", [128, D], BF16, es)
        stat = [kb.sb(f"stat{i}", [128, 4], F32, es) for i in range(2)]
        psT = [kb.ps(f"psT{i}", [128, 8, 128], BF16, es) for i in range(2)]
        pm = [kb.ps(f"pm{i}", [128, 512], F32, es) for i in range(6)]
        wfm = [kb.sb(f"wfm{i}", [128, 8, 128], BF16, es) for i in range(4)]
        wtm = [kb.sb(f"wtm{i}", [128, 8, 512], BF16, es) for i in range(2)]
        sgt = [kb.sb(f"sgt{i}", [128, 512], F32, es) for i in range(2)]
        obf = [kb.sb(f"obf{i}", [128, 512], BF16, es) for i in range(3)]
        of32 = [kb.sb(f"of32{i}", [128, 512], F32, es) for i in range(3)]
        ovb = [kb.sb(f"ovb{i}", [128, 256], BF16, es) for i in range(2)]
        utail = kb.sb("utail", [128, 8, 32 + NS], F32, es)
        uto = kb.sb("uto", [32 + NS, D], F32, es)

        kb.dma("pool", ident.t[:], g["ident_in"][:, :], writes=[ident])
        kb.dma("sp", identf.t[:], g["ident_in"][:, :], writes=[identf])
        kb.dma("sp", g_bc.t[:], g["ln1_g"][0:1, :].to_broadcast([128, D]), writes=[g_bc])

        ntile = NT + 1
        for i in range(ntile):
            rows = 128 if i < NT else NS
            src = xp[128 * i:128 * i + 128, :] if i < NT else xs[:, :]
            x_t, h_b, s_t, p_t = xt[i % 2], hb[i % 2], stat[i % 2], psT[i % 2]
            kb.dma("sp", x_t.t[0:rows, :], src, writes=[x_t])
            kb.op("act", lambda e: e.activation(out=junk.t[0:rows, :], in_=x_t.t[0:rows, :], func=AF.Square,
                                                accum_out=s_t.t[0:rows, 0:1]),
                  reads=[x_t], writes=[junk, s_t])
            kb.op("act", lambda e: e.activation(out=s_t.t[0:rows, 1:2], in_=s_t.t[0:rows, 0:1], func=AF.Sqrt,
                                                scale=1.0 / D, bias=EPS), reads=[s_t], writes=[s_t])
            kb.op("dve", lambda e: e.reciprocal(out=s_t.t[0:rows, 2:3], in_=s_t.t[0:rows, 1:2]),
                  reads=[s_t], writes=[s_t])
            kb.op("dve", lambda e: e.scalar_tensor_tensor(out=h_b.t[0:rows, :], in0=x_t.t[0:rows, :],
                                                          scalar=s_t.t[0:rows, 2:3], in1=g_bc.t[0:rows, :],
                                                          op0=ALU.mult, op1=ALU.mult),
                  reads=[x_t, s_t, g_bc], writes=[h_b])
            for j in range(8):
                kb.op("pe", lambda e: e.transpose(out=p_t.t[:, j, 0:rows], in_=h_b.t[0:rows, 128 * j:128 * j + 128],
                                                  identity=ident.t[0:rows, 0:rows]),
                      reads=[h_b, ident], writes=[p_t])
            eng = "act" if i % 2 == 0 else "dve"
            if eng == "act":
                kb.op("act", lambda e: e.activation(out=hT.t[:, :, 128 * i:128 * i + rows], in_=p_t.t[:, :, 0:rows],
                                                    func=AF.Copy), reads=[p_t], writes=[hT])
            else:
                kb.op("dve", lambda e: e.tensor_copy(out=hT.t[:, :, 128 * i:128 * i + rows], in_=p_t.t[:, :, 0:rows]),
                      reads=[p_t], writes=[hT])

        if STOP == 1:
            kb.barrier(); return
        tblocks = [(TB * b, TB) for b in range(NTB)] + [(T, NS)]

        def load_wfm(slot, c0):
            wb = wfm[slot]
            kb.dma("pool", wb.t[:, :, :], w_in[:, c0:c0 + 128].rearrange("(k p) c -> p k c", p=128), writes=[wb])
            return wb

        pmi = [0]

        def next_pm():
            p = pm[pmi[0] % len(pm)]
            pmi[0] += 1
            return p

        def fm_matmul(wb, t0, n):
            p = next_pm()
            for k in range(8):
                kb.op("pe", lambda e: e.matmul(p.t[:, 0:n], lhsT=wb.t[:, k, :], rhs=hT.t[:, k, t0:t0 + n],
                                               start=(k == 0), stop=(k == 7)),
                      reads=[wb, hT], writes=[p])
            return p

        cnt = [0]

        uT_s = g["uT_s"]
        for cc in range(8):
            wa = load_wfm((2 * cc) % 4, OFF_U2 + 128 * cc)
            wg = load_wfm((2 * cc + 1) % 4, OFF_U2 + 1024 + 128 * cc)
            for (t0, n) in tblocks:
                pa = fm_matmul(wa, t0, n)
                pg = fm_matmul(wg, t0, n)
                sg = sgt[cnt[0] % 2]
                ob = obf[cnt[0] % 3]
                cnt[0] += 1
                kb.op("act", lambda e: e.activation(out=sg.t[:, 0:n], in_=pg.t[:, 0:n], func=AF.Sigmoid),
                      reads=[pg], writes=[sg])
                kb.op("dve", lambda e: e.tensor_tensor(out=ob.t[:, 0:n], in0=pa.t[:, 0:n], in1=sg.t[:, 0:n], op=ALU.mult),
                      reads=[pa, sg], writes=[ob])
                if t0 + n == T:
                    kb.op("dve", lambda e: e.tensor_tensor(out=utail.t[:, cc, 0:32], in0=pa.t[:, n - 32:n],
                                                           in1=sg.t[:, n - 32:n], op=ALU.mult),
                          reads=[pa, sg], writes=[utail])
                if t0 == T:
                    kb.op("dve", lambda e: e.tensor_tensor(out=utail.t[:, cc, 32:32 + NS], in0=pa.t[:, 0:n],
                                                           in1=sg.t[:, 0:n], op=ALU.mult),
                          reads=[pa, sg], writes=[utail])
                kb.dma("sp", uT_s[128 * cc:128 * cc + 128, t0:t0 + n], ob.t[:, 0:n], reads=[ob])

        if STOP == 2:
            kb.barrier(); return
        ptl = pm[0]
        for cc in range(8):
            kb.op("pe", lambda e: e.transpose(out=ptl.t[0:32 + NS, 128 * (cc % 4):128 * (cc % 4) + 128],
                                              in_=utail.t[:, cc, :], identity=identf.t[:, :]),
                  reads=[utail, identf], writes=[ptl])
            if cc % 4 == 3:
                h0 = 512 * (cc // 4)
                kb.op("act", lambda e: e.activation(out=uto.t[:, h0:h0 + 512], in_=ptl.t[0:32 + NS, :], func=AF.Copy),
                      reads=[ptl], writes=[uto])
        kb.dma("sp", g["conv_p"][:, :], uto.t[2:32, :], reads=[uto])
        for s in range(NS):
            kb.dma("sp", g["conv_s"][s, 29:30, :], uto.t[32 + s:33 + s, :], reads=[uto])
            kb.dma("pool", g["conv_s"][s, 0:29, :], g["st_conv"][s, 1:30, :])

        if STOP == 3:
            kb.barrier(); return
        fm_cols = [(OFF_Q + 128 * j, g["qT_s"][128 * j:128 * j + 128, :], 0.125) for j in range(8)]
        for bi, off in enumerate((OFF_KC, OFF_KS, OFF_KW)):
            for j in range(2):
                fm_cols.append((off + 128 * j, g["kT_s"][bi, 128 * j:128 * j + 128, :], 1.0))
        for ci, (c0, dst, scale) in enumerate(fm_cols):
            wb = load_wfm(ci % 4, c0)
            for (t0, n) in tblocks:
                p = fm_matmul(wb, t0, n)
                ob = obf[cnt[0] % 3]
                cnt[0] += 1
                if cnt[0] % 2 == 0:
                    kb.op("act", lambda e: e.activation(out=ob.t[:, 0:n], in_=p.t[:, 0:n], func=AF.Copy, scale=scale),
                          reads=[p], writes=[ob])
                else:
                    kb.op("dve", lambda e: e.tensor_scalar(out=ob.t[:, 0:n], in0=p.t[:, 0:n], scalar1=scale, scalar2=None,
                                                           op0=ALU.mult), reads=[p], writes=[ob])
                kb.dma("sp", dst[:, t0:t0 + n], ob.t[:, 0:n], reads=[ob])

        if STOP == 4:
            kb.barrier(); return
        tm_blocks = [("kv", 0, OFF_KC, 512), ("kv", 1, OFF_KS, 512), ("kv", 2, OFF_KW, 512)]
        c0 = OFF_NG
        while c0 < DIN:
            w = min(512, DIN - c0)
            tm_blocks.append(("gate", None, c0, w))
            c0 += w
        kv_out_p = [g["cmp_p"], g["slc_p"], None]
        kv_out_s = [g["cmp_s"], g["slc_s"], None]
        for bi, (kind, which, c0, w) in enumerate(tm_blocks):
            wb = wtm[bi % 2]
            for q0 in range(0, w, 128):
                q1 = min(w, q0 + 128)
                kb.dma("pool", wb.t[:, :, q0:q1], w_in[:, c0 + q0:c0 + q1].rearrange("(k p) c -> p k c", p=128), writes=[wb])
            for i in range(ntile - (1 if SKIP & 16 else 0)):
                rows = 128 if i < NT else NS
                r0 = 128 * i
                p = next_pm()
                for k in range(8):
                    kb.op("pe", lambda e: e.matmul(p.t[0:rows, 0:w], lhsT=hT.t[:, k, r0:r0 + rows], rhs=wb.t[:, k, 0:w],
                                                   start=(k == 0), stop=(k == 7)),
                          reads=[wb, hT], writes=[p])
                cnt[0] += 1
                if kind == "kv":
                    of = of32[cnt[0] % 3]
                    kb.op("act", lambda e: e.activation(out=of.t[0:rows, :], in_=p.t[0:rows, :], func=AF.Copy),
                          reads=[p], writes=[of])
                    if which < 2 and not (SKIP & 4):
                        dst = kv_out_p[which][r0:r0 + rows, :] if i < NT else kv_out_s[which][:, :]
                        kb.dma("sp", dst, of.t[0:rows, :], reads=[of])
                    elif which == 2 and not (SKIP & 4):
                        if i < NT and r0 >= T - 512:
                            kb.dma("sp", g["win_p"][r0 - (T - 512):r0 - (T - 512) + 128, :], of.t[0:rows, :], reads=[of])
                        if i == NT:
                            for s in range(NS):
                                kb.dma("sp", g["win_s"][s, 511:512, :], of.t[s:s + 1, :], reads=[of])
                                if not (SKIP & 1):
                                    kb.dma("pool", g["win_s"][s, 0:511, :], g["st_win"][s, 1:512, :])
                    if which >= 1 and not (SKIP & 8):
                        ov = ovb[cnt[0] % 2]
                        if True:
                            kb.op("dve", lambda e: e.tensor_copy(out=ov.t[0:rows, :], in_=of.t[0:rows, 256:512]),
                                  reads=[of], writes=[ov])
                        else:
                            kb.op("dve", lambda e: e.tensor_copy(out=ov.t[0:rows, :], in_=p.t[0:rows, 256:512]),
                                  reads=[p], writes=[ov])
                        kb.dma(os.environ.get("VQ", "sp"), g["v_s"][which - 1, r0:r0 + rows, :], ov.t[0:rows, :], reads=[ov])
                elif not (SKIP & 2):
                    lo = 0
                    if c0 == OFF_NG:
                        of = of32[cnt[0] % 3]
                        ob = obf[cnt[0] % 3]
                        kb.op("act", lambda e: e.activation(out=of.t[0:rows, :], in_=p.t[0:rows, :], func=AF.Sigmoid),
                              reads=[p], writes=[of])
                        kb.dma("sp", g["ng_s"][r0:r0 + rows, :], of.t[0:rows, :], reads=[of])
                        kb.op("dve", lambda e: e.tensor_copy(out=ob.t[0:rows, 48:512], in_=of.t[0:rows, 48:512]),
                              reads=[of], writes=[ob])
                        kb.dma("sp", g["sg_s"][r0:r0 + rows, 0:464], ob.t[0:rows, 48:512], reads=[ob])
                        continue
                    ob = obf[cnt[0] % 3]
                    kb.op("act", lambda e: e.activation(out=ob.t[0:rows, lo:w], in_=p.t[0:rows, lo:w], func=AF.Sigmoid),
                          reads=[p], writes=[ob])
                    d0 = c0 + lo - OFF_GC
                    kb.dma("sp", g["sg_s"][r0:r0 + rows, d0:d0 + (w - lo)], ob.t[0:rows, lo:w], reads=[ob])
        kb.barrier()


def phase_b(kb, cfg, g):
    T, NS, TT, NT, TB, NTB = cfg.T, cfg.NS, cfg.TT, cfg.NT, cfg.TB, cfg.NTB
    uT_s, sg_s, mc_s = g["uT_s"], g["sg_s"], g["mc_s"]
    es = ExitStack()
    with es:
        identf = kb.sb("b_identf", [128, 128], F32, es)
        ones = kb.sb("b_ones", [128, 128], F32, es)
        raw = kb.sb("b_raw", [34, D], F32, es)
        wT = kb.sb("b_wT", [128, 8, 34], F32, es)
        wpw = kb.sb("b_wpw", [128, 8, D], BF16, es)
        ut = [kb.sb(f"b_ut{i}", [128, 30 + TB], BF16, es) for i in range(3)]
        acc = kb.sb("b_acc", [128, 8, TB], F32, es)
        ysq = kb.sb("b_ysq", [128, 8, TB], F32, es)
        sT = kb.sb("b_sT", [128, 8, TB], BF16, es)
        mean = kb.sb("b_mean", [128, TB], F32, es)
        ex2 = kb.sb("b_ex2", [128, TB], F32, es)
        rstd = kb.sb("b_rstd", [128, TB], F32, es)
        tmp = [kb.sb(f"b_tmp{i}", [128, TB], F32, es) for i in range(2)]
        sgc = [kb.sb(f"b_sgc{i}", [128, D], BF16, es) for i in range(2)]
        mc = [kb.sb(f"b_mc{i}", [128, D], F32, es) for i in range(2)]
        stt = kb.sb("b_st", [30, D], F32, es)
        uext = kb.sb("b_uext", [128, 8, NS, 31], F32, es)
        unew = kb.sb("b_unew", [128, 8, NS], BF16, es)
        prod = kb.sb("b_prod", [128, 8, 31], F32, es)
        ys = kb.sb("b_ys", [128, 8], F32, es)
        pt = kb.ps("b_pt", [128, 8, 64], F32, es)
        pmean = kb.ps("b_pmean", [128, 512], F32, es)
        psq = kb.ps("b_psq", [128, 512], F32, es)
        pc = [kb.ps(f"b_pc{i}", [128, 512], F32, es) for i in range(4)]

        kb.dma("sp", identf.t[:], g["ident_in"][:, :], writes=[identf])
        kb.dma("sp", raw.t[:], g["cvec"][:, :], writes=[raw])
        kb.op("dve", lambda e: e.memset(ones.t[:], 1.0 / D), writes=[ones])
        for q0 in range(0, D, 256):
            kb.dma("pool", wpw.t[:, :, q0:q0 + 256], g["w_pw"][:, q0:q0 + 256].rearrange("(k p) c -> p k c", p=128),
                   writes=[wpw])
        for cc in range(8):
            kb.op("pe", lambda e: e.transpose(out=pt.t[:, cc, 0:34], in_=raw.t[0:34, 128 * cc:128 * cc + 128],
                                              identity=identf.t[0:34, 0:34]), reads=[raw, identf], writes=[pt])
        kb.op("act", lambda e: e.activation(out=wT.t[:, :, :], in_=pt.t[:, :, 0:34], func=AF.Copy), reads=[pt], writes=[wT])

        for s in range(NS):
            kb.dma("sp", stt.t[:, :], g["st_conv"][s, :, :], writes=[stt])
            for cc in range(8):
                kb.op("pe", lambda e: e.transpose(out=pt.t[:, cc, 0:30], in_=stt.t[0:30, 128 * cc:128 * cc + 128],
                                                  identity=identf.t[0:30, 0:30]), reads=[stt, identf], writes=[pt])
            kb.op("act", lambda e: e.activation(out=uext.t[:, :, s, 0:30], in_=pt.t[:, :, 0:30], func=AF.Copy),
                  reads=[pt], writes=[uext])
        for cc in range(8):
            kb.dma("sp", unew.t[:, cc, :], uT_s[128 * cc:128 * cc + 128, T:T + NS], writes=[unew])
        kb.op("dve", lambda e: e.tensor_copy(out=uext.t[:, :, :, 30], in_=unew.t[:, :, :]), reads=[unew], writes=[uext])

        tblocks = [(TB * b, TB) for b in range(NTB)] + [(T, NS)]
        ui = 0
        tix = 0
        for (t0, n) in tblocks:
            if t0 < T:
                for cc in range(8):
                    u = ut[ui % 3]
                    ui += 1
                    if t0 == 0:
                        kb.op("pool", lambda e: e.memset(u.t[:, 0:30], 0.0), writes=[u])
                        kb.dma("sp", u.t[:, 30:30 + n], uT_s[128 * cc:128 * cc + 128, 0:n], writes=[u])
                    else:
                        kb.dma("sp", u.t[:, 0:30 + n], uT_s[128 * cc:128 * cc + 128, t0 - 30:t0 + n], writes=[u])
                    eng = "dve"
                    kb.op(eng, lambda e: e.tensor_scalar(out=acc.t[:, cc, 0:n], in0=u.t[:, 0:n], scalar1=wT.t[:, cc, 0:1],
                                                         scalar2=wT.t[:, cc, 31:32], op0=ALU.mult, op1=ALU.add),
                          reads=[u, wT], writes=[acc])
                    for k in range(1, CW):
                        kb.op(eng, lambda e: e.scalar_tensor_tensor(out=acc.t[:, cc, 0:n], in0=u.t[:, k:k + n],
                                                                    scalar=wT.t[:, cc, k:k + 1], in1=acc.t[:, cc, 0:n],
                                                                    op0=ALU.mult, op1=ALU.add),
                              reads=[u, wT, acc], writes=[acc])
            else:
                for s in range(NS):
                    kb.op("dve", lambda e: e.tensor_tensor(out=prod.t[:, :, :], in0=uext.t[:, :, s, :], in1=wT.t[:, :, 0:31],
                                                           op=ALU.mult), reads=[uext, wT], writes=[prod])
                    kb.op("dve", lambda e: e.tensor_reduce(out=ys.t[:, :], in_=prod.t[:, :, :], axis=AX.X, op=ALU.add),
                          reads=[prod], writes=[ys])
                    kb.op("dve", lambda e: e.tensor_tensor(out=acc.t[:, :, s], in0=ys.t[:, :], in1=wT.t[:, :, 31], op=ALU.add),
                          reads=[ys, wT], writes=[acc])
            kb.op("act", lambda e: e.activation(out=ysq.t[:, :, 0:n], in_=acc.t[:, :, 0:n], func=AF.Square),
                  reads=[acc], writes=[ysq])
            for cc in range(8):
                kb.op("pe", lambda e: e.matmul(pmean.t[:, 0:n], lhsT=ones.t[:, :], rhs=acc.t[:, cc, 0:n],
                                               start=(cc == 0), stop=(cc == 7)), reads=[ones, acc], writes=[pmean])
            for cc in range(8):
                kb.op("pe", lambda e: e.matmul(psq.t[:, 0:n], lhsT=ones.t[:, :], rhs=ysq.t[:, cc, 0:n],
                                               start=(cc == 0), stop=(cc == 7)), reads=[ones, ysq], writes=[psq])
            kb.op("act", lambda e: e.activation(out=mean.t[:, 0:n], in_=pmean.t[:, 0:n], func=AF.Copy),
                  reads=[pmean], writes=[mean])
            kb.op("dve", lambda e: e.tensor_copy(out=ex2.t[:, 0:n], in_=psq.t[:, 0:n]), reads=[psq], writes=[ex2])
            kb.op("dve", lambda e: e.tensor_tensor(out=rstd.t[:, 0:n], in0=mean.t[:, 0:n], in1=mean.t[:, 0:n], op=ALU.mult),
                  reads=[mean], writes=[rstd])
            kb.op("dve", lambda e: e.tensor_tensor(out=ex2.t[:, 0:n], in0=ex2.t[:, 0:n], in1=rstd.t[:, 0:n], op=ALU.subtract),
                  reads=[ex2, rstd], writes=[ex2])
            kb.op("act", lambda e: e.activation(out=ex2.t[:, 0:n], in_=ex2.t[:, 0:n], func=AF.Sqrt, bias=EPS),
                  reads=[ex2], writes=[ex2])
            kb.op("dve", lambda e: e.reciprocal(out=rstd.t[:, 0:n], in_=ex2.t[:, 0:n]), reads=[ex2], writes=[rstd])
            for cc in range(8):
                tm_ = tmp[cc % 2]
                eng = "dve" if cc % 2 == 0 else "pool"
                kb.op(eng, lambda e: e.tensor_tensor(out=tm_.t[:, 0:n], in0=acc.t[:, cc, 0:n], in1=mean.t[:, 0:n],
                                                     op=ALU.subtract), reads=[acc, mean], writes=[tm_])
                kb.op(eng, lambda e: e.tensor_tensor(out=tm_.t[:, 0:n], in0=tm_.t[:, 0:n], in1=rstd.t[:, 0:n], op=ALU.mult),
                      reads=[tm_, rstd], writes=[tm_])
                kb.op("act", lambda e: e.activation(out=sT.t[:, cc, 0:n], in_=tm_.t[:, 0:n], func=AF.Silu,
                                                    scale=wT.t[:, cc, 32:33], bias=wT.t[:, cc, 33:34]),
                      reads=[tm_, wT], writes=[sT])
            for r in range(0, n, 128):
                rows = min(128, n - r)
                tok = t0 + r
                sgt_ = sgc[tix % 2]
                mct = mc[tix % 2]
                tix += 1
                kb.dma("sp", sgt_.t[0:rows, :], sg_s[tok:tok + rows, 0:D], writes=[sgt_])
                for half in range(2):
                    p = pc[(2 * tix + half) % 4]
                    for cc in range(8):
                        kb.op("pe", lambda e: e.matmul(p.t[0:rows, :], lhsT=sT.t[:, cc, r:r + rows],
                                                       rhs=wpw.t[:, cc, 512 * half:512 * half + 512],
                                                       start=(cc == 0), stop=(cc == 7)), reads=[sT, wpw], writes=[p])
                    kb.op("dve", lambda e: e.tensor_tensor(out=mct.t[0:rows, 512 * half:512 * half + 512], in0=p.t[0:rows, :],
                                                           in1=sgt_.t[0:rows, 512 * half:512 * half + 512], op=ALU.mult),
                          reads=[p, sgt_], writes=[mct])
                kb.dma("sp", mc_s[tok:tok + rows, :], mct.t[0:rows, :], reads=[mct])
        kb.barrier()


def phase_e(kb, cfg, g):
    T, NS, TT, NT, TB, NTB = cfg.T, cfg.NS, cfg.TT, cfg.NT, cfg.TB, cfg.NTB
    xp, xs = g["xp"], g["xs"]
    mc_s, x1_s, aT_s = g["mc_s"], g["x1_s"], g["aT_s"]
    ntile = NT + 1
    tblocks = [(TB * b, TB) for b in range(NTB)] + [(T, NS)]
    es0 = ExitStack()
    with es0:
        h2T = kb.sb("e_h2T", [128, 8, TT], BF16, es0)
        ident = kb.sb("e_ident", [128, 128], BF16, es0)
        kb.dma("pool", ident.t[:], g["ident_in"][:, :], writes=[ident])
        es = ExitStack()
        with es:
            wo = kb.sb("e_wo", [128, 8, D], BF16, es)
            g2 = kb.sb("e_g2", [128, D], F32, es)
            mt = [kb.sb(f"e_mt{i}", [128, D], F32, es) for i in range(2)]
            mb = [kb.sb(f"e_mb{i}", [128, D], BF16, es) for i in range(2)]
            e_at = [kb.sb(f"e_ao{i}", [128, D], F32, es) for i in range(2)]
            e_sga = [kb.sb(f"e_sga{i}", [128, D], BF16, es) for i in range(2)]
            mT = [kb.sb(f"e_mT{i}", [128, 8, 128], BF16, es) for i in range(2)]
            xt = [kb.sb(f"e_xt{i}", [128, D], F32, es) for i in range(2)]
            x1 = [kb.sb(f"e_x1{i}", [128, D], F32, es) for i in range(2)]
            hb = [kb.sb(f"e_hb{i}", [128, D], BF16, es) for i in range(2)]
            junk = kb.sb("e_junk", [128, D], BF16, es)
            stat = [kb.sb(f"e_stat{i}", [128, 4], F32, es) for i in range(2)]
            psT = [kb.ps(f"e_psT{i}", [128, 8, 128], BF16, es) for i in range(2)]
            psH = [kb.ps(f"e_psH{i}", [128, 8, 128], BF16, es) for i in range(2)]
            px = [kb.ps(f"e_px{i}", [128, 512], F32, es) for i in range(4)]
            for q0 in range(0, D, 256):
                kb.dma("pool", wo.t[:, :, q0:q0 + 256], g["w_out"][:, q0:q0 + 256].rearrange("(k p) c -> p k c", p=128),
                       writes=[wo])
            kb.dma("sp", g2.t[:], g["ln2_g"][0:1, :].to_broadcast([128, D]), writes=[g2])
            for i in range(ntile):
                rows = 128 if i < NT else NS
                r0 = 128 * i
                m_t, m_b, m_T, x_t, x_1, h_b, s_t = mt[i % 2], mb[i % 2], mT[i % 2], xt[i % 2], x1[i % 2], hb[i % 2], stat[i % 2]
                p_T, p_H = psT[i % 2], psH[i % 2]
                a_t, sga = e_at[i % 2], e_sga[i % 2]
                kb.dma("sp", m_t.t[0:rows, :], mc_s[r0:r0 + rows, :], writes=[m_t])
                kb.dma("sp", x_t.t[0:rows, :], xp[r0:r0 + rows, :] if i < NT else xs[:, :], writes=[x_t])
                kb.dma("sp", a_t.t[0:rows, :], g["ao_s"][r0:r0 + rows, :], writes=[a_t])
                kb.dma("sp", sga.t[0:rows, :], g["sg_s"][r0:r0 + rows, D:2 * D], writes=[sga])
                kb.op("pool", lambda e: e.tensor_tensor(out=a_t.t[0:rows, :], in0=a_t.t[0:rows, :], in1=sga.t[0:rows, :],
                                                        op=ALU.mult), reads=[a_t, sga], writes=[a_t])
                kb.op("pool", lambda e: e.tensor_tensor(out=m_b.t[0:rows, :], in0=a_t.t[0:rows, :], in1=m_t.t[0:rows, :],
                                                        op=ALU.add), reads=[a_t, m_t], writes=[m_b])
                for j in range(8):
                    kb.op("pe", lambda e: e.transpose(out=p_T.t[:, j, 0:rows], in_=m_b.t[0:rows, 128 * j:128 * j + 128],
                                                      identity=ident.t[0:rows, 0:rows]), reads=[m_b, ident], writes=[p_T])
                kb.op("dve", lambda e: e.tensor_copy(out=m_T.t[:, :, 0:rows], in_=p_T.t[:, :, 0:rows]), reads=[p_T], writes=[m_T])
                for half in range(2):
                    p = px[(2 * i + half) % 4]
                    for k in range(8):
                        kb.op("pe", lambda e: e.matmul(p.t[0:rows, :], lhsT=m_T.t[:, k, 0:rows],
                                                       rhs=wo.t[:, k, 512 * half:512 * half + 512],
                                                       start=(k == 0), stop=(k == 7)), reads=[m_T, wo], writes=[p])
                    kb.op("dve", lambda e: e.tensor_tensor(out=x_1.t[0:rows, 512 * half:512 * half + 512], in0=p.t[0:rows, :],
                                                           in1=x_t.t[0:rows, 512 * half:512 * half + 512], op=ALU.add),
                          reads=[p, x_t], writes=[x_1])
                kb.dma("sp", x1_s[r0:r0 + rows, :], x_1.t[0:rows, :], reads=[x_1])
                kb.op("act", lambda e: e.activation(out=junk.t[0:rows, :], in_=x_1.t[0:rows, :], func=AF.Square,
                                                    accum_out=s_t.t[0:rows, 0:1]), reads=[x_1], writes=[junk, s_t])
                kb.op("act", lambda e: e.activation(out=s_t.t[0:rows, 1:2], in_=s_t.t[0:rows, 0:1], func=AF.Sqrt,
                                                    scale=1.0 / D, bias=EPS), reads=[s_t], writes=[s_t])
                kb.op("dve", lambda e: e.reciprocal(out=s_t.t[0:rows, 2:3], in_=s_t.t[0:rows, 1:2]), reads=[s_t], writes=[s_t])
                kb.op("dve", lambda e: e.scalar_tensor_tensor(out=h_b.t[0:rows, :], in0=x_1.t[0:rows, :],
                                                              scalar=s_t.t[0:rows, 2:3], in1=g2.t[0:rows, :],
                                                              op0=ALU.mult, op1=ALU.mult), reads=[x_1, s_t, g2], writes=[h_b])
                for j in range(8):
                    kb.op("pe", lambda e: e.transpose(out=p_H.t[:, j, 0:rows], in_=h_b.t[0:rows, 128 * j:128 * j + 128],
                                                      identity=ident.t[0:rows, 0:rows]), reads=[h_b, ident], writes=[p_H])
                kb.op("act", lambda e: e.activation(out=h2T.t[:, :, r0:r0 + rows], in_=p_H.t[:, :, 0:rows], func=AF.Copy),
                      reads=[p_H], writes=[h2T])
            kb.barrier()
        es = ExitStack()
        with es:
            wfm = [kb.sb(f"e_wfm{i}", [128, 8, 128], BF16, es) for i in range(3)]
            rl = [kb.sb(f"e_rl{i}", [128, 512], F32, es) for i in range(2)]
            ab = [kb.sb(f"e_ab{i}", [128, 512], BF16, es) for i in range(3)]
            pm = [kb.ps(f"e_pm{i}", [128, 512], F32, es) for i in range(6)]
            c = 0
            for f in range(DFF // 128):
                wb = wfm[f % 3]
                kb.dma("pool", wb.t[:, :, :], g["w_up"][:, 128 * f:128 * f + 128].rearrange("(k p) c -> p k c", p=128),
                       writes=[wb])
                for (t0, n) in tblocks:
                    p = pm[c % 6]
                    r_, a_ = rl[c % 2], ab[c % 3]
                    c += 1
                    for k in range(8):
                        kb.op("pe", lambda e: e.matmul(p.t[:, 0:n], lhsT=wb.t[:, k, :], rhs=h2T.t[:, k, t0:t0 + n],
                                                       start=(k == 0), stop=(k == 7)), reads=[wb, h2T], writes=[p])
                    kb.op("act", lambda e: e.activation(out=r_.t[:, 0:n], in_=p.t[:, 0:n], func=AF.Relu), reads=[p], writes=[r_])
                    eng = "dve" if c % 2 == 0 else "pool"
                    kb.op(eng, lambda e: e.tensor_tensor(out=a_.t[:, 0:n], in0=r_.t[:, 0:n], in1=r_.t[:, 0:n], op=ALU.mult),
                          reads=[r_], writes=[a_])
                    kb.dma("sp", aT_s[128 * f:128 * f + 128, t0:t0 + n], a_.t[:, 0:n], reads=[a_])
            kb.barrier()
    es = ExitStack()
    with es:
        NF = DFF // 128
        wd = kb.sb("e_wd", [128, NF, D], BF16, es)
        gf = kb.sb("e_gf", [128, D], F32, es)
        at = [kb.sb(f"e_at{i}", [128, NF, 128], BF16, es) for i in range(2)]
        x1 = [kb.sb(f"e3_x1{i}", [128, D], F32, es) for i in range(2)]
        yp = [kb.sb(f"e_yp{i}", [128, D], F32, es) for i in range(2)]
        yo = [kb.sb(f"e_yo{i}", [128, D], F32, es) for i in range(2)]
        junk = kb.sb("e3_junk", [128, D], BF16, es)
        stat = [kb.sb(f"e3_stat{i}", [128, 4], F32, es) for i in range(2)]
        py = [kb.ps(f"e_py{i}", [128, 512], F32, es) for i in range(4)]
        for f0 in range(0, NF, 4):
            for q0 in range(0, D, 512):
                kb.dma("pool", wd.t[:, f0:f0 + 4, q0:q0 + 512],
                       g["w_down"][128 * f0:128 * f0 + 512, q0:q0 + 512].rearrange("(f p) c -> p f c", p=128), writes=[wd])
        kb.dma("sp", gf.t[:], g["lnf_g"][0:1, :].to_broadcast([128, D]), writes=[gf])
        for i in range(ntile):
            rows = 128 if i < NT else NS
            r0 = 128 * i
            a_t, x_1, y_p, y_o, s_t = at[i % 2], x1[i % 2], yp[i % 2], yo[i % 2], stat[i % 2]
            for f0 in range(0, NF, 8):
                kb.dma("sp", a_t.t[:, f0:f0 + 8, 0:rows],
                       aT_s[128 * f0:128 * f0 + 1024, r0:r0 + rows].rearrange("(f p) t -> p f t", p=128), writes=[a_t])
            kb.dma("sp", x_1.t[0:rows, :], x1_s[r0:r0 + rows, :], writes=[x_1])
            for half in range(2):
                p = py[(2 * i + half) % 4]
                for f in range(NF):
                    kb.op("pe", lambda e: e.matmul(p.t[0:rows, :], lhsT=a_t.t[:, f, 0:rows],
                                                   rhs=wd.t[:, f, 512 * half:512 * half + 512],
                                                   start=(f == 0), stop=(f == NF - 1)), reads=[a_t, wd], writes=[p])
                kb.op("dve", lambda e: e.tensor_tensor(out=y_p.t[0:rows, 512 * half:512 * half + 512], in0=p.t[0:rows, :],
                                                       in1=x_1.t[0:rows, 512 * half:512 * half + 512], op=ALU.add),
                      reads=[p, x_1], writes=[y_p])
            kb.op("act", lambda e: e.activation(out=junk.t[0:rows, :], in_=y_p.t[0:rows, :], func=AF.Square,
                                                accum_out=s_t.t[0:rows, 0:1]), reads=[y_p], writes=[junk, s_t])
            kb.op("act", lambda e: e.activation(out=s_t.t[0:rows, 1:2], in_=s_t.t[0:rows, 0:1], func=AF.Sqrt,
                                                scale=1.0 / D, bias=EPS), reads=[s_t], writes=[s_t])
            kb.op("dve", lambda e: e.reciprocal(out=s_t.t[0:rows, 2:3], in_=s_t.t[0:rows, 1:2]), reads=[s_t], writes=[s_t])
            kb.op("dve", lambda e: e.scalar_tensor_tensor(out=y_o.t[0:rows, :], in0=y_p.t[0:rows, :],
                                                          scalar=s_t.t[0:rows, 2:3], in1=gf.t[0:rows, :],
                                                          op0=ALU.mult, op1=ALU.mult), reads=[y_p, s_t, gf], writes=[y_o])
            kb.dma("sp", g["y_p"][r0:r0 + rows, :] if i < NT else g["y_s"][:, :], y_o.t[0:rows, :], reads=[y_o])
        kb.barrier()


ALIBI_CUT = 90.0
SLOPES = [float(np.float32(2.0) ** np.float32(-8.0 * (h + 1) / NH)) for h in range(NH)]


def phase_att(kb, cfg, g):
    T, NS, TT, NT = cfg.T, cfg.NS, cfg.TT, cfg.NT
    NB = 8 * NT - 1
    NCT = (NB + 127) // 128
    NSB = T // 64
    do_sel = NSB > 16
    qT_s, kT_s, v_s = g["qT_s"], g["kT_s"], g["v_s"]
    es = ExitStack()
    with es:
        ident = kb.sb("a_ident", [128, 128], BF16, es)
        relc = kb.sb("a_relc", [128, 4096], F32, es)
        rels = kb.sb("a_rels", [128, T + 128], F32, es)
        relw = kb.sb("a_relw", [128, 768], F32, es)
        fmk = kb.sb("a_fmk", [128, 32, 63], F32, es)
        ksa = kb.sb("a_ksa", [128, 4, T], BF16, es)
        vsa = kb.sb("a_vsa", [128, NT, 4, 65], BF16, es)
        ckT = kb.sb("a_ckT", [64, 4, 256], BF16, es)
        rc = kb.sb("a_rc", [128, 2, 4, 128], BF16, es)
        kb.dma("pool", ident.t[:], g["ident_in"][:, :], writes=[ident])
        kb.dma("sp", relc.t[:], g["relc"][:, :], writes=[relc])
        kb.dma("sp", rels.t[:], g["rels"][:, 0:T + 128], writes=[rels])
        kb.dma("sp", relw.t[:], g["relw"][:, :], writes=[relw])
        kb.dma("sp", fmk.t[:], g["fmk"][:, :].rearrange("p (a b) -> p a b", b=63), writes=[fmk])
        for kv in range(4):
            kb.dma("sp", ksa.t[0:64, kv, :], kT_s[1, 64 * kv:64 * kv + 64, 0:T], writes=[ksa])
            kb.dma("pool", ksa.t[64:128, kv, :], g["onehot"][:, 0:T], writes=[ksa])
        kb.op("pool", lambda e: e.memset(vsa.t[:, :, :, 64:65], 1.0), writes=[vsa])
        for kv in range(4):
            for a0 in range(0, NT, 8):
                a1 = min(NT, a0 + 8)
                kb.dma("sp", vsa.t[:, a0:a1, kv, 0:64],
                       v_s[0, 128 * a0:128 * a1, 64 * kv:64 * kv + 64].rearrange("(a p) d -> p a d", p=128), writes=[vsa])
        kb.op("pool", lambda e: e.memset(ckT.t[:], 0.0), writes=[ckT])
        kb.op("pool", lambda e: e.memset(rc.t[:], 0.0), writes=[rc])
        kb.op("pool", lambda e: e.memset(rc.t[:, :, :, 64:65], 1.0), reads=[rc], writes=[rc])
        for kv in range(4):
            kb.dma("pool", rc.t[:, :, kv, 65:128], g["cmat"][:, :].rearrange("p (a b) -> p a b", b=63), writes=[rc])

        es1 = ExitStack()
        with es1:
            wpool = kb.sb("c_wpool", [128, 2, 16], F32, es1)
            cposd = kb.sb("c_cposd", [32, 2, 128], F32, es1)
            cw = kb.sb("c_cw", [32, 2], F32, es1)
            w1bd = kb.sb("c_w1bd", [128, 2, 128], F32, es1)
            w2bd = kb.sb("c_w2bd", [128, 2, 128], F32, es1)
            w2sel = kb.sb("c_w2sel", [128, 2, 64], F32, es1)
            pg = [kb.sb(f"c_pg{i}", [128, 512], F32, es1) for i in range(2)]
            ab = kb.sb("c_ab", [128, 4, NT, 16], F32, es1)
            af = kb.sb("c_af", [128, 4, 8 * NT], F32, es1)
            bf = kb.sb("c_bf", [128, 4, 8 * NT], F32, es1)
            pooled = kb.sb("c_pooled", [128, 4, 8 * NT], F32, es1)
            hid = kb.sb("c_hid", [128, 4, 8 * NT], F32, es1)
            pe_sb = kb.sb("c_pe", [128, 2], F32, es1)
            pp = [kb.ps(f"c_pp{i}", [128, 32, 16], F32, es1) for i in range(4)]
            pe_ps = kb.ps("c_peps", [128, 2], F32, es1)
            hp = [kb.ps(f"c_hp{i}", [128, 512], F32, es1) for i in range(2)]
            for nm, t_, src in (("wpool", wpool, g["wpool"][:, :].rearrange("p (a b) -> p a b", b=16)),
                                ("cposd", cposd, g["cposd"][:, :].rearrange("p (a b) -> p a b", b=128)),
                                ("cw", cw, g["cw"][:, :]),
                                ("w1bd", w1bd, g["w1bd"][:, :].rearrange("p (a b) -> p a b", b=128)),
                                ("w2bd", w2bd, g["w2bd"][:, :].rearrange("p (a b) -> p a b", b=128)),
                                ("w2sel", w2sel, g["w2sel"][:, :].rearrange("p (a b) -> p a b", b=64))):
                kb.dma("sp", t_.t[:], src, writes=[t_])
            for j in range(2):
                kb.op("pe", lambda e: e.matmul(pe_ps.t[:, j:j + 1], lhsT=cposd.t[:, j, :], rhs=cw.t[:, j:j + 1],
                                               start=True, stop=True), reads=[cposd, cw], writes=[pe_ps])
            kb.op("act", lambda e: e.activation(out=pe_sb.t[:, :], in_=pe_ps.t[:, :], func=AF.Copy), reads=[pe_ps], writes=[pe_sb])
            for i in range(NT):
                pgt = pg[i % 2]
                kb.dma("sp", pgt.t[:, :], g["cmp_p"][128 * i:128 * i + 128, :], writes=[pgt])
                for fc in range(4):
                    kb.op("pe", lambda e: e.matmul(pp[fc].t[:, i, :], lhsT=pgt.t[:, 128 * fc:128 * fc + 128],
                                                   rhs=wpool.t[:, fc // 2, :], start=True, stop=True),
                          reads=[pgt, wpool], writes=[pp[fc]])
            for fc in range(4):
                kb.op("act", lambda e: e.activation(out=ab.t[:, fc, :, :], in_=pp[fc].t[:, 0:NT, :], func=AF.Copy),
                      reads=[pp[fc]], writes=[ab])
            kb.op("dve", lambda e: e.tensor_copy(out=af.t[:, :, :].rearrange("p f (t c) -> p f t c", c=8), in_=ab.t[:, :, :, 0:8]),
                  reads=[ab], writes=[af])
            kb.op("dve", lambda e: e.tensor_copy(out=bf.t[:, :, :].rearrange("p f (t c) -> p f t c", c=8), in_=ab.t[:, :, :, 8:16]),
                  reads=[ab], writes=[bf])
            kb.op("dve", lambda e: e.tensor_tensor(out=pooled.t[:, :, 0:NB], in0=af.t[:, :, 0:NB], in1=bf.t[:, :, 1:NB + 1],
                                                   op=ALU.add), reads=[af, bf], writes=[pooled])
            for fc in range(4):
                kb.op("dve", lambda e: e.tensor_scalar(out=pooled.t[:, fc, 0:NB], in0=pooled.t[:, fc, 0:NB],
                                                       scalar1=pe_sb.t[:, fc // 2:fc // 2 + 1], scalar2=None, op0=ALU.add),
                      reads=[pooled, pe_sb], writes=[pooled])
            for fc in range(4):
                h_ = hp[fc % 2]
                kb.op("pe", lambda e: e.matmul(h_.t[:, 0:NB], lhsT=w1bd.t[:, fc // 2, :], rhs=pooled.t[:, fc, 0:NB],
                                               start=True, stop=True), reads=[w1bd, pooled], writes=[h_])
                kb.op("act", lambda e: e.activation(out=hid.t[:, fc, 0:NB], in_=h_.t[:, 0:NB], func=AF.Silu),
                      reads=[h_], writes=[hid])
            for kv in range(4):
                h_ = hp[kv % 2]
                kb.op("pe", lambda e: e.matmul(h_.t[0:64, 0:NB], lhsT=w2sel.t[:, kv % 2, :], rhs=hid.t[:, kv // 2, 0:NB],
                                               start=True, stop=True), reads=[w2sel, hid], writes=[h_])
                kb.op("act", lambda e: e.activation(out=ckT.t[0:64, kv, 0:NB], in_=h_.t[0:64, 0:NB], func=AF.Copy),
                      reads=[h_], writes=[ckT])
            for ct in range(NCT):
                nb = min(128, NB - 128 * ct)
                for fv in range(2):
                    h_ = hp[fv % 2]
                    kb.op("pe", lambda e: e.matmul(h_.t[0:nb, 0:128], lhsT=hid.t[:, 2 + fv, 128 * ct:128 * ct + nb],
                                                   rhs=w2bd.t[:, 1, :], start=True, stop=True), reads=[hid, w2bd], writes=[h_])
                    kb.op("act", lambda e: e.activation(out=rc.t[0:nb, ct, 2 * fv:2 * fv + 2, 0:64],
                                                        in_=h_.t[0:nb, 0:128].rearrange("p (a b) -> p a b", b=64), func=AF.Copy),
                          reads=[h_], writes=[rc])
            kb.barrier()

        es2 = ExitStack()
        with es2:
            qa = [kb.sb(f"a_qa{i}", [128, 16, 128], BF16, es2) for i in range(2)]
            ngt = [kb.sb(f"a_ng{i}", [128, 512], F32, es2) for i in range(2)]
            kwt = [kb.sb(f"a_kw{i}", [64, 4, 640], BF16, es2) for i in range(2)]
            vwt = [kb.sb(f"a_vw{i}", [128, 5, 4, 65], BF16, es2) for i in range(2)]
            sbs = [kb.sb(f"a_sbs{i}", [128, 512], F32, es2) for i in range(3)]
            ptb = [kb.sb(f"a_ptb{i}", [128, 512], BF16, es2) for i in range(3)]
            o4 = [kb.sb(f"a_o4{i}", [128, 4, 128], F32, es2) for i in range(2)]
            rd = [kb.sb(f"a_rd{i}", [128, 4], F32, es2) for i in range(2)]
            gsc = [kb.sb(f"a_gsc{i}", [128, 4], F32, es2) for i in range(2)]
            sc = kb.sb("a_sc", [128, 64], F32, es2)
            sc2 = kb.sb("a_sc2", [128, 64], F32, es2)
            m8a = kb.sb("a_m8a", [128, 8], F32, es2)
            m8b = kb.sb("a_m8b", [128, 8], F32, es2)
            mk = kb.sb("a_mk", [128, 128], BF16, es2)
            ao = [kb.sb(f"a_ao{i}", [128, D], F32, es2) for i in range(2)]
            ps_s = [kb.ps(f"a_pss{i}", [128, 512], F32, es2) for i in range(3)]
            ps_o = [kb.ps(f"a_pso{i}", [128, 4, 128], F32, es2) for i in range(3)]
            ps_t = kb.ps("a_pst", [128, 128], BF16, es2)
            for i in range(2):
                kb.op("pool", lambda e: e.memset(vwt[i].t[:, :, :, 64:65], 1.0), writes=[vwt[i]])
            kb.op("pool", lambda e: e.memset(mk.t[:], 0.0), writes=[mk])
            rr = [0, 0]

            def branch(qt, kv, q_a, kts, lhs_fn, rel, rel_x0_fn, rhs_fn, nk_rows):
                po = ps_o[rr[1] % 3]
                rr[1] += 1
                for gi in range(4):
                    h = 4 * kv + gi
                    kts_h = [kt for kt in kts if SLOPES[h] * max(0, 128 * (qt - kt) - 127) <= ALIBI_CUT]
                    chunks = [kts_h[a:a + 4] for a in range(0, len(kts_h), 4)]
                    first = True
                    for ci, ch in enumerate(chunks):
                        pss = ps_s[rr[0] % 3]
                        sb_, pt_ = sbs[rr[0] % 3], ptb[rr[0] % 3]
                        rr[0] += 1
                        w = 128 * len(ch)
                        for a, kt in enumerate(ch):
                            lhsT = lhs_fn(kv, kt)
                            kb.op("pe", lambda e: e.matmul(pss.t[:, 128 * a:128 * a + 128], lhsT=lhsT[0], rhs=q_a.t[0:nk_rows, h, :],
                                                           start=True, stop=True), reads=[lhsT[1], q_a], writes=[pss])
                        x0 = rel_x0_fn(ch[0])
                        kb.op("dve", lambda e: e.scalar_tensor_tensor(out=sb_.t[:, 0:w], in0=rel.t[:, x0:x0 + w], scalar=SLOPES[h],
                                                                      in1=pss.t[:, 0:w], op0=ALU.mult, op1=ALU.add),
                              reads=[rel, pss], writes=[sb_])
                        kb.op("act", lambda e: e.activation(out=pt_.t[:, 0:w], in_=sb_.t[:, 0:w], func=AF.Exp),
                              reads=[sb_], writes=[pt_])
                        for a, kt in enumerate(ch):
                            rhs = rhs_fn(kv, kt)
                            last = (ci == len(chunks) - 1) and (a == len(ch) - 1)
                            kb.op("pe", lambda e: e.matmul(po.t[:, gi, 0:rhs[2]], lhsT=pt_.t[:, 128 * a:128 * a + 128], rhs=rhs[0],
                                                           start=first, stop=last), reads=[pt_, rhs[1]], writes=[po])
                            first = False
                return po

            for qt in range(NT):
                q0 = 128 * qt
                q_a, ng_, kw_, vw_, ao_ = qa[qt % 2], ngt[qt % 2], kwt[qt % 2], vwt[qt % 2], ao[qt % 2]
                kb.dma("sp", q_a.t[0:64, :, :], qT_s[:, q0:q0 + 128].rearrange("(h d) t -> d h t", d=64), writes=[q_a])
                kb.dma("sp", ng_.t[:, :], g["ng_s"][q0:q0 + 128, :], writes=[ng_])
                wk0 = max(0, qt - 4)
                nwk = qt - wk0 + 1
                for kv in range(4):
                    kb.dma("sp", kw_.t[0:64, kv, 0:128 * nwk], kT_s[2, 64 * kv:64 * kv + 64, 128 * wk0:128 * (qt + 1)], writes=[kw_])
                    kb.dma("sp", vw_.t[:, 0:nwk, kv, 0:64],
                           v_s[1, 128 * wk0:128 * (qt + 1), 64 * kv:64 * kv + 64].rearrange("(a p) d -> p a d", p=128), writes=[vw_])
                for kv in range(4):
                    cts = [ct for ct in range(NCT) if qt >= 16 * ct]
                    o_c = o4[0]
                    poc = ps_o[rr[1] % 3]
                    rr[1] += 1
                    for gi in range(4):
                        h = 4 * kv + gi
                        for ci, ct in enumerate(cts):
                            pss = ps_s[rr[0] % 3]
                            sb_, pt_ = sbs[rr[0] % 3], ptb[rr[0] % 3]
                            rr[0] += 1
                            kb.op("pe", lambda e: e.matmul(pss.t[:, 0:128], lhsT=ckT.t[0:64, kv, 128 * ct:128 * ct + 128],
                                                           rhs=q_a.t[0:64, h, :], start=True, stop=True),
                                  reads=[ckT, q_a], writes=[pss])
                            y0 = q0 - 2048 * ct
                            kb.op("dve", lambda e: e.scalar_tensor_tensor(out=sb_.t[:, 0:128], in0=relc.t[:, y0:y0 + 128],
                                                                          scalar=SLOPES[h], in1=pss.t[:, 0:128],
                                                                          op0=ALU.mult, op1=ALU.add),
                                  reads=[relc, pss], writes=[sb_])
                            kb.op("act", lambda e: e.activation(out=pt_.t[:, 0:128], in_=sb_.t[:, 0:128], func=AF.Exp),
                                  reads=[sb_], writes=[pt_])
                            kb.op("pe", lambda e: e.matmul(poc.t[:, gi, :], lhsT=pt_.t[:, 0:128], rhs=rc.t[:, ct, kv, :],
                                                           start=(ci == 0), stop=(ci == len(cts) - 1)),
                                  reads=[pt_, rc], writes=[poc])
                    kb.op("act", lambda e: e.activation(out=o_c.t[:, :, :], in_=poc.t[:, :, :], func=AF.Copy), reads=[poc], writes=[o_c])
                    r_c, g_c = rd[0], gsc[0]
                    kb.op("dve", lambda e: e.tensor_scalar(out=r_c.t[:, :], in0=o_c.t[:, :, 64], scalar1=1e-30, scalar2=None,
                                                           op0=ALU.max), reads=[o_c], writes=[r_c])
                    kb.op("dve", lambda e: e.reciprocal(out=r_c.t[:, :], in_=r_c.t[:, :]), reads=[r_c], writes=[r_c])
                    kb.op("dve", lambda e: e.tensor_tensor(out=g_c.t[:, :], in0=r_c.t[:, :], in1=ng_.t[:, 4 * kv:4 * kv + 4],
                                                           op=ALU.mult), reads=[r_c, ng_], writes=[g_c])
                    for gi in range(4):
                        h = 4 * kv + gi
                        kb.op("pool", lambda e: e.tensor_scalar(out=ao_.t[:, 64 * h:64 * h + 64], in0=o_c.t[:, gi, 0:64],
                                                                scalar1=g_c.t[:, gi:gi + 1], scalar2=None, op0=ALU.mult),
                              reads=[o_c, g_c], writes=[ao_])
                    if do_sel:
                        kb.op("dve", lambda e: e.tensor_scalar(out=sc.t[:, 0:63], in0=o_c.t[:, 0, 65:128], scalar1=r_c.t[:, 0:1],
                                                               scalar2=None, op0=ALU.mult), reads=[o_c, r_c], writes=[sc])
                        for gi in range(1, 4):
                            kb.op("dve", lambda e: e.scalar_tensor_tensor(out=sc.t[:, 0:63], in0=o_c.t[:, gi, 65:128],
                                                                          scalar=r_c.t[:, gi:gi + 1], in1=sc.t[:, 0:63],
                                                                          op0=ALU.mult, op1=ALU.add),
                                  reads=[o_c, r_c, sc], writes=[sc])
                        kb.op("dve", lambda e: e.tensor_tensor(out=sc.t[:, 0:63], in0=sc.t[:, 0:63], in1=fmk.t[:, qt, :], op=ALU.add),
                              reads=[sc, fmk], writes=[sc])
                        kb.op("dve", lambda e: e.max(out=m8a.t[:, :], in_=sc.t[:, 0:63]), reads=[sc], writes=[m8a])
                        kb.op("dve", lambda e: e.match_replace(out=sc2.t[:, 0:63], in_to_replace=m8a.t[:, :], in_values=sc.t[:, 0:63],
                                                               imm_value=-3.0e38), reads=[sc, m8a], writes=[sc2])
                        kb.op("dve", lambda e: e.max(out=m8b.t[:, :], in_=sc2.t[:, 0:63]), reads=[sc2], writes=[m8b])
                        kb.op("dve", lambda e: e.tensor_scalar(out=sc2.t[:, 0:63], in0=sc.t[:, 0:63], scalar1=m8b.t[:, 6:7],
                                                               scalar2=None, op0=ALU.is_ge), reads=[sc, m8b], writes=[sc2])
                        kb.op("dve", lambda e: e.tensor_scalar(out=mk.t[:, 65:128], in0=sc2.t[:, 0:63], scalar1=-1.0, scalar2=30000.0,
                                                               op0=ALU.add, op1=ALU.mult), reads=[sc2], writes=[mk])
                    kb.op("pe", lambda e: e.transpose(out=ps_t.t[:, :], in_=mk.t[:, :], identity=ident.t[:, :]),
                          reads=[mk, ident], writes=[ps_t])
                    kb.op("act", lambda e: e.activation(out=q_a.t[64:128, 4 * kv, :], in_=ps_t.t[64:128, :], func=AF.Copy),
                          reads=[ps_t], writes=[q_a])
                    for gi in range(1, 4):
                        kb.op("pool", lambda e: e.tensor_copy(out=q_a.t[64:128, 4 * kv + gi, :], in_=q_a.t[64:128, 4 * kv, :]),
                              reads=[q_a], writes=[q_a])
                    for br in (1, 2):
                        if br == 1:
                            kts = list(range(qt, -1, -1))
                            pob = branch(qt, kv, q_a, kts,
                                         lambda kv_, kt: (ksa.t[:, kv_, 128 * kt:128 * kt + 128], ksa),
                                         rels, lambda kt: 128 * (qt - kt),
                                         lambda kv_, kt: (vsa.t[:, kt, kv_, :], vsa, 65), 128)
                        else:
                            kts = list(range(qt, wk0 - 1, -1))
                            pob = branch(qt, kv, q_a, kts,
                                         lambda kv_, kt: (kw_.t[0:64, kv_, 128 * (kt - wk0):128 * (kt - wk0) + 128], kw_),
                                         relw, lambda kt: 128 * (qt - kt),
                                         lambda kv_, kt: (vw_.t[:, kt - wk0, kv_, :], vw_, 65), 64)
                        o_b, r_b, g_b = o4[1], rd[1], gsc[1]
                        kb.op("act", lambda e: e.activation(out=o_b.t[:, :, 0:65], in_=pob.t[:, :, 0:65], func=AF.Copy),
                              reads=[pob], writes=[o_b])
                        kb.op("dve", lambda e: e.tensor_scalar(out=r_b.t[:, :], in0=o_b.t[:, :, 64], scalar1=1e-30, scalar2=None,
                                                               op0=ALU.max), reads=[o_b], writes=[r_b])
                        kb.op("dve", lambda e: e.reciprocal(out=r_b.t[:, :], in_=r_b.t[:, :]), reads=[r_b], writes=[r_b])
                        kb.op("dve", lambda e: e.tensor_tensor(out=g_b.t[:, :], in0=r_b.t[:, :],
                                                               in1=ng_.t[:, 16 * br + 4 * kv:16 * br + 4 * kv + 4], op=ALU.mult),
                              reads=[r_b, ng_], writes=[g_b])
                        for gi in range(4):
                            h = 4 * kv + gi
                            kb.op("dve", lambda e: e.scalar_tensor_tensor(out=ao_.t[:, 64 * h:64 * h + 64], in0=o_b.t[:, gi, 0:64],
                                                                          scalar=g_b.t[:, gi:gi + 1], in1=ao_.t[:, 64 * h:64 * h + 64],
                                                                          op0=ALU.mult, op1=ALU.add),
                                  reads=[o_b, g_b, ao_], writes=[ao_])
                kb.dma("sp", g["ao_s"][q0:q0 + 128, :], ao_.t[:, :], reads=[ao_])
        kb.barrier()


def _att_consts(cmp_pos, cmp_w, cmp_w1, cmp_w2):
    f32 = np.float32
    NEG = f32(-1e32)
    j = np.arange(128, dtype=np.int64)[:, None]
    y = np.arange(4096, dtype=np.int64)[None, :]
    relc = np.where(y >= 16 * j + 31, (16 * j + 31 - y).astype(f32), NEG).astype(f32)
    x = np.arange(4096 + 128, dtype=np.int64)[None, :]
    rels = np.where(x >= j, (j - x).astype(f32), NEG).astype(f32)
    xw = np.arange(768, dtype=np.int64)[None, :]
    relw = np.where((xw - j >= 0) & (xw - j <= 512), (j - xw).astype(f32), NEG).astype(f32)
    t = (128 * np.arange(32)[None, :, None] + np.arange(128)[:, None, None])
    sb = np.arange(1, 64)[None, None, :]
    cur = t // 64
    fmk = np.where((sb == cur) | (sb == cur - 1), f32(1e30), np.where(sb > cur, f32(-1e30), f32(0))).astype(f32).reshape(128, 32 * 63)
    blk = (128 * np.arange(2)[None, :, None] + np.arange(128)[:, None, None])
    cm = np.where((blk == 4 * sb - 1) | (blk == 4 * sb + 3), f32(1), np.where((blk >= 4 * sb) & (blk <= 4 * sb + 2), f32(2), f32(0)))
    cmat = cm.astype(f32).reshape(128, 2 * 63)
    onehot = (np.arange(4096)[None, :] // 64 == np.arange(64)[:, None]).astype(f32)
    wpool = np.zeros((128, 2, 16), f32)
    for c in range(8):
        for p in range(16):
            wpool[16 * c + p, :, c] = cmp_w[p, :]
            wpool[16 * c + p, :, 8 + c] = cmp_w[16 + p, :]
    cposd = np.concatenate([cmp_pos, cmp_pos], axis=2)
    w1bd = np.zeros((128, 2, 128), f32)
    w2bd = np.zeros((128, 2, 128), f32)
    w2sel = np.zeros((128, 2, 64), f32)
    for jj in range(2):
        for hh in range(2):
            w1bd[64 * hh:64 * hh + 64, jj, 64 * hh:64 * hh + 64] = cmp_w1[jj]
            w2bd[64 * hh:64 * hh + 64, jj, 64 * hh:64 * hh + 64] = cmp_w2[jj]
    for hh in range(2):
        w2sel[64 * hh:64 * hh + 64, hh, :] = cmp_w2[0]
    c_ = np.ascontiguousarray
    return {"relc": c_(relc), "rels": c_(rels), "relw": c_(relw), "fmk": c_(fmk), "cmat": c_(cmat), "onehot": c_(onehot),
            "wpool": c_(wpool.reshape(128, 32)), "cposd": c_(cposd.reshape(32, 256)), "cw": c_(cmp_w.astype(f32)),
            "w1bd": c_(w1bd.reshape(128, 256)), "w2bd": c_(w2bd.reshape(128, 256)), "w2sel": c_(w2sel.reshape(128, 128))}


def phase_smp(kb, cfg, g):
    T, NS, TT, NP = cfg.T, cfg.NS, cfg.TT, cfg.NP
    NBs = 8 * NP - 1
    NCT = (NBs + 127) // 128
    NSC = 2 * NP
    NCH = (2 * NP + 63) // 64
    do_sel = (2 * NP + 1) > 16
    GP = min(16, NP)
    NG = NP // GP
    qT_s, kT_s, v_s = g["qT_s"], g["kT_s"], g["v_s"]
    es = ExitStack()
    with es:
        ident = kb.sb("s_ident", [128, 128], BF16, es)
        bsel = kb.sb("s_bsel", [128, NP, 16], F32, es)
        bwin = kb.sb("s_bwin", [128, 4, 16], F32, es)
        bcmp = kb.sb("s_bcmp", [128, NCT, 16], F32, es)
        cms = kb.sb("s_cms", [128, NCT, NSC], BF16, es)
        fms = kb.sb("s_fms", [4, NSC], F32, es)
        iota = kb.sb("s_iota", [128, 1], I32, es)
        kaug = [kb.sb(f"s_kaug{i}", [128, 4, 128 * GP], BF16, es) for i in range(2)]
        vg = [kb.sb(f"s_vg{i}", [128, GP, 4, 65], BF16, es) for i in range(2)]
        qs4 = kb.sb("s_qs4", [64, 16, NS], BF16, es)
        kn = kb.sb("s_kn", [64, 2, 4, NS], BF16, es)
        vn = kb.sb("s_vn", [1, 2, NS, 256], BF16, es)
        vnew = kb.sb("s_vnew", [1, 2, 4, 65], BF16, es)
        wpool = kb.sb("s_wpool", [128, 2, 16], F32, es)
        cposd = kb.sb("s_cposd", [32, 2, 128], F32, es)
        cw = kb.sb("s_cw", [32, 2], F32, es)
        w1bd = kb.sb("s_w1bd", [128, 2, 128], F32, es)
        w2bd = kb.sb("s_w2bd", [128, 2, 128], F32, es)
        w2sel = kb.sb("s_w2sel", [128, 2, 64], F32, es)
        pe_sb = kb.sb("s_pe", [128, 2], F32, es)
        pg = [kb.sb(f"s_pg{i}", [128, 512], F32, es) for i in range(3)]
        pkb = [kb.sb(f"s_pkb{i}", [128, 256], BF16, es) for i in range(2)]
        abseg = kb.sb("s_abseg", [128, 4, 32, 16], F32, es)
        af = kb.sb("s_af", [128, 4, 8 * NP], F32, es)
        bf = kb.sb("s_bf", [128, 4, 8 * NP], F32, es)
        hid = kb.sb("s_hid", [128, 4, 8 * NP], F32, es)
        ckT = kb.sb("s_ckT", [64, 4, 128 * NCT], BF16, es)
        rcs = kb.sb("s_rcs", [128, NCT, 4, 65], BF16, es)
        ptb_i = kb.sb("s_ptb", [128, NP], I32, es)
        idx = kb.sb("s_idx", [128, NP], I32, es)
        qsa = kb.sb("s_qsa", [128, NCH, 16], BF16, es)
        sbs = [kb.sb(f"s_sbs{i}", [128, 32, 16], F32, es) for i in range(2)]
        ptt = [kb.sb(f"s_ptt{i}", [128, 32, 16], BF16, es) for i in range(2)]
        ptn = kb.sb("s_ptn", [1, 16], BF16, es)
        osum = kb.sb("s_osum", [4, 4, 65], F32, es)
        oc = kb.sb("s_oc", [4, 4, NSC], F32, es)
        rd = kb.sb("s_rd", [4, 4], F32, es)
        gsc = kb.sb("s_gsc", [4, 4], F32, es)
        gs = kb.sb("s_gs", [4, 3, 4], F32, es)
        rsel = kb.sb("s_rsel", [4, 4, 4], F32, es)
        sc = kb.sb("s_sc", [4, NSC], F32, es)
        sc2 = kb.sb("s_sc2", [4, NSC], F32, es)
        m8a = kb.sb("s_m8a", [4, 8], F32, es)
        m8b = kb.sb("s_m8b", [4, 8], F32, es)
        maskp = kb.sb("s_maskp", [4, NCH, 128], BF16, es)
        mT = kb.sb("s_mT", [128, 4], BF16, es)
        a_s = kb.sb("s_as", [4, 4, 64], F32, es)
        pp = [kb.ps(f"s_pp{i}", [128, 32, 16], F32, es) for i in range(4)]
        hp = [kb.ps("s_hp0", [128, 512], F32, es)] * 2
        pss = kb.ps("s_pss", [128, 32, 16], F32, es)
        pmisc = kb.ps("s_pmisc", [128, 512], F32, es)
        ptr = None

        kb.dma("pool", ident.t[:], g["ident_in"][:, :], writes=[ident])
        kb.dma("sp", bsel.t[:], g["bsel"][:, :].rearrange("p (a b) -> p a b", b=16), writes=[bsel])
        kb.dma("sp", bwin.t[:], g["bwin"][:, :].rearrange("p (a b) -> p a b", b=16), writes=[bwin])
        kb.dma("sp", bcmp.t[:], g["bcmp"][:, :].rearrange("p (a b) -> p a b", b=16), writes=[bcmp])
        kb.dma("pool", cms.t[:], g["cms"][:, :].rearrange("p (a b) -> p a b", b=NSC), writes=[cms])
        kb.dma("sp", fms.t[:], g["fms"][:, :], writes=[fms])
        kb.dma("sp", iota.t[:], g["iota"][:, :], writes=[iota])
        for nm, t_, src in ((0, wpool, g["wpool"][:, :].rearrange("p (a b) -> p a b", b=16)),
                            (1, cposd, g["cposd"][:, :].rearrange("p (a b) -> p a b", b=128)),
                            (2, cw, g["cw"][:, :]),
                            (3, w1bd, g["w1bd"][:, :].rearrange("p (a b) -> p a b", b=128)),
                            (4, w2bd, g["w2bd"][:, :].rearrange("p (a b) -> p a b", b=128)),
                            (5, w2sel, g["w2sel"][:, :].rearrange("p (a b) -> p a b", b=64))):
            kb.dma("sp", t_.t[:], src, writes=[t_])
        for i in range(2):
            off = 128 * ((GP * i) % 32)
            for kv in range(4):
                kb.dma("pool", kaug[i].t[64:128, kv, :], g["onehot"][:, off:off + 128 * GP], writes=[kaug[i]])
            kb.op("pool", lambda e: e.memset(vg[i].t[:, :, :, 64:65], 1.0), writes=[vg[i]])
        kb.dma("sp", qs4.t[:, :, :], qT_s[:, T:T + NS].rearrange("(h d) t -> d h t", d=64), writes=[qs4])
        for b in range(2):
            kb.dma("sp", kn.t[:, b, :, :], kT_s[1 + b, :, T:T + NS].rearrange("(k d) t -> d k t", d=64), writes=[kn])
            kb.dma("sp", vn.t[0:1, b, :, :], v_s[b:b + 1, T:T + NS, :], writes=[vn])
        kb.op("pool", lambda e: e.memset(vnew.t[:, :, :, 64:65], 1.0), writes=[vnew])
        kb.op("pool", lambda e: e.memset(ckT.t[:], 0.0), writes=[ckT])
        kb.op("pool", lambda e: e.memset(rcs.t[:], 0.0), writes=[rcs])
        kb.op("pool", lambda e: e.memset(rcs.t[:, :, :, 64:65], 1.0), reads=[rcs], writes=[rcs])
        kb.op("pool", lambda e: e.memset(maskp.t[:], 0.0), writes=[maskp])
        kb.op("pool", lambda e: e.memset(rsel.t[:], 0.0), writes=[rsel])
        kb.op("pool", lambda e: e.memset(qsa.t[:], 0.0), writes=[qsa])
        for j in range(2):
            kb.op("pe", lambda e: e.matmul(pmisc.t[:, j:j + 1], lhsT=cposd.t[:, j, :], rhs=cw.t[:, j:j + 1],
                                           start=True, stop=True), reads=[cposd, cw], writes=[pmisc])
        kb.op("act", lambda e: e.activation(out=pe_sb.t[:, :], in_=pmisc.t[:, 0:2], func=AF.Copy), reads=[pmisc], writes=[pe_sb])
        pv = pmisc.t[0:4, 0:260].rearrange("p (k c) -> p k c", c=65)
        scps = pmisc.t[0:4, 0:NSC]
        cnt = [0]

        def prep_page(pgt, kb_, vg_, pi):
            pk = pkb[cnt[0] % 2]
            cnt[0] += 1
            kb.op("act", lambda e: e.activation(out=pk.t[:, :], in_=pgt.t[:, 0:256], func=AF.Copy), reads=[pgt], writes=[pk])
            for kv in range(4):
                kb.op("pe", lambda e: e.transpose(out=hpb.t[0:64, kv, :], in_=pk.t[:, 64 * kv:64 * kv + 64], identity=ident.t[:, :]),
                      reads=[pk, ident], writes=[hpb_buf])
            kb.op("dve", lambda e: e.tensor_copy(out=kb_.t[0:64, :, 128 * pi:128 * pi + 128], in_=hpb.t[0:64, :, :]),
                  reads=[hpb_buf], writes=[kb_])
            kb.op("pool", lambda e: e.tensor_copy(out=vg_.t[:, pi, :, 0:64],
                                                  in_=pgt.t[:, 256:512].rearrange("p (k d) -> p k d", d=64)),
                  reads=[pgt], writes=[vg_])

        hpb_buf = kb.ps("s_ptr", [128, 4, 128], BF16, es)
        hpb = hpb_buf

        def pv_batch(lhs_fn, ntile, v_fn, first):
            for kv in range(4):
                for a in range(ntile):
                    l_ = lhs_fn(a, kv)
                    v_ = v_fn(a, kv)
                    kb.op("pe", lambda e: e.matmul(pv[:, kv, :], lhsT=l_[0], rhs=v_[0], start=(a == 0), stop=(a == ntile - 1)),
                          reads=[l_[1], v_[1]], writes=[pmisc])
            if first:
                kb.op("act", lambda e: e.activation(out=osum.t[:, :, :], in_=pv, func=AF.Copy), reads=[pmisc], writes=[osum])
            else:
                kb.op("dve", lambda e: e.tensor_tensor(out=osum.t[:, :, :], in0=osum.t[:, :, :], in1=pv, op=ALU.add),
                      reads=[pmisc, osum], writes=[osum])

        def finish_branch(br, first):
            kb.op("dve", lambda e: e.tensor_scalar(out=rd.t[:, :], in0=osum.t[:, :, 64], scalar1=1e-30, scalar2=None, op0=ALU.max),
                  reads=[osum], writes=[rd])
            kb.op("dve", lambda e: e.reciprocal(out=rd.t[:, :], in_=rd.t[:, :]), reads=[rd], writes=[rd])
            kb.op("dve", lambda e: e.tensor_tensor(out=gsc.t[:, :], in0=rd.t[:, :], in1=gs.t[:, br, :], op=ALU.mult),
                  reads=[rd, gs], writes=[gsc])
            for kv in range(4):
                if first:
                    kb.op("dve", lambda e: e.tensor_scalar(out=a_s.t[:, kv, :], in0=osum.t[:, kv, 0:64], scalar1=gsc.t[:, kv:kv + 1],
                                                           scalar2=None, op0=ALU.mult), reads=[osum, gsc], writes=[a_s])
                else:
                    kb.op("dve", lambda e: e.scalar_tensor_tensor(out=a_s.t[:, kv, :], in0=osum.t[:, kv, 0:64],
                                                                  scalar=gsc.t[:, kv:kv + 1], in1=a_s.t[:, kv, :],
                                                                  op0=ALU.mult, op1=ALU.add), reads=[osum, gsc, a_s], writes=[a_s])

        def new_row(b, s, first):
            for kv in range(4):
                kb.op("pe", lambda e: e.matmul(pss.t[0:1, 0, 4 * kv:4 * kv + 4], lhsT=kn.t[0:64, b, kv, s:s + 1],
                                               rhs=qsa.t[0:64, 0, 4 * kv:4 * kv + 4], start=True, stop=True),
                      reads=[kn, qsa], writes=[pss])
            kb.op("act", lambda e: e.activation(out=ptn.t[0:1, :], in_=pss.t[0:1, 0, :], func=AF.Exp), reads=[pss], writes=[ptn])
            kb.op("pool", lambda e: e.tensor_copy(out=vnew.t[0:1, b, :, 0:64],
                                                  in_=vn.t[0:1, b, s, :].rearrange("p (k d) -> p k d", d=64)),
                  reads=[vn], writes=[vnew])
            pv_batch(lambda a, kv: (ptn.t[0:1, 4 * kv:4 * kv + 4], ptn), 1, lambda a, kv: (vnew.t[0:1, b, kv, :], vnew), first)

        def score_batch(ntile, lhs_fn, rows, chunk, bias_ap, bias_buf):
            sb_, pt_ = sbs[cnt[0] % 2], ptt[cnt[0] % 2]
            cnt[0] += 1
            for a in range(ntile):
                for kv in range(4):
                    l_ = lhs_fn(a, kv)
                    kb.op("pe", lambda e: e.matmul(pss.t[:, a, 4 * kv:4 * kv + 4], lhsT=l_[0], rhs=qsa.t[0:rows, chunk, 4 * kv:4 * kv + 4],
                                                   start=True, stop=True), reads=[l_[1], qsa], writes=[pss])
            kb.op("dve", lambda e: e.tensor_tensor(out=sb_.t[:, 0:ntile, :], in0=pss.t[:, 0:ntile, :], in1=bias_ap, op=ALU.add),
                  reads=[pss, bias_buf], writes=[sb_])
            kb.op("act", lambda e: e.activation(out=pt_.t[:, 0:ntile, :], in_=sb_.t[:, 0:ntile, :], func=AF.Exp),
                  reads=[sb_], writes=[pt_])
            return pt_

        for s in range(NS):
            kb.dma("sp", ptb_i.t[:, :], g["ptab"][s:s + 1, :].to_broadcast([128, NP]), writes=[ptb_i])
            kb.op("dve", lambda e: e.tensor_scalar(out=idx.t[:, :], in0=ptb_i.t[:, :], scalar1=128, scalar2=iota.t[:, 0:1],
                                                   op0=ALU.mult, op1=ALU.add), reads=[ptb_i, iota], writes=[idx])
            kb.dma("sp", gs.t[:, :, :], g["ng_s"][T + s:T + s + 1, 0:48].rearrange("o (b k g) -> g (o b) k", b=3, k=4),
                   writes=[gs], allow_slow_non_contiguous=True)
            for c in range(NCH):
                kb.op("pool", lambda e: e.tensor_copy(out=qsa.t[0:64, c, :], in_=qs4.t[0:64, :, s]), reads=[qs4], writes=[qsa])
            for p in range(NP):
                pgt = pg[p % 3]
                kb.gather(pgt.t[:, :], g["cache_cmp"][:, :], idx.t[:, p:p + 1], reads=[idx], writes=[pgt])
                for fc in range(4):
                    kb.op("pe", lambda e: e.matmul(pp[fc].t[:, p % 32, :], lhsT=pgt.t[:, 128 * fc:128 * fc + 128],
                                                   rhs=wpool.t[:, fc // 2, :], start=True, stop=True),
                          reads=[pgt, wpool], writes=[pp[fc]])
                if p % 32 == 31 or p == NP - 1:
                    seg0 = 32 * (p // 32)
                    n = p - seg0 + 1
                    for fc in range(4):
                        kb.op("act", lambda e: e.activation(out=abseg.t[:, fc, 0:n, :], in_=pp[fc].t[:, 0:n, :], func=AF.Copy),
                              reads=[pp[fc]], writes=[abseg])
                    kb.op("dve", lambda e: e.tensor_copy(out=af.t[:, :, 8 * seg0:8 * (seg0 + n)].rearrange("p f (t c) -> p f t c", c=8),
                                                         in_=abseg.t[:, :, 0:n, 0:8]), reads=[abseg], writes=[af])
                    kb.op("pool", lambda e: e.tensor_copy(out=bf.t[:, :, 8 * seg0:8 * (seg0 + n)].rearrange("p f (t c) -> p f t c", c=8),
                                                          in_=abseg.t[:, :, 0:n, 8:16]), reads=[abseg], writes=[bf])
            kb.op("dve", lambda e: e.tensor_tensor(out=af.t[:, :, 0:NBs], in0=af.t[:, :, 0:NBs], in1=bf.t[:, :, 1:NBs + 1], op=ALU.add),
                  reads=[af, bf], writes=[af])
            for fc in range(4):
                kb.op("dve", lambda e: e.tensor_scalar(out=af.t[:, fc, 0:NBs], in0=af.t[:, fc, 0:NBs],
                                                       scalar1=pe_sb.t[:, fc // 2:fc // 2 + 1], scalar2=None, op0=ALU.add),
                      reads=[af, pe_sb], writes=[af])
            hi = 0
            for fc in range(4):
                for c0 in range(0, NBs, 512):
                    n = min(512, NBs - c0)
                    h_ = hp[hi % 2]
                    hi += 1
                    kb.op("pe", lambda e: e.matmul(h_.t[:, 0:n], lhsT=w1bd.t[:, fc // 2, :], rhs=af.t[:, fc, c0:c0 + n],
                                                   start=True, stop=True), reads=[w1bd, af], writes=[h_])
                    kb.op("act", lambda e: e.activation(out=hid.t[:, fc, c0:c0 + n], in_=h_.t[:, 0:n], func=AF.Silu),
                          reads=[h_], writes=[hid])
            for kv in range(4):
                for c0 in range(0, NBs, 512):
                    n = min(512, NBs - c0)
                    h_ = hp[hi % 2]
                    hi += 1
                    kb.op("pe", lambda e: e.matmul(h_.t[0:64, 0:n], lhsT=w2sel.t[:, kv % 2, :], rhs=hid.t[:, kv // 2, c0:c0 + n],
                                                   start=True, stop=True), reads=[w2sel, hid], writes=[h_])
                    kb.op("act", lambda e: e.activation(out=ckT.t[0:64, kv, c0:c0 + n], in_=h_.t[0:64, 0:n], func=AF.Copy),
                          reads=[h_], writes=[ckT])
            for ct in range(NCT):
                nb = min(128, NBs - 128 * ct)
                for fv in range(2):
                    h_ = hp[hi % 2]
                    hi += 1
                    kb.op("pe", lambda e: e.matmul(h_.t[0:nb, 0:128], lhsT=hid.t[:, 2 + fv, 128 * ct:128 * ct + nb],
                                                   rhs=w2bd.t[:, 1, :], start=True, stop=True), reads=[hid, w2bd], writes=[h_])
                    kb.op("act", lambda e: e.activation(out=rcs.t[0:nb, ct, 2 * fv:2 * fv + 2, 0:64],
                                                        in_=h_.t[0:nb, 0:128].rearrange("p (a b) -> p a b", b=64), func=AF.Copy),
                          reads=[h_], writes=[rcs])
            pt_ = score_batch(NCT, lambda a, kv: (ckT.t[0:64, kv, 128 * a:128 * a + 128], ckT), 64, 0, bcmp.t[:, :, :], bcmp)
            pv_batch(lambda a, kv: (pt_.t[:, a, 4 * kv:4 * kv + 4], pt_), NCT, lambda a, kv: (rcs.t[:, a, kv, :], rcs), True)
            finish_branch(0, True)
            for kv in range(4):
                h_ = hp[hi % 2]
                hi += 1
                for a in range(NCT):
                    kb.op("pe", lambda e: e.matmul(h_.t[0:4, 0:NSC], lhsT=pt_.t[:, a, 4 * kv:4 * kv + 4], rhs=cms.t[:, a, :],
                                                   start=(a == 0), stop=(a == NCT - 1)), reads=[pt_, cms], writes=[h_])
                kb.op("act", lambda e: e.activation(out=oc.t[:, kv, :], in_=h_.t[0:4, 0:NSC], func=AF.Copy), reads=[h_], writes=[oc])
                kb.op("pool", lambda e: e.tensor_copy(out=rsel.t[:, kv, kv:kv + 1], in_=rd.t[:, kv:kv + 1]), reads=[rd], writes=[rsel])
            for kv in range(4):
                kb.op("pe", lambda e: e.matmul(scps, lhsT=rsel.t[:, kv, :], rhs=oc.t[:, kv, :], start=(kv == 0), stop=(kv == 3)),
                      reads=[rsel, oc], writes=[pmisc])
            kb.op("dve", lambda e: e.tensor_tensor(out=sc.t[:, :], in0=scps, in1=fms.t[:, :], op=ALU.add),
                  reads=[pmisc, fms], writes=[sc])
            if do_sel:
                kb.op("dve", lambda e: e.max(out=m8a.t[:, :], in_=sc.t[:, :]), reads=[sc], writes=[m8a])
                kb.op("dve", lambda e: e.match_replace(out=sc2.t[:, :], in_to_replace=m8a.t[:, :], in_values=sc.t[:, :],
                                                       imm_value=-3.0e38), reads=[sc, m8a], writes=[sc2])
                kb.op("dve", lambda e: e.max(out=m8b.t[:, :], in_=sc2.t[:, :]), reads=[sc2], writes=[m8b])
                kb.op("dve", lambda e: e.tensor_scalar(out=sc2.t[:, :], in0=sc.t[:, :], scalar1=m8b.t[:, 6:7], scalar2=None,
                                                       op0=ALU.is_ge), reads=[sc, m8b], writes=[sc2])
                kb.op("dve", lambda e: e.tensor_scalar(out=sc2.t[:, :], in0=sc2.t[:, :], scalar1=-1.0, scalar2=30000.0,
                                                       op0=ALU.add, op1=ALU.mult), reads=[sc2], writes=[sc2])
                for c in range(NCH):
                    b0 = max(1, 64 * c)
                    b1 = min(2 * NP - 1, 64 * c + 63)
                    kb.op("dve", lambda e: e.tensor_copy(out=maskp.t[:, c, 64 + b0 - 64 * c:64 + b1 - 64 * c + 1], in_=sc2.t[:, b0 - 1:b1]),
                          reads=[sc2], writes=[maskp])
            for c in range(NCH):
                kb.op("pe", lambda e: e.transpose(out=hpb.t[:, 0, 0:4], in_=maskp.t[0:4, c, :], identity=ident.t[0:4, 0:4]),
                      reads=[maskp, ident], writes=[hpb_buf])
                kb.op("act", lambda e: e.activation(out=mT.t[64:128, :], in_=hpb.t[64:128, 0, 0:4], func=AF.Copy),
                      reads=[hpb_buf], writes=[mT])
                for gi in range(4):
                    kb.op("pool", lambda e: e.tensor_copy(out=qsa.t[64:128, c, :].rearrange("p (k g) -> p k g", g=4)[:, :, gi],
                                                          in_=mT.t[64:128, :]), reads=[mT], writes=[qsa])
            first = True
            for grp in range(NG):
                kb_, vg_ = kaug[grp % 2], vg[grp % 2]
                for pi in range(GP):
                    p = GP * grp + pi
                    pgt = pg[p % 3]
                    kb.gather(pgt.t[:, :], g["cache_slc"][:, :], idx.t[:, p:p + 1], reads=[idx], writes=[pgt])
                    prep_page(pgt, kb_, vg_, pi)
                c = (GP * grp) // 32
                pt_ = score_batch(GP, lambda a, kv: (kb_.t[:, kv, 128 * a:128 * a + 128], kb_), 128, c,
                                  bsel.t[:, GP * grp:GP * grp + GP, :], bsel)
                pv_batch(lambda a, kv: (pt_.t[:, a, 4 * kv:4 * kv + 4], pt_), GP, lambda a, kv: (vg_.t[:, a, kv, :], vg_), first)
                first = False
            new_row(0, s, False)
            finish_branch(1, False)
            kb_, vg_ = kaug[NG % 2], vg[NG % 2]
            for a in range(4):
                pgt = pg[a % 3]
                kb.dma("sp", pgt.t[:, :], g["st_win"][s, 128 * a:128 * a + 128, :], writes=[pgt])
                prep_page(pgt, kb_, vg_, a)
            pt_ = score_batch(4, lambda a, kv: (kb_.t[0:64, kv, 128 * a:128 * a + 128], kb_), 64, 0, bwin.t[:, :, :], bwin)
            pv_batch(lambda a, kv: (pt_.t[:, a, 4 * kv:4 * kv + 4], pt_), 4, lambda a, kv: (vg_.t[:, a, kv, :], vg_), True)
            new_row(1, s, False)
            finish_branch(2, False)
            kb.dma("sp", g["ao_s"][T + s:T + s + 1, :].rearrange("o (k g d) -> g (o k) d", k=4, g=4), a_s.t[:, :, :], reads=[a_s])
        kb.barrier()


def _smp_consts(NP):
    f32 = np.float32
    P = 128 * NP
    NBs = 8 * NP - 1
    NCT = (NBs + 127) // 128
    sl = np.asarray(SLOPES, dtype=np.float64)[None, None, :]
    j = np.arange(128)[:, None, None]
    pos = 128 * np.arange(NP)[None, :, None] + j
    bsel = (sl * (pos - P)).astype(f32).reshape(128, NP * 16)
    dist = 512 - 128 * np.arange(4)[None, :, None] - j
    bwin = (-sl * dist).astype(f32).reshape(128, 64)
    blk = 128 * np.arange(NCT)[None, :, None] + j
    bcmp = np.where(blk <= NBs - 1, sl * (16 * blk + 31 - P), -1e32).astype(f32).reshape(128, NCT * 16)
    sb = np.arange(1, 2 * NP + 1)[None, None, :]
    cm = np.where((blk == 4 * sb - 1) | (blk == 4 * sb + 3), 1.0, np.where((blk >= 4 * sb) & (blk <= 4 * sb + 2), 2.0, 0.0))
    cms = cm.astype(f32).reshape(128, NCT * 2 * NP)
    fms = np.zeros((4, 2 * NP), f32)
    fms[:, 2 * NP - 2:] = 1e30
    c_ = np.ascontiguousarray
    return {"bsel": c_(bsel), "bwin": c_(bwin), "bcmp": c_(bcmp), "cms": c_(cms), "fms": c_(fms),
            "iota": np.arange(128, dtype=np.int32).reshape(128, 1)}


_PROG_CACHE = {}


def _get_prog(cfg_key):
    if cfg_key not in _PROG_CACHE:
        _PROG_CACHE[cfg_key] = build_program(Cfg(*cfg_key))
    return _PROG_CACHE[cfg_key]


def run_cores(cfg_key, per_core_inputs):
    nc = _get_prog(cfg_key)
    res = run_bass_kernel_spmd(nc, per_core_inputs, core_ids=list(range(len(per_core_inputs))))
    return res.results


def kernel(x_prompt, x_sample, cache_cmp_kv, cache_slc_kv, state_win_kv, state_conv, page_table,
           ln1_g, w_in, cmp_pos, cmp_w, cmp_w1, cmp_w2, conv_w, conv_b, conv_ln_g, conv_ln_b,
           w_conv_pw, w_out, ln2_g, w_up, w_down, lnf_g):
    f = lambda a: np.ascontiguousarray(np.asarray(a))
    B, T, _ = x_prompt.shape
    NSB = x_sample.shape[0]
    ncores = 8
    NS = NSB // ncores
    x_prompt = f(x_prompt); x_sample = f(x_sample)
    state_win_kv = f(state_win_kv); state_conv = f(state_conv)
    ident = np.eye(128, dtype=np.float32)
    cvec = np.ascontiguousarray(np.concatenate([f(conv_w)[0], f(conv_b), f(conv_ln_g), f(conv_ln_b)], axis=0))
    consts = _att_consts(f(cmp_pos)[0], f(cmp_w)[0], f(cmp_w1)[0], f(cmp_w2)[0])
    in_maps = []
    for c in range(ncores):
        in_maps.append({
            "xp": x_prompt[c],
            "xs": x_sample[NS * c:NS * c + NS, 0, :],
            "st_win": state_win_kv[0, NS * c:NS * c + NS].reshape(NS, 512, 512),
            "st_conv": state_conv[0, NS * c:NS * c + NS],
            "ln1_g": f(ln1_g).reshape(1, D),
            "w_in": f(w_in)[0],
            "ident": ident,
            "cvec": cvec,
            "w_pw": f(w_conv_pw)[0],
            "w_out": f(w_out)[0],
            "ln2_g": f(ln2_g).reshape(1, D),
            "w_up": f(w_up)[0],
            "w_down": f(w_down)[0],
            "lnf_g": f(lnf_g).reshape(1, D),
            **consts,
        })
    NP = int(page_table.shape[1])
    NPHYS = int(cache_cmp_kv.shape[1])
    sc_ = _smp_consts(NP)
    cc_ = f(cache_cmp_kv).reshape(NPHYS * 128, 512)
    cs_ = f(cache_slc_kv).reshape(NPHYS * 128, 512)
    ptab = np.ascontiguousarray(np.asarray(page_table).astype(np.int32))
    for c in range(ncores):
        in_maps[c].update(sc_)
        in_maps[c].update({"cache_cmp": cc_, "cache_slc": cs_, "ptab": ptab[NS * c:NS * c + NS]})
    res = run_cores((T, NS, NP, NPHYS), in_maps)
    cat = lambda k: np.stack([r[k] for r in res], axis=0)
    y_prompt = cat("y_p")
    y_sample = np.concatenate([r["y_s"] for r in res], axis=0).reshape(NSB, 1, D)
    cmp_kv_p = cat("cmp_p").reshape(1, B, T, 2, NKV, HD)
    cmp_kv_s = np.concatenate([r["cmp_s"] for r in res], axis=0).reshape(1, NSB, 1, 2, NKV, HD)
    slc_kv_p = cat("slc_p").reshape(1, B, T, 2, NKV, HD)
    slc_kv_s = np.concatenate([r["slc_s"] for r in res], axis=0).reshape(1, NSB, 1, 2, NKV, HD)
    win_kv_p = cat("win_p").reshape(1, B, 512, 2, NKV, HD)
    win_kv_s = np.concatenate([r["win_s"] for r in res], axis=0).reshape(1, NSB, 512, 2, NKV, HD)
    conv_pp = cat("conv_p").reshape(1, B, 30, D)
    conv_ss = np.concatenate([r["conv_s"] for r in res], axis=0).reshape(1, NSB, 30, D)
    return (y_prompt, y_sample, cmp_kv_p, cmp_kv_s, slc_kv_p, slc_kv_s, win_kv_p, win_kv_s, conv_pp, conv_ss)
```
